# Optimizing a Trainium2 kernel written in Bass

```python
import math
import jax, jax.numpy as jnp
from jax import lax
import numpy as np

D_MODEL = 1024
BATCH = 16
SEQ = 2048
DEPTH = 2

D_MIX = D_MODEL
GLA_WIDTH = D_MIX // 4
GLA_HEADS = 4
GLA_DV = GLA_WIDTH // GLA_HEADS
GLA_DK = GLA_DV // 2
GLA_KEY = GLA_HEADS * GLA_DK
GLA_GATE_RANK = 16
GLA_TAU = 16.0
GLA_CHUNK = 64
SGU_WIDTH = D_MIX // 4
SGU_GROUPS = 4
SGU_DG = SGU_WIDTH // SGU_GROUPS
SGU_CHUNK = 128
SSD_WIDTH = D_MIX // 2
SSD_HEAD_DIM = 64
SSD_HEADS = SSD_WIDTH // SSD_HEAD_DIM
SSD_GROUPS = 2
SSD_STATE = 128
SSD_CONV = 4
SSD_CHUNK = 128
SSD_CONV_DIM = SSD_WIDTH + 2 * SSD_GROUPS * SSD_STATE
D_FF = 2816
FFN_CONV = 3
IN_SPLITS = (GLA_KEY, GLA_KEY, GLA_WIDTH, GLA_WIDTH, GLA_GATE_RANK,
             SGU_WIDTH, SGU_WIDTH,
             SSD_WIDTH, SSD_CONV_DIM, SSD_HEADS)
D_IN_PROJ = sum(IN_SPLITS)
LN_EPS = 1e-5
DEEPNORM_ALPHA = (2 * DEPTH) ** 0.25
DEEPNORM_BETA = (8 * DEPTH) ** -0.25

kernel_name = 'hybrid_gla_sgu_ssd_deepnorm'

F32 = jnp.float32


def _offsets(sizes):
    out, acc = [], 0
    for s in sizes[:-1]:
        acc += s
        out.append(acc)
    return out


def layer_norm(x, g, b):
    xf = x.astype(F32)
    mu = jnp.mean(xf, axis=-1, keepdims=True)
    var = jnp.mean(jnp.square(xf - mu), axis=-1, keepdims=True)
    return ((xf - mu) * lax.rsqrt(var + LN_EPS) * g + b).astype(x.dtype)


def rms_norm(x, g):
    xf = x.astype(F32)
    ms = jnp.mean(jnp.square(xf), axis=-1, keepdims=True)
    return (xf * lax.rsqrt(ms + LN_EPS) * g).astype(x.dtype)


def causal_dwconv(x, w, b):
    k = w.shape[0]
    y = lax.conv_general_dilated(x, w[:, None, :], window_strides=(1,), padding=[(k - 1, 0)],
                                 dimension_numbers=('NWC', 'WIO', 'NWC'),
                                 feature_group_count=x.shape[-1])
    return y + b


def segsum(a):
    t = a.shape[-1]
    aa = jnp.broadcast_to(a[..., :, None], a.shape + (t,))
    aa = jnp.where(jnp.tril(jnp.ones((t, t), bool), -1), aa, 0.0)
    cs = jnp.cumsum(aa, axis=-2)
    return jnp.where(jnp.tril(jnp.ones((t, t), bool)), cs, -jnp.inf)


def gla_mixer(q, k, v, g, gate_lr, w_gate, b_gate, norm_g):
    bsz, s, _ = q.shape
    n = s // GLA_CHUNK
    log_a = jax.nn.log_sigmoid((gate_lr @ w_gate + b_gate).astype(F32)) / GLA_TAU

    def heads(t, d):
        return t.astype(F32).reshape(bsz, n, GLA_CHUNK, GLA_HEADS, d).transpose(1, 0, 3, 2, 4)

    qh = heads(q, GLA_DK) * (GLA_DK ** -0.5)
    kh = heads(k, GLA_DK)
    vh = heads(v, GLA_DV)
    ah = heads(log_a, GLA_DK)
    causal = jnp.tril(jnp.ones((GLA_CHUNK, GLA_CHUNK), bool))[:, :, None]

    def step(state, inp):
        qc, kc, vc, ac = inp
        cum = jnp.cumsum(ac, axis=-2)
        o_inter = jnp.einsum('bhik,bhkv->bhiv', qc * jnp.exp(cum), state)
        diff = cum[:, :, :, None, :] - cum[:, :, None, :, :]
        decay = jnp.exp(jnp.where(causal, diff, -jnp.inf))
        scores = jnp.einsum('bhik,bhjk,bhijk->bhij', qc, kc, decay)
        o_intra = jnp.einsum('bhij,bhjv->bhiv', scores, vc)
        last = cum[:, :, -1:, :]
        state = (jnp.exp(last[:, :, 0, :])[..., None] * state
                 + jnp.einsum('bhjk,bhjv->bhkv', kc * jnp.exp(last - cum), vc))
        return state, o_inter + o_intra

    init = jnp.zeros((bsz, GLA_HEADS, GLA_DK, GLA_DV), F32)
    _, o = lax.scan(step, init, (qh, kh, vh, ah))
    o = o.transpose(1, 0, 3, 2, 4).reshape(bsz, s, GLA_HEADS, GLA_DV)
    o = rms_norm(o, norm_g).reshape(bsz, s, GLA_WIDTH) * jax.nn.silu(g.astype(F32))
    return o.astype(q.dtype)


def sgu_mixer(u, v, norm_g, norm_b, w_s, b_s):
    bsz, s, _ = u.shape
    n = s // SGU_CHUNK
    u = jax.nn.gelu(u)
    v = layer_norm(jax.nn.gelu(v), norm_g, norm_b)
    vh = v.reshape(bsz, n, SGU_CHUNK, SGU_GROUPS, SGU_DG)
    w = w_s * jnp.tril(jnp.ones((SGU_CHUNK, SGU_CHUNK), w_s.dtype))
    mixed = jnp.einsum('gts,bnsgc->bntgc', w, vh) + b_s.T[None, None, :, :, None]
    return u * mixed.reshape(bsz, s, SGU_WIDTH)


def ssd_mixer(z, xbc, dt_raw, conv_w, conv_b, dt_bias, a_log, d_skip, norm_g):
    bsz, s, _ = z.shape
    n, l = s // SSD_CHUNK, SSD_CHUNK
    g, r, p, ns = SSD_GROUPS, SSD_HEADS // SSD_GROUPS, SSD_HEAD_DIM, SSD_STATE
    xbc = jax.nn.silu(causal_dwconv(xbc, conv_w, conv_b))
    xs, bm, cm = jnp.split(xbc, [SSD_WIDTH, SSD_WIDTH + g * ns], axis=-1)
    dt = jax.nn.softplus(dt_raw.astype(F32) + dt_bias)
    a_cont = -jnp.exp(a_log.astype(F32))
    x = xs.astype(F32).reshape(bsz, n, l, g, r, p)
    bm = bm.astype(F32).reshape(bsz, n, l, g, ns)
    cm = cm.astype(F32).reshape(bsz, n, l, g, ns)
    xdt = x * dt.reshape(bsz, n, l, g, r)[..., None]
    a = (dt * a_cont).reshape(bsz, n, l, g, r).transpose(0, 3, 4, 1, 2)
    a_cs = jnp.cumsum(a, axis=-1)
    lmat = jnp.exp(segsum(a))
    cb = jnp.einsum('bclgn,bcsgn->bgcls', cm, bm)
    y_diag = jnp.einsum('bgcls,bgrcls,bcsgrp->bclgrp', cb, lmat, xdt)
    decay_states = jnp.exp(a_cs[..., -1:] - a_cs)
    states = jnp.einsum('bclgn,bgrcl,bclgrp->bcgrpn', bm, decay_states, xdt)
    states = jnp.concatenate([jnp.zeros_like(states[:, :1]), states], axis=1)
    chunk_decay = jnp.exp(segsum(jnp.pad(a_cs[..., -1], ((0, 0), (0, 0), (0, 0), (1, 0)))))
    states = jnp.einsum('bgrzc,bcgrpn->bzgrpn', chunk_decay, states)[:, :-1]
    y_off = jnp.einsum('bclgn,bcgrpn,bgrcl->bclgrp', cm, states, jnp.exp(a_cs))
    y = y_diag + y_off + x * d_skip.astype(F32).reshape(g, r)[:, :, None]
    y = y.reshape(bsz, s, SSD_WIDTH) * jax.nn.silu(z.astype(F32))
    y = rms_norm(y.reshape(bsz, s, g, SSD_WIDTH // g), norm_g.reshape(g, SSD_WIDTH // g))
    return y.reshape(bsz, s, SSD_WIDTH).astype(z.dtype)


def conv_ffn(x, w_up, conv_w, conv_b, w_down):
    h = causal_dwconv(x @ w_up, conv_w, conv_b)
    gate, val = jnp.split(h, 2, axis=-1)
    return (jax.nn.silu(gate) * val) @ w_down


def setup_inputs(seed: int = 0) -> dict:
    key = jax.random.key(seed)
    ks = iter(jax.random.split(key, 32))
    nrm = lambda shape, scale: jax.random.normal(next(ks), shape, F32) * scale
    dt0 = jnp.exp(jax.random.uniform(next(ks), (DEPTH, SSD_HEADS), F32)
                  * (math.log(0.1) - math.log(1e-3)) + math.log(1e-3))
    return {
        'x': nrm((BATCH, SEQ, D_MODEL), 1.0),
        'ln_in_g': 1.0 + nrm((D_MODEL,), 0.02),
        'ln_in_b': nrm((D_MODEL,), 0.02),
        'w_in': nrm((DEPTH, D_MODEL, D_IN_PROJ), D_MODEL ** -0.5),
        'gla_w_gate': nrm((DEPTH, GLA_GATE_RANK, GLA_KEY), GLA_GATE_RANK ** -0.5),
        'gla_b_gate': nrm((DEPTH, GLA_KEY), 0.1),
        'gla_norm_g': 1.0 + nrm((DEPTH, GLA_DV), 0.02),
        'sgu_norm_g': 1.0 + nrm((DEPTH, SGU_WIDTH), 0.02),
        'sgu_norm_b': nrm((DEPTH, SGU_WIDTH), 0.02),
        'sgu_w': nrm((DEPTH, SGU_GROUPS, SGU_CHUNK, SGU_CHUNK), SGU_CHUNK ** -0.5),
        'sgu_b': 1.0 + nrm((DEPTH, SGU_GROUPS, SGU_CHUNK), 0.01),
        'ssd_conv_w': nrm((DEPTH, SSD_CONV, SSD_CONV_DIM), SSD_CONV ** -0.5),
        'ssd_conv_b': nrm((DEPTH, SSD_CONV_DIM), 0.02),
        'ssd_dt_bias': dt0 + jnp.log(-jnp.expm1(-dt0)),
        'ssd_a_log': jnp.log(jax.random.uniform(next(ks), (DEPTH, SSD_HEADS), F32, 1.0, 16.0)),
        'ssd_d': 1.0 + nrm((DEPTH, SSD_HEADS), 0.1),
        'ssd_norm_g': 1.0 + nrm((DEPTH, SSD_WIDTH), 0.02),
        'w_out': nrm((DEPTH, D_MIX, D_MODEL), DEEPNORM_BETA * D_MIX ** -0.5),
        'ln1_g': 1.0 + nrm((DEPTH, D_MODEL), 0.02),
        'ln1_b': nrm((DEPTH, D_MODEL), 0.02),
        'ffn_w_up': nrm((DEPTH, D_MODEL, 2 * D_FF), D_MODEL ** -0.5),
        'ffn_conv_w': nrm((DEPTH, FFN_CONV, 2 * D_FF), FFN_CONV ** -0.5),
        'ffn_conv_b': nrm((DEPTH, 2 * D_FF), 0.02),
        'ffn_w_down': nrm((DEPTH, D_FF, D_MODEL), DEEPNORM_BETA * D_FF ** -0.5),
        'ln2_g': 1.0 + nrm((DEPTH, D_MODEL), 0.02),
        'ln2_b': nrm((DEPTH, D_MODEL), 0.02),
    }


def reference(x, ln_in_g, ln_in_b, w_in, gla_w_gate, gla_b_gate, gla_norm_g,
              sgu_norm_g, sgu_norm_b, sgu_w, sgu_b,
              ssd_conv_w, ssd_conv_b, ssd_dt_bias, ssd_a_log, ssd_d, ssd_norm_g,
              w_out, ln1_g, ln1_b, ffn_w_up, ffn_conv_w, ffn_conv_b, ffn_w_down,
              ln2_g, ln2_b):
    h = layer_norm(x, ln_in_g, ln_in_b)
    offs = _offsets(IN_SPLITS)
    for i in range(DEPTH):
        proj = h @ w_in[i]
        q, k, v, g, g_lr, su, sv, z, xbc, dt = jnp.split(proj, offs, axis=-1)
        o_gla = gla_mixer(q, k, v, g, g_lr, gla_w_gate[i], gla_b_gate[i], gla_norm_g[i])
        o_sgu = sgu_mixer(su, sv, sgu_norm_g[i], sgu_norm_b[i], sgu_w[i], sgu_b[i])
        o_ssd = ssd_mixer(z, xbc, dt, ssd_conv_w[i], ssd_conv_b[i], ssd_dt_bias[i],
                          ssd_a_log[i], ssd_d[i], ssd_norm_g[i])
        mix = jnp.concatenate([o_gla, o_sgu, o_ssd], axis=-1) @ w_out[i]
        h = layer_norm(DEEPNORM_ALPHA * h + mix, ln1_g[i], ln1_b[i])
        ffn = conv_ffn(h, ffn_w_up[i], ffn_conv_w[i], ffn_conv_b[i], ffn_w_down[i])
        h = layer_norm(DEEPNORM_ALPHA * h + ffn, ln2_g[i], ln2_b[i])
    return h
```

```python
import types
import numpy as np
from contextlib import ExitStack
import concourse.bass as bass
import concourse.mybir as mybir
from concourse.bass_utils import run_bass_kernel_spmd

F32 = mybir.dt.float32
BF16 = mybir.dt.bfloat16
AF = mybir.ActivationFunctionType
ALU = mybir.AluOpType
AX = mybir.AxisListType

D = 1024
DEPTH = 2
DIN = 2840
DFF = 2816
NJ = DFF // 128
ALPHA = float((2 * DEPTH) ** 0.25)
EPS = 1e-5
NCORES = 8

O_Q, O_K, O_V, O_G, O_GLR, O_SU, O_SV, O_Z, O_XBC, O_DT = 0, 128, 256, 512, 768, 784, 1040, 1296, 1808, 2832


class Buf:
    def __init__(self, name, psum=False):
        self.name = name
        self.psum = psum
        self.writer = None
        self.readers = {}
        self.aliases = []
        self.dcount = 0


class Prog:
    def __init__(self, nc, stack):
        self.nc = nc
        self.stack = stack
        self.eng = {'pe': nc.tensor, 'act': nc.scalar, 'dve': nc.vector, 'pool': nc.gpsimd, 'sp': nc.sync}
        self.sems = {}
        self.cnt = {}
        self.seen = {e: {} for e in self.eng}
        for e in ('pe', 'act', 'dve', 'pool'):
            self.sems[e] = stack.enter_context(nc.semaphore("s_" + e))
            self.cnt[e] = 0
        self.nwaits = 0
        self.nins = 0
        self.dtot = {}
        self.nodes = []

    def _sem(self, key):
        if key not in self.sems:
            self.sems[key] = self.stack.enter_context(self.nc.semaphore("d_" + key))
        return self.sems[key]

    def _deps(self, eng, reads, writes):
        raw = {}
        oth = {}

        def add(d, ev):
            if ev is None:
                return
            k, v = ev
            if d.get(k, 0) < v:
                d[k] = v
        for b in reads:
            add(raw, b.writer)
            for a in b.aliases:
                add(raw, a.writer)
            if b.psum:
                for k, v in b.readers.items():
                    add(oth, (k, v))
        for b in writes:
            for bb in [b] + b.aliases:
                add(oth, bb.writer)
                for k, v in bb.readers.items():
                    add(oth, (k, v))
        need = {}
        for k, v in raw.items():
            if k == eng:
                if eng == 'pe':
                    continue
                if eng != 'pool' and self.cnt[eng] - v >= 8:
                    continue
            if need.get(k, 0) < v:
                need[k] = v
        for k, v in oth.items():
            if k == eng:
                continue
            if need.get(k, 0) < v:
                need[k] = v
        e = self.eng[eng]
        for k, v in need.items():
            if self.seen[eng].get(k, 0) < v:
                e.wait_ge(self._sem(k), v)
                self.seen[eng][k] = v
                self.nwaits += 1

    def _emit_op(self, eng, fn, reads=(), writes=()):
        self._deps(eng, reads, writes)
        ins = fn(self.eng[eng])
        self.cnt[eng] += 1
        ins.then_inc(self.sems[eng], 1)
        ev = (eng, self.cnt[eng])
        for b in reads:
            if b.readers.get(eng, 0) < ev[1]:
                b.readers[eng] = ev[1]
        for b in writes:
            b.writer = ev
            b.readers = {}
        self.nins += 1
        return ins

    def _emit_dma(self, eng, out, in_, reads, writes, **kw):
        self._deps(eng, reads, writes)
        dst = writes[0]
        key = "dma_" + dst.name
        sem = self._sem(key)
        ins = self.eng[eng].dma_start(out=out, in_=in_, **kw)
        cnt = self.dtot.get(key, 0) + 16
        self.dtot[key] = cnt
        ins.then_inc(sem, 16)
        ev = (key, cnt)
        for b in reads:
            if b.readers.get(key, 0) < ev[1]:
                b.readers[key] = ev[1]
        for b in writes:
            b.writer = ev
            b.readers = {}
        return ins

    COST = {'pe': 0.21, 'act': 0.5, 'dve': 0.5, 'pool': 1.05, 'sp': 0.1}
    WINDOW = 600
    SLACK = 0.6

    @staticmethod
    def _freeze(fn):
        if not fn.__closure__:
            return fn
        cells = []
        for c in fn.__closure__:
            try:
                cells.append(types.CellType(c.cell_contents))
            except ValueError:
                cells.append(c)
        return types.FunctionType(fn.__code__, fn.__globals__, fn.__name__, fn.__defaults__, tuple(cells))

    def op(self, eng, fn, reads=(), writes=(), cost=None):
        tbl = None
        if eng == 'act':
            names = fn.__code__.co_names
            if 'Silu' in names:
                tbl = 'silu'
            elif 'Sigmoid' in names:
                tbl = 'sigmoid'
            elif 'Exp' in names or 'Ln' in names:
                tbl = 'exp'
        self.nodes.append(('op', eng, self._freeze(fn), list(reads), list(writes), tbl, cost))

    def dma(self, eng, out, in_, reads, writes, **kw):
        self.nodes.append(('dma', eng, (out, in_), list(reads), list(writes), kw, None))

    def flush(self):
        nodes = self.nodes
        self.nodes = []
        n = len(nodes)
        if n == 0:
            return
        lastw = {}
        readers = {}
        deps = [None] * n
        succ = [[] for _ in range(n)]
        for i, (kind, eng, fn, reads, writes, kw, cost) in enumerate(nodes):
            d = set()
            for b in reads:
                for bb in [b] + b.aliases:
                    w = lastw.get(id(bb))
                    if w is not None:
                        d.add(w)
            for b in writes:
                for bb in [b] + b.aliases:
                    w = lastw.get(id(bb))
                    if w is not None:
                        d.add(w)
                    for r in readers.get(id(bb), ()):
                        d.add(r)
            d.discard(i)
            deps[i] = d
            for j in d:
                succ[j].append(i)
            for b in reads:
                readers.setdefault(id(b), []).append(i)
            for b in writes:
                lastw[id(b)] = i
                readers[id(b)] = []
        ndep = [len(d) for d in deps]
        lp = [0.0] * n
        for i in range(n - 1, -1, -1):
            kind_, eng_, _, _, _, _, cost_ = nodes[i]
            c_ = (cost_ if cost_ is not None else self.COST[eng_]) if kind_ == 'op' else 3.0
            m_ = 0.0
            for k in succ[i]:
                if lp[k] > m_:
                    m_ = lp[k]
            lp[i] = c_ + m_ + 0.35
        finish = [0.0] * n
        etime = {e: 0.0 for e in self.eng}
        ready = [i for i in range(n) if ndep[i] == 0]
        cur_tbl = None
        done = [False] * n
        lo = 0
        nsched = 0
        while nsched < n:
            while lo < n and done[lo]:
                lo += 1
            cands = []
            tmin = None
            for i in ready:
                if i > lo + self.WINDOW:
                    continue
                eng = nodes[i][1]
                st = etime[eng]
                for j in deps[i]:
                    f = finish[j] + (0.0 if nodes[j][1] == eng else 0.35)
                    if f > st:
                        st = f
                if eng == 'act' and nodes[i][0] == 'op' and nodes[i][5] is not None and nodes[i][5] != cur_tbl:
                    st += 1.3
                cands.append((st, i))
                if tmin is None or st < tmin:
                    tmin = st
            best, bkey = None, None
            for (st, i) in cands:
                if st <= tmin + self.SLACK:
                    key = (-lp[i], i)
                    if bkey is None or key < bkey:
                        best, bkey, bst = i, key, st
            i = best
            kind, eng, fn, reads, writes, kw, cost = nodes[i]
            st = bst
            if kind == 'op':
                c = cost if cost is not None else self.COST[eng]
                if eng == 'act' and kw is not None:
                    if kw != cur_tbl:
                        c += 1.3
                    cur_tbl = kw
                self._emit_op(eng, fn, reads, writes)
                etime[eng] = st + c
                finish[i] = st + c
            else:
                self._emit_dma(eng, fn[0], fn[1], reads, writes, **kw)
                etime[eng] = st + 0.1
                finish[i] = st + 3.0
            done[i] = True
            nsched += 1
            ready.remove(i)
            for k in succ[i]:
                ndep[k] -= 1
                if ndep[k] == 0:
                    ready.append(k)

    def barrier(self):
        self.flush()
        tot = dict(self.cnt)
        tot.update(self.dtot)
        for eng in self.eng:
            e = self.eng[eng]
            for k, v in tot.items():
                if v == 0:
                    continue
                if self.seen[eng].get(k, 0) < v:
                    e.wait_ge(self._sem(k), v)
                    self.seen[eng][k] = v
                    self.nwaits += 1

    def wait_all(self, eng, bufs):
        self.flush()
        self._deps(eng, bufs, ())


def build_program(ntok=4096, seqlen=2048, phases=None, dbg=False, mixers=('gla', 'sgu', 'ssd')):
    if phases is None:
        phases = []
        for l in range(DEPTH):
            phases += [('A', l), ('B', l)]
    nc = bass.Bass("TRN2", target_bir_lowering=False)
    dt_in = lambda name, shape: nc.dram_tensor(name, shape, F32, kind="ExternalInput").ap()
    x_d = dt_in("x", [ntok, D])
    ln_in_g = dt_in("ln_in_g", [D])
    ln_in_b = dt_in("ln_in_b", [D])
    w_in_d = dt_in("w_in", [DEPTH, D, DIN])
    gla_w_gate = dt_in("gla_w_gate", [DEPTH, 16, 128])
    gla_b_gate = dt_in("gla_b_gate", [DEPTH, 128])
    gla_norm_g = dt_in("gla_norm_g", [DEPTH, 64])
    sgu_norm_g = dt_in("sgu_norm_g", [DEPTH, 256])
    sgu_norm_b = dt_in("sgu_norm_b", [DEPTH, 256])
    sgu_w = dt_in("sgu_w", [DEPTH, 4, 128, 128])
    sgu_b = dt_in("sgu_b", [DEPTH, 4, 128])
    ssd_conv_w = dt_in("ssd_conv_w", [DEPTH, 4, 1024])
    ssd_conv_b = dt_in("ssd_conv_b", [DEPTH, 1024])
    ssd_dt_bias = dt_in("ssd_dt_bias", [DEPTH, 8])
    ssd_a_log = dt_in("ssd_a_log", [DEPTH, 8])
    ssd_d = dt_in("ssd_d", [DEPTH, 8])
    ssd_norm_g = dt_in("ssd_norm_g", [DEPTH, 512])
    w_out_d = dt_in("w_out", [DEPTH, D, D])
    ln1_g = dt_in("ln1_g", [DEPTH, D])
    ln1_b = dt_in("ln1_b", [DEPTH, D])
    w_up_d = dt_in("ffn_w_up", [DEPTH, D, 2 * DFF])
    ffn_conv_w = dt_in("ffn_conv_w", [DEPTH, 3, 2 * DFF])
    ffn_conv_b = dt_in("ffn_conv_b", [DEPTH, 2 * DFF])
    w_down_d = dt_in("ffn_w_down", [DEPTH, DFF, D])
    ln2_g = dt_in("ln2_g", [DEPTH, D])
    ln2_b = dt_in("ln2_b", [DEPTH, D])
    y_d = nc.dram_tensor("y", [ntok, D], F32, kind="ExternalOutput").ap()
    hA_d = nc.dram_tensor("hA", [ntok, D], F32, kind="Internal").ap()
    hB_d = nc.dram_tensor("hB", [ntok, D], F32, kind="Internal").ap()

    stack = ExitStack()
    with stack:
        P = Prog(nc, stack)
        sb = lambda name, shape, dt=F32: stack.enter_context(nc.sbuf_tensor(name, shape, dt))

        B_wdown, B_wup, B_wout, B_win = Buf("wdown"), Buf("wup"), Buf("wout"), Buf("win")

        ident_f = sb("ident_f", [128, 128], F32)
        ident_b = sb("ident_b", [128, 128], BF16)
        B_const = Buf("const")
        lng = sb("lng", [128, D], F32)
        lnb = sb("lnb", [128, D], F32)
        B_ln = Buf("ln")
        B_ln0 = Buf("ln0")
        eps_t = sb("eps_t", [128, 1], F32)

        psum = [stack.enter_context(nc.psum_tensor(f"ps{i}", [128, 512], F32)) for i in range(8)]
        B_ps = [Buf(f"ps{i}", psum=True) for i in range(8)]

        P.op('pool', lambda e: e.memset(ident_f[:], 1.0), writes=[B_const])
        P.op('pool', lambda e: e.affine_select(out=ident_f[:], in_=ident_f[:], pattern=[[-1, 128]],
                                               compare_op=ALU.is_equal, fill=0.0, base=0, channel_multiplier=1),
             reads=[B_const], writes=[B_const])
        P.op('pool', lambda e: e.tensor_copy(out=ident_b[:], in_=ident_f[:]), reads=[B_const], writes=[B_const])
        P.op('pool', lambda e: e.memset(eps_t[:], EPS), writes=[B_const])

        def layernorm(src, dst, g_t, b_t, Bsrc, Bdst, Bg, tmp, Btmp):
            st, mv, sc = tmp
            for hh in range(2):
                P.op('dve', lambda e, hh=hh: e.bn_stats(out=st[:, hh, :], in_=src[:, hh * 512:(hh + 1) * 512]),
                     reads=[Bsrc], writes=[Btmp])
            P.op('dve', lambda e: e.bn_aggr(out=mv[:], in_=st[:].rearrange("p a b -> p (a b)")), reads=[Btmp], writes=[Btmp])
            P.op('act', lambda e: e.activation(out=sc[:, 0:1], in_=mv[:, 1:2], func=AF.Ln, bias=eps_t[:], scale=1.0),
                 reads=[Btmp, B_const], writes=[Btmp])
            P.op('act', lambda e: e.activation(out=sc[:, 1:2], in_=sc[:, 0:1], func=AF.Exp, scale=-0.5),
                 reads=[Btmp], writes=[Btmp])
            P.op('dve', lambda e: e.tensor_scalar(out=sc[:, 2:3], in0=mv[:, 0:1], scalar1=sc[:, 1:2], scalar2=-1.0,
                                                  op0=ALU.mult, op1=ALU.mult), reads=[Btmp], writes=[Btmp])
            P.op('act', lambda e: e.activation(out=dst, in_=src, func=AF.Identity, bias=sc[:, 2:3], scale=sc[:, 1:2]),
                 reads=[Bsrc, Btmp], writes=[Bdst], cost=0.95)
            P.op('dve', lambda e: e.tensor_tensor(out=dst, in0=dst, in1=g_t[:], op=ALU.mult), reads=[Bdst, Bg], writes=[Bdst], cost=1.15)
            P.op('pool', lambda e: e.tensor_tensor(out=dst, in0=dst, in1=b_t[:], op=ALU.add), reads=[Bdst, Bg], writes=[Bdst], cost=2.0)

        def load_ln_consts(g_row, b_row, g_t, b_t, Bg):
            P.dma('sp', g_t[:], g_row.partition_broadcast(128), reads=[], writes=[Bg])
            P.dma('sp', b_t[:], b_row.partition_broadcast(128), reads=[], writes=[Bg])

        def load_weight(dst3, src2, nk, Bw, rows_per=128):
            N = src2.shape[1]
            for k in range(nk):
                c0 = 0
                while c0 < N:
                    c1 = min(N, c0 + 2048)
                    P.dma('pool', dst3[:, k, c0:c1], src2[k * 128:(k + 1) * 128, c0:c1], reads=[], writes=[Bw])
                    c0 = c1

        TB = 256
        ntile = ntok // TB
        B_hA = [Buf(f"hA{i}") for i in range(ntile)]
        B_hB = [Buf(f"hB{i}") for i in range(ntile)]
        B_y = [Buf(f"y{i}") for i in range(ntile)]

        def phase_b(layer, src_d, Bsrc_tiles, dst_d, Bdst_tiles, pre_ln):
            bstack = ExitStack()
            with bstack:
                sbb = lambda name, shape, dt=F32: bstack.enter_context(nc.sbuf_tensor(f"b{layer}_{name}", shape, dt))
                w_up = sbb("w_up", [128, 8, 2 * DFF], BF16)
                w_down = sbb("w_down", [128, NJ, D], BF16)
                NG = 4
                JG = 6
                B_wupg = [Buf(f"wup{g}") for g in range(NG)]
                B_wdng = [Buf(f"wdn{g}") for g in range(NG)]
                for g in range(NG):
                    j0, j1 = g * JG, min(NJ, (g + 1) * JG)
                    for a in range(2):
                        c0, c1 = a * DFF + j0 * 128, a * DFF + j1 * 128
                        for k in range(8):
                            P.dma('pool', w_up[:, k, c0:c1], w_up_d[layer][k * 128:(k + 1) * 128, c0:c1], reads=[], writes=[B_wupg[g]])
                    for j in range(j0, j1):
                        P.dma('pool', w_down[:, j, :], w_down_d[layer][j * 128:(j + 1) * 128, :], reads=[], writes=[B_wdng[g]])
                load_ln_consts(ln2_g[layer], ln2_b[layer], lng, lnb, B_ln)
                if pre_ln:
                    lng0 = sbb("lng0", [128, D], F32)
                    lnb0 = sbb("lnb0", [128, D], F32)
                    load_ln_consts(ln_in_g, ln_in_b, lng0, lnb0, B_ln0)
                cwraw = sbb("cwraw", [44, 4, 128])
                cw = sbb("cw", [128, 4, 44])
                B_cwraw, B_cw = Buf("cwraw"), Buf("cw")
                for k in range(3):
                    P.dma('sp', cwraw[:, k, :], ffn_conv_w[layer, k].rearrange("(c p) -> c p", p=128), reads=[], writes=[B_cwraw])
                P.dma('sp', cwraw[:, 3, :], ffn_conv_b[layer].rearrange("(c p) -> c p", p=128), reads=[], writes=[B_cwraw])
                for k in range(4):
                    P.op('pe', lambda e, k=k: e.transpose(out=psum[6][:, 0:44], in_=cwraw[:, k, :], identity=ident_f[0:44, 0:44]),
                         reads=[B_cwraw, B_const], writes=[B_ps[6]])
                    P.op('dve', lambda e, k=k: e.tensor_copy(out=cw[:, k, :], in_=psum[6][:, 0:44]), reads=[B_ps[6]], writes=[B_cw])

                xin = [sbb(f"xin{i}", [128, 2, D]) for i in range(2)]
                B_xin = [Buf(f"xin{i}") for i in range(2)]
                xbf = sbb("xbf", [128, 2, D], BF16)
                B_xbf = Buf("xbf")
                hT = [sbb(f"hT{i}", [128, 8, TB], BF16) for i in range(2)]
                B_hT = [Buf(f"hT{i}") for i in range(2)]
                halo = sbb("halo", [128, NJ, 2, 2])
                B_halo = Buf("halo")
                NP = 3
                xs = [sbb(f"xs{i}", [128, 2, 2 + TB]) for i in range(NP)]
                B_xs = [Buf(f"xs{i}") for i in range(NP)]
                cg = [sbb(f"cg{i}", [128, TB]) for i in range(NP)]
                cv = [sbb(f"cv{i}", [128, TB]) for i in range(NP)]
                B_cg = [Buf(f"cg{i}") for i in range(NP)]
                B_cv = [Buf(f"cv{i}") for i in range(NP)]
                sg = [sbb(f"sg{i}", [128, TB]) for i in range(NP)]
                B_sg = [Buf(f"sg{i}") for i in range(NP)]
                NA = 4
                aT = [sbb(f"aT{i}", [128, TB], BF16) for i in range(NA)]
                B_aT = [Buf(f"aT{i}") for i in range(NA)]
                rr = [sbb(f"rr{i}", [128, D]) for i in range(2)]
                B_rr = [Buf(f"rr{i}") for i in range(2)]
                lst = sbb("lst", [128, 2, 6]); lmv = sbb("lmv", [128, 2]); lsc = sbb("lsc", [128, 4])
                B_ltmp = Buf("ltmp")
                ltmp = (lst, lmv, lsc)
                pT = psum[7][:].bitcast(BF16)

                def load(t):
                    tok0 = t * TB
                    xi, Bxi = xin[t % 2], B_xin[t % 2]
                    P.dma('sp', xi[:], src_d[tok0:tok0 + TB, :].rearrange("(c p) f -> p c f", p=128),
                          reads=[Bsrc_tiles[t]], writes=[Bxi])
                    if pre_ln:
                        for c in range(2):
                            layernorm(xi[:, c, :], xi[:, c, :], lng0, lnb0, Bxi, Bxi, B_ln0, ltmp, B_ltmp)

                def prologue(t):
                    xi, Bxi = xin[t % 2], B_xin[t % 2]
                    h_, Bh_ = hT[t % 2], B_hT[t % 2]
                    P.op('act', lambda e: e.copy(out=xbf[:], in_=xi[:]), reads=[Bxi], writes=[B_xbf])
                    for c in range(2):
                        for fc in range(8):
                            P.op('pe', lambda e, c=c, fc=fc: e.transpose(
                                out=pT[:, fc * 128:(fc + 1) * 128], in_=xbf[:, c, fc * 128:(fc + 1) * 128], identity=ident_b[:]),
                                reads=[B_xbf, B_const], writes=[B_ps[7]])
                        P.op('dve', lambda e, c=c: e.tensor_copy(
                            out=h_[:, :, c * 128:(c + 1) * 128], in_=pT[:, :].rearrange("p (q t) -> p q t", q=8)),
                            reads=[B_ps[7]], writes=[Bh_])

                def up(t, j):
                    h_, Bh_ = hT[t % 2], B_hT[t % 2]
                    bk = 4 + j % NP
                    pu3 = psum[bk][:].rearrange("p (a t) -> p a t", a=2)
                    for a in range(2):
                        col0 = a * DFF + j * 128
                        for kc in range(8):
                            P.op('pe', lambda e, a=a, kc=kc, col0=col0: e.matmul(
                                pu3[:, a, :], lhsT=w_up[:, kc, col0:col0 + 128], rhs=h_[:, kc, :],
                                start=(kc == 0), stop=(kc == 7)),
                                reads=[B_wupg[j // JG], Bh_], writes=[B_ps[bk]])

                def ew(t, j):
                    bk = 4 + j % NP
                    Bpu = B_ps[bk]
                    pu3 = psum[bk][:].rearrange("p (a t) -> p a t", a=2)
                    x_, Bx_ = xs[j % NP], B_xs[j % NP]
                    P.op('act', lambda e: e.copy(out=x_[:, :, 2:2 + TB], in_=pu3), reads=[Bpu], writes=[Bx_])
                    P.op('pool', lambda e: e.tensor_copy(out=x_[:, :, 0:2], in_=halo[:, j, :, :]), reads=[B_halo], writes=[Bx_])
                    P.op('pool', lambda e: e.tensor_copy(out=halo[:, j, :, :], in_=x_[:, :, TB:TB + 2]), reads=[Bx_], writes=[B_halo])
                    for a, (ct, Bct) in enumerate(((cg[j % NP], B_cg[j % NP]), (cv[j % NP], B_cv[j % NP]))):
                        ch = a * NJ + j
                        P.op('act', lambda e, a=a, ch=ch, ct=ct: e.activation(
                            out=ct[:], in_=pu3[:, a, :], func=AF.Identity, bias=cw[:, 3, ch:ch + 1], scale=cw[:, 2, ch:ch + 1]),
                            reads=[Bpu, B_cw], writes=[Bct])
                        P.op('dve', lambda e, a=a, ch=ch, ct=ct: e.scalar_tensor_tensor(
                            out=ct[:], in0=x_[:, a, 1:1 + TB], scalar=cw[:, 1, ch:ch + 1], in1=ct[:], op0=ALU.mult, op1=ALU.add),
                            reads=[Bx_, B_cw, Bct], writes=[Bct])
                        P.op('dve', lambda e, a=a, ch=ch, ct=ct: e.scalar_tensor_tensor(
                            out=ct[:], in0=x_[:, a, 0:TB], scalar=cw[:, 0, ch:ch + 1], in1=ct[:], op0=ALU.mult, op1=ALU.add),
                            reads=[Bx_, B_cw, Bct], writes=[Bct])
                    s_, Bs_ = sg[j % NP], B_sg[j % NP]
                    P.op('act', lambda e: e.activation(out=s_[:], in_=cg[j % NP][:], func=AF.Silu),
                         reads=[B_cg[j % NP]], writes=[Bs_])
                    a_, Ba_ = aT[j % NA], B_aT[j % NA]
                    P.op('pool', lambda e: e.tensor_tensor(out=a_[:], in0=s_[:], in1=cv[j % NP][:], op=ALU.mult),
                         reads=[Bs_, B_cv[j % NP]], writes=[Ba_])

                def down(t, j):
                    a_, Ba_ = aT[j % NA], B_aT[j % NA]
                    for c in range(2):
                        for hf in range(2):
                            bk = c * 2 + hf
                            P.op('pe', lambda e, c=c, hf=hf, bk=bk: e.matmul(
                                psum[bk][:], lhsT=a_[:, c * 128:(c + 1) * 128], rhs=w_down[:, j, hf * 512:(hf + 1) * 512],
                                start=(j == 0), stop=(j == NJ - 1)),
                                reads=[Ba_, B_wdng[j // JG]], writes=[B_ps[bk]])

                def epilogue(t):
                    tok0 = t * TB
                    xi, Bxi = xin[t % 2], B_xin[t % 2]
                    for c in range(2):
                        r_, Br_ = rr[c], B_rr[c]
                        for hf in range(2):
                            bk = c * 2 + hf
                            P.op('dve', lambda e, c=c, hf=hf, bk=bk, r_=r_: e.scalar_tensor_tensor(
                                out=r_[:, hf * 512:(hf + 1) * 512], in0=xi[:, c, hf * 512:(hf + 1) * 512], scalar=ALPHA,
                                in1=psum[bk][:], op0=ALU.mult, op1=ALU.add),
                                reads=[Bxi, B_ps[bk]], writes=[Br_])
                    for c in range(2):
                        layernorm(rr[c][:], xi[:, c, :], lng, lnb, B_rr[c], Bxi, B_ln, ltmp, B_ltmp)
                    P.dma('sp', dst_d[tok0:tok0 + TB, :].rearrange("(c p) f -> p c f", p=128), xi[:],
                          reads=[Bxi], writes=[Bdst_tiles[t]])

                load(0)
                prologue(0)
                for t in range(ntile):
                    tok0 = t * TB
                    if t + 1 < ntile:
                        load(t + 1)
                    if tok0 % seqlen == 0:
                        P.op('pool', lambda e: e.memset(halo[:], 0.0), writes=[B_halo])
                    for j in range(min(NP - 1, NJ)):
                        up(t, j)
                    for j in range(NJ):
                        if j + NP - 1 < NJ:
                            up(t, j + NP - 1)
                        ew(t, j)
                        down(t, j)
                        if j == NJ - 4 and t + 1 < ntile:
                            prologue(t + 1)
                    epilogue(t)

        def phase_a(layer, src_d, Bsrc_tiles, dst_d, Bdst_tiles, pre_ln):
            NT = TB
            NCH = NT // 128
            astack = ExitStack()
            with astack:
                sba = lambda name, shape, dt=F32: astack.enter_context(nc.sbuf_tensor(f"a{layer}_{name}", shape, dt))
                w_in = sba("w_in", [128, 8, DIN], BF16)
                w_out = sba("w_out", [128, 8, D], BF16)
                WG = [(0, 784), (784, 1808), (1808, DIN)]
                B_wing = [Buf(f"win{g}") for g in range(3)]
                for g, (c0, c1) in enumerate(WG):
                    for k in range(8):
                        P.dma('pool', w_in[:, k, c0:c1], w_in_d[layer][k * 128:(k + 1) * 128, c0:c1], reads=[], writes=[B_wing[g]])

                def Bw(c0):
                    for g, (a0, a1) in enumerate(WG):
                        if a0 <= c0 < a1:
                            return B_wing[g]
                load_weight(w_out, w_out_d[layer], 8, B_wout)
                load_ln_consts(ln1_g[layer], ln1_b[layer], lng, lnb, B_ln)
                if pre_ln:
                    lng0 = sba("lng0", [128, D], F32)
                    lnb0 = sba("lnb0", [128, D], F32)
                    load_ln_consts(ln_in_g, ln_in_b, lng0, lnb0, B_ln0)
                Bc = Buf("aconst")
                tri_f = sba("tri_f", [128, 128])
                maskge_b = sba("maskge_b", [128, 128], BF16)
                maskgt_f = sba("maskgt_f", [128, 128])
                ones_f = sba("ones_f", [128, 128])
                ones_b = sba("ones_b", [128, 128], BF16)
                one_t = sba("one_t", [128, 1])
                P.op('pool', lambda e: e.memset(ones_f[:], 1.0), writes=[Bc])
                P.op('pool', lambda e: e.memset(ones_b[:], 1.0), writes=[Bc])
                P.op('pool', lambda e: e.memset(one_t[:], 1.0), writes=[Bc])
                hm = sba("hm", [64, 4])
                P.op('pool', lambda e: e.memset(hm[:], 0.0), writes=[Bc])
                for h in range(4):
                    hb = (h % 2) * 32
                    P.op('pool', lambda e, h=h, hb=hb: e.memset(hm[hb:hb + 32, h:h + 1], 32.0 ** -0.5), writes=[Bc])
                P.op('pool', lambda e: e.affine_select(out=tri_f[:], in_=ones_f[:], pattern=[[1, 128]], compare_op=ALU.is_ge,
                                                       fill=0.0, base=0, channel_multiplier=-1), reads=[Bc], writes=[Bc])
                P.op('pool', lambda e: e.tensor_copy(out=maskge_b[:], in_=tri_f[:]), reads=[Bc], writes=[Bc])
                P.op('pool', lambda e: e.affine_select(out=maskgt_f[:], in_=ones_f[:], pattern=[[-1, 128]], compare_op=ALU.is_gt,
                                                       fill=0.0, base=0, channel_multiplier=1), reads=[Bc], writes=[Bc])
                craw = sba("craw", [44, 128])
                ccol = sba("ccol", [128, 44])
                B_craw = Buf("craw")
                P.dma('sp', craw[0:32, :], ssd_conv_w[layer].rearrange("k (c p) -> (k c) p", p=128), reads=[], writes=[B_craw])
                P.dma('sp', craw[32:40, :], ssd_conv_b[layer].rearrange("(c p) -> c p", p=128), reads=[], writes=[B_craw])
                P.dma('sp', craw[40:42, :], sgu_norm_g[layer].rearrange("(c p) -> c p", p=128), reads=[], writes=[B_craw])
                P.dma('sp', craw[42:44, :], sgu_norm_b[layer].rearrange("(c p) -> c p", p=128), reads=[], writes=[B_craw])
                P.op('pe', lambda e: e.transpose(out=psum[0][:, 0:44], in_=craw[:, :], identity=ident_f[0:44, 0:44]),
                     reads=[B_craw, B_const], writes=[B_ps[0]])
                P.op('dve', lambda e: e.tensor_copy(out=ccol[:], in_=psum[0][:, 0:44]), reads=[B_ps[0]], writes=[Bc])
                CW = lambda k, fc: ccol[:, k * 8 + fc:k * 8 + fc + 1]
                CB_ = lambda fc: ccol[:, 32 + fc:33 + fc]
                SGG = lambda fc: ccol[:, 40 + fc:41 + fc]
                SGB = lambda fc: ccol[:, 42 + fc:43 + fc]
                dtb_bc = sba("dtb_bc", [128, 8]); acont_bc = sba("acont_bc", [128, 8]); dsk_bc = sba("dsk_bc", [128, 8])
                sng_bc = sba("sng_bc", [128, 512]); gng_bc = sba("gng_bc", [128, 64])
                B_bc = Buf("bcast")
                P.dma('sp', dtb_bc[:], ssd_dt_bias[layer].partition_broadcast(128), reads=[], writes=[B_bc])
                P.dma('sp', acont_bc[:], ssd_a_log[layer].partition_broadcast(128), reads=[], writes=[B_bc])
                P.dma('sp', dsk_bc[:], ssd_d[layer].partition_broadcast(128), reads=[], writes=[B_bc])
                P.dma('sp', sng_bc[:], ssd_norm_g[layer].partition_broadcast(128), reads=[], writes=[B_bc])
                P.dma('sp', gng_bc[:], gla_norm_g[layer].partition_broadcast(128), reads=[], writes=[B_bc])
                P.op('act', lambda e: e.activation(out=acont_bc[:], in_=acont_bc[:], func=AF.Exp), reads=[B_bc], writes=[B_bc])
                P.op('act', lambda e: e.mul(acont_bc[:], acont_bc[:], -1.0), reads=[B_bc], writes=[B_bc])
                wg_f = sba("wg_f", [17, 128]); wg_b = sba("wg_b", [17, 128], BF16)
                B_wg = Buf("wg")
                P.dma('sp', wg_f[0:16, :], gla_w_gate[layer], reads=[], writes=[B_wg])
                P.dma('sp', wg_f[16:17, :], gla_b_gate[layer].rearrange("(a n) -> a n", a=1), reads=[], writes=[B_wg])
                P.op('dve', lambda e: e.tensor_copy(out=wg_b[:], in_=wg_f[:]), reads=[B_wg], writes=[Bc])
                wsg = sba("wsg", [128, 4, 128]); wmT = sba("wmT", [128, 4, 128], BF16)
                bsbc = sba("bsbc", [128, 2, 128]); csg = sba("csg", [128, 2, 128])
                B_wsg = Buf("wsg")
                P.dma('sp', wsg[:], sgu_w[layer].rearrange("g t s -> t g s"), reads=[], writes=[B_wsg])
                for g in range(4):
                    hp = (g % 2) * 64
                    P.dma('sp', bsbc[hp:hp + 64, g // 2, :], sgu_b[layer, g].partition_broadcast(64), reads=[], writes=[B_bc])
                P.op('pool', lambda e: e.affine_select(out=wsg[:], in_=wsg[:], pattern=[[0, 4], [-1, 128]], compare_op=ALU.is_ge,
                                                       fill=0.0, base=0, channel_multiplier=1), reads=[B_wsg], writes=[B_wsg])
                for g in range(4):
                    P.op('pe', lambda e, g=g: e.transpose(out=psum[1][:, g * 128:(g + 1) * 128], in_=wsg[:, g, :], identity=ident_f[:]),
                         reads=[B_wsg, B_const], writes=[B_ps[1]])
                P.op('dve', lambda e: e.tensor_copy(out=wmT[:], in_=psum[1][:].rearrange("p (g t) -> p g t", g=4)),
                     reads=[B_ps[1]], writes=[Bc])
                for g in range(4):
                    P.op('pe', lambda e, g=g: e.matmul(psum[2][:, g * 128:(g + 1) * 128], lhsT=ones_b[:], rhs=wmT[:, g, :],
                                                       start=True, stop=True), reads=[Bc], writes=[B_ps[2]])
                for g in range(4):
                    hp, fc = (g % 2) * 64, g // 2
                    P.op('dve', lambda e, g=g, hp=hp, fc=fc: e.scalar_tensor_tensor(
                        out=csg[hp:hp + 64, fc, :], in0=psum[2][hp:hp + 64, g * 128:(g + 1) * 128], scalar=ccol[hp:hp + 64, 42 + fc:43 + fc],
                        in1=bsbc[hp:hp + 64, fc, :], op0=ALU.mult, op1=ALU.add), reads=[B_ps[2], Bc, B_bc], writes=[Bc])

                xin = [sba(f"xin{i}", [128, NCH, D]) for i in range(2)]
                B_xin = [Buf(f"axin{i}") for i in range(2)]
                xbf = sba("xbf", [128, NCH, D], BF16); B_xbf = Buf("axbf")
                hT2 = [sba(f"hT{i}", [128, 8, NT], BF16) for i in range(2)]; B_hT2 = [Buf(f"ahT{i}") for i in range(2)]
                qk2 = [sba(f"qk_f{i}", [64, 4, NT]) for i in range(2)]; B_qk2 = [Buf(f"qk{i}") for i in range(2)]
                glr2 = [sba(f"glrT{i}", [32, NT], BF16) for i in range(2)]; B_glr2 = [Buf(f"glr{i}") for i in range(2)]
                gu2 = [sba(f"gu{i}", [128, 2, NT]) for i in range(2)]; B_gu2 = [Buf(f"gu{i}") for i in range(2)]
                gtp1 = sba("gtp1", [128, 512]); gtp2 = sba("gtp2", [128, 512]); B_gtp = Buf("gtp")
                lstp = sba("lstp", [128, 2, 6]); lmvp = sba("lmvp", [128, 2]); lscp = sba("lscp", [128, 4])
                B_ltmpp = Buf("altmpp")
                ltmpp = (lstp, lmvp, lscp)
                B7 = [B_ps[7], B_ps[7]]
                xr = sba("xr", [128, 8, 3 + NT]); B_xr = Buf("xr")
                xhalo = sba("xhalo", [128, 8, 3]); B_xhalo = Buf("xhalo")
                ct = [sba(f"ct{i}", [128, NT]) for i in range(2)]; B_ct = [Buf(f"ct{i}") for i in range(2)]
                xc2 = [sba(f"xc{i}", [128, 8, NT], BF16) for i in range(2)]; B_xc2 = [Buf(f"xc{i}") for i in range(2)]
                mixT = sba("mixT", [128, 8, 128], BF16); B_mixT = Buf("mixT")
                vb2 = [sba(f"vb{i}", [128, NCH, 256], BF16) for i in range(2)]; B_vb2 = [Buf(f"vb{i}") for i in range(2)]
                gs2 = [sba(f"gs{i}", [128, NCH, 256]) for i in range(2)]; B_gs2 = [Buf(f"gs{i}") for i in range(2)]
                svg = sba("svg", [128, 256]); B_sv = Buf("sv")
                xhat2 = [sba(f"xhat{i}", [128, NCH, 256], BF16) for i in range(2)]; B_xhat2 = [Buf(f"xhat{i}") for i in range(2)]
                zs2 = [sba(f"zs{i}", [128, NCH, 512]) for i in range(2)]; B_zs2 = [Buf(f"zs{i}") for i in range(2)]
                dta2 = [sba(f"dta{i}", [128, NCH, 16]) for i in range(2)]; B_dta2 = [Buf(f"dta{i}") for i in range(2)]
                la2 = [sba(f"la{i}", [128, NCH, 128]) for i in range(2)]; B_la2 = [Buf(f"la{i}") for i in range(2)]
                dts = sba("dts", [128, 64]); B_dts = Buf("dts")
                e1 = sba("e1", [128, 128]); B_e1 = Buf("e1")
                ecp = sba("ecp", [64, 2, 128]); ecn = sba("ecn", [64, 2, 128]); B_ec = Buf("ec")
                qt = sba("qt", [64, 4, 128], BF16); kt = sba("kt", [64, 2, 128], BF16); B_qkt = Buf("qkt")
                ktm = sba("ktm", [128, 128], BF16); B_ktm = Buf("ktm")
                sm = sba("sm", [128, 4, 128], BF16); B_sm = Buf("sm")
                gst = sba("gst", [64, 2, 128]); gstt = sba("gstt", [64, 2, 128]); gstb = sba("gstb", [64, 2, 128], BF16)
                B_gst = Buf("gst"); B_gstb = Buf("gstb")
                osq = sba("osq", [128, 512]); B_osq = Buf("osq")
                osq2 = sba("osq2", [128, 512]); B_osq2 = Buf("osq2")
                B5a = B5b = B_ps[5]
                B6a = B6c = B_ps[6]
                og = sba("og", [128, 256]); B_og = Buf("og")
                ogl = sba("ogl", [128, 256], BF16); B_ogl = Buf("ogl")
                gsc = sba("gsc", [128, 16]); B_gsc = Buf("gsc")
                sgt = sba("sgt", [128, 128]); B_sgt = Buf("sgt")
                ldec = sba("ldec", [128, 8, 128]); B_ldec = Buf("ldec")
                dm = sba("dm", [128, 8, 128], BF16); B_dm = Buf("dm")
                mm_ = sba("mm_", [128, 8, 128], BF16); B_mm = Buf("mm")
                cbm = sba("cbm", [128, 2, 128], BF16); B_cbm = Buf("cbm")
                xdt = sba("xdt", [128, 512], BF16); xdd = sba("xdd", [128, 512], BF16); B_xdt = Buf("xdt"); B_xdd = Buf("xdd")
                xsd = sba("xsd", [128, 512]); B_xsd = Buf("xsd")
                bmtm = sba("bmtm", [128, 256], BF16); B_bmtm = Buf("bmtm")
                y1 = sba("y1", [128, 512]); B_y1 = Buf("y1")
                yb = sba("yb", [128, 512], BF16); B_yb = Buf("yb")
                sst = sba("sst", [128, 512]); sstb = sba("sstb", [128, 512], BF16); B_sst = Buf("sst"); B_sstb = Buf("sstb")
                lst = sba("lst", [128, 2, 6]); lmv = sba("lmv", [128, 2]); lsc = sba("lsc", [128, 4])
                B_ltmp = Buf("altmp")
                ltmp = (lst, lmv, lsc)
                pbf = [psum[i][:].bitcast(BF16) for i in range(8)]

                for i in range(2):
                    P.op('pool', lambda e, i=i: e.memset(glr2[i][:], 1.0), writes=[B_glr2[i]])

                def gelu(dst, x_sb, n, Bx, Bdst, gt1=None, gt2=None, B_gt=None):
                    P.op('dve', lambda e: e.scalar_tensor_tensor(out=gt1[:, 0:n], in0=x_sb, scalar=0.044715, in1=x_sb,
                                                                 op0=ALU.mult, op1=ALU.mult), reads=[Bx], writes=[B_gt])
                    P.op('dve', lambda e: e.scalar_tensor_tensor(out=gt2[:, 0:n], in0=gt1[:, 0:n], scalar=1.0, in1=x_sb,
                                                                 op0=ALU.add, op1=ALU.mult), reads=[Bx, B_gt], writes=[B_gt])
                    P.op('act', lambda e: e.activation(out=gt1[:, 0:n], in_=gt2[:, 0:n], func=AF.Sigmoid, scale=1.5957691216057308),
                         reads=[B_gt], writes=[B_gt])
                    P.op('pool', lambda e: e.tensor_tensor(out=dst, in0=gt1[:, 0:n], in1=x_sb, op=ALU.mult),
                         reads=[B_gt, Bx], writes=[Bdst])

                def inproj_tm(bank, c, c0, N, o0=0):
                    for kc in range(8):
                        P.op('pe', lambda e, kc=kc: e.matmul(
                            psum[bank][:, o0:o0 + N], lhsT=hT[:, kc, c * 128:(c + 1) * 128], rhs=w_in[:, kc, c0:c0 + N],
                            start=(kc == 0), stop=(kc == 7)), reads=[B_win, B_hT], writes=[B_ps[bank]])

                ntile_a = ntok // NT

                def gen_pro(t):
                    par = t % 2
                    tok0 = t * NT
                    xi, Bxi = xin[par], B_xin[par]
                    h_, Bh_ = hT2[par], B_hT2[par]
                    qk_, Bqk_ = qk2[par], B_qk2[par]
                    gl_, Bgl_ = glr2[par], B_glr2[par]
                    gu_, Bgu_ = gu2[par], B_gu2[par]
                    xc_, Bxc_ = xc2[par], B_xc2[par]
                    P.dma('sp', xi[:], src_d[tok0:tok0 + NT, :].rearrange("(c p) f -> p c f", p=128),
                          reads=[Bsrc_tiles[t]], writes=[Bxi])
                    yield
                    if pre_ln:
                        for c in range(NCH):
                            layernorm(xi[:, c, :], xi[:, c, :], lng0, lnb0, Bxi, Bxi, B_ln0, ltmpp, B_ltmpp)
                            yield
                    if tok0 % seqlen == 0:
                        P.op('pool', lambda e: e.memset(xhalo[:], 0.0), writes=[B_xhalo])
                    P.op('act', lambda e: e.copy(out=xbf[:], in_=xi[:]), reads=[Bxi], writes=[B_xbf])
                    yield
                    hh = 0
                    for c in range(NCH):
                        for half in range(2):
                            pt, Bpt = pbf[7][:, hh * 512:(hh + 1) * 512], B7[hh]
                            for q in range(4):
                                fc = half * 4 + q
                                P.op('pe', lambda e, c=c, fc=fc, q=q, pt=pt: e.transpose(
                                    out=pt[:, q * 128:(q + 1) * 128], in_=xbf[:, c, fc * 128:(fc + 1) * 128], identity=ident_b[:]),
                                    reads=[B_xbf, B_const], writes=[Bpt])
                            P.op('dve', lambda e, c=c, half=half, pt=pt: e.tensor_copy(
                                out=h_[:, half * 4:(half + 1) * 4, c * 128:(c + 1) * 128],
                                in_=pt.rearrange("p (q t) -> p q t", q=4)),
                                reads=[Bpt], writes=[Bh_])
                            hh ^= 1
                            yield
                    def fm(specs):
                        for (c0, M, o0) in specs:
                            for kc in range(8):
                                P.op('pe', lambda e, c0=c0, M=M, o0=o0, kc=kc: e.matmul(
                                    psum[7][0:M, o0:o0 + NT], lhsT=w_in[:, kc, c0:c0 + M], rhs=h_[:, kc, :],
                                    start=(kc == 0), stop=(kc == 7)), reads=[Bw(c0), Bh_], writes=[B_ps[7]])
                    for gi, c0 in enumerate((O_Q, O_K)):
                        fm([(c0, 64, 0), (c0 + 64, 64, NT)])
                        P.op('act', lambda e, gi=gi: e.copy(out=qk_[:, 2 * gi:2 * gi + 2, :],
                                                            in_=psum[7][0:64, :].rearrange("p (a t) -> p a t", a=2)),
                             reads=[B_ps[7]], writes=[Bqk_])
                        yield
                    fm([(O_GLR, 16, 0)])
                    P.op('act', lambda e: e.copy(out=gl_[0:16, :], in_=psum[7][0:16, 0:NT]), reads=[B_ps[7]], writes=[Bgl_])
                    yield
                    fm([(O_SU, 128, 0), (O_SU + 128, 128, NT)])
                    P.op('act', lambda e: e.copy(out=gu_[:], in_=psum[7][:].rearrange("p (a t) -> p a t", a=2)),
                         reads=[B_ps[7]], writes=[Bgu_])
                    yield
                    gelu(gu_[:].rearrange("p a t -> p (a t)"), gu_[:].rearrange("p a t -> p (a t)"), 2 * NT, Bgu_, Bgu_,
                         gt1=gtp1, gt2=gtp2, B_gt=B_gtp)
                    yield
                    for pr in range(4):
                        fm([(O_XBC + (2 * pr) * 128, 128, 0), (O_XBC + (2 * pr + 1) * 128, 128, NT)])
                        P.op('act', lambda e, pr=pr: e.copy(out=xr[:, 2 * pr:2 * pr + 2, 3:3 + NT],
                                                            in_=psum[7][:].rearrange("p (a t) -> p a t", a=2)),
                             reads=[B_ps[7]], writes=[B_xr])
                        yield
                    P.op('pool', lambda e: e.tensor_copy(out=xr[:, :, 0:3], in_=xhalo[:]), reads=[B_xhalo], writes=[B_xr])
                    P.op('pool', lambda e: e.tensor_copy(out=xhalo[:], in_=xr[:, :, NT:NT + 3]), reads=[B_xr], writes=[B_xhalo])
                    yield
                    for fc in range(8):
                        c_, Bc_ = ct[fc % 2], B_ct[fc % 2]
                        P.op('act', lambda e, fc=fc, c_=c_: e.activation(out=c_[:], in_=xr[:, fc, 3:3 + NT], func=AF.Identity,
                                                                         bias=CB_(fc), scale=CW(3, fc)), reads=[B_xr, Bc], writes=[Bc_])
                        for k in (2, 1, 0):
                            P.op('dve', lambda e, fc=fc, k=k, c_=c_: e.scalar_tensor_tensor(
                                out=c_[:], in0=xr[:, fc, k:k + NT], scalar=CW(k, fc), in1=c_[:], op0=ALU.mult, op1=ALU.add),
                                reads=[B_xr, Bc, Bc_], writes=[Bc_])
                        P.op('act', lambda e, fc=fc, c_=c_: e.activation(out=xc_[:, fc, :], in_=c_[:], func=AF.Silu),
                             reads=[Bc_], writes=[Bxc_])
                        yield
                    vb_, gs_, zs_, xh_, dta_, la_ = vb2[par], gs2[par], zs2[par], xhat2[par], dta2[par], la2[par]
                    Bvb_, Bgs_, Bzs_, Bxh_, Bdta_, Bla_ = B_vb2[par], B_gs2[par], B_zs2[par], B_xhat2[par], B_dta2[par], B_la2[par]

                    def tm(c, c0, N, o0=0):
                        for kc in range(8):
                            P.op('pe', lambda e, kc=kc: e.matmul(
                                psum[7][:, o0:o0 + N], lhsT=h_[:, kc, c * 128:(c + 1) * 128], rhs=w_in[:, kc, c0:c0 + N],
                                start=(kc == 0), stop=(kc == 7)), reads=[Bw(c0), Bh_], writes=[B_ps[7]])
                    for c in range(NCH):
                        cs_ = c * 128
                        tm(c, O_V, 512)
                        P.op('act', lambda e, c=c: e.copy(out=vb_[:, c, :], in_=psum[7][:, 0:256]), reads=[B_ps[7]], writes=[Bvb_])
                        P.op('act', lambda e, c=c: e.activation(out=gs_[:, c, :], in_=psum[7][:, 256:512], func=AF.Silu),
                             reads=[B_ps[7]], writes=[Bgs_])
                        yield
                        P.op('pool', lambda e, c=c: e.tensor_tensor(out=gs_[:, c, :].rearrange("p (h v) -> p h v", h=4),
                                                                    in0=gs_[:, c, :].rearrange("p (h v) -> p h v", h=4),
                                                                    in1=gng_bc[:].unsqueeze(1).broadcast_to([128, 4, 64]), op=ALU.mult),
                             reads=[Bgs_, B_bc], writes=[Bgs_])
                        tm(c, O_Z, 512)
                        P.op('act', lambda e, c=c: e.activation(out=zs_[:, c, :], in_=psum[7][:], func=AF.Silu), reads=[B_ps[7]], writes=[Bzs_])
                        yield
                        tm(c, O_SV, 256)
                        tm(c, O_DT, 8, o0=256)
                        P.op('act', lambda e: e.copy(out=svg[:], in_=psum[7][:, 0:256]), reads=[B_ps[7]], writes=[B_sv])
                        P.op('dve', lambda e, c=c: e.tensor_tensor(out=dta_[:, c, 0:8], in0=psum[7][:, 256:264], in1=dtb_bc[:], op=ALU.add),
                             reads=[B_ps[7], B_bc], writes=[Bdta_])
                        yield
                        P.op('act', lambda e, c=c: e.activation(out=dta_[:, c, 0:8], in_=dta_[:, c, 0:8], func=AF.Exp), reads=[Bdta_], writes=[Bdta_])
                        P.op('act', lambda e, c=c: e.activation(out=dta_[:, c, 0:8], in_=dta_[:, c, 0:8], func=AF.Ln, bias=one_t[:], scale=1.0),
                             reads=[Bdta_, Bc], writes=[Bdta_])
                        P.op('dve', lambda e, c=c: e.tensor_tensor(out=dta_[:, c, 8:16], in0=dta_[:, c, 0:8], in1=acont_bc[:], op=ALU.mult),
                             reads=[Bdta_, B_bc], writes=[Bdta_])
                        yield
                        P.op('pe', lambda e, cs_=cs_: e.matmul(psum[7][:, 0:128], lhsT=gl_[0:17, cs_:cs_ + 128], rhs=wg_b[:, :], start=True, stop=True),
                             reads=[Bgl_, Bc], writes=[B_ps[7]])
                        P.op('act', lambda e: e.activation(out=e1[:], in_=psum[7][:, 0:128], func=AF.Exp, scale=-1.0),
                             reads=[B_ps[7]], writes=[B_e1])
                        P.op('act', lambda e, c=c: e.activation(out=la_[:, c, :], in_=e1[:], func=AF.Ln, bias=one_t[:], scale=1.0),
                             reads=[B_e1, Bc], writes=[Bla_])
                        yield
                        gelu(svg[:], svg[:], 256, B_sv, B_sv, gt1=gtp1, gt2=gtp2, B_gt=B_gtp)
                        yield
                        P.op('dve', lambda e: e.bn_stats(out=lstp[:, 0, :], in_=svg[:]), reads=[B_sv], writes=[B_ltmpp])
                        P.op('dve', lambda e: e.bn_aggr(out=lmvp[:], in_=lstp[:, 0, :]), reads=[B_ltmpp], writes=[B_ltmpp])
                        P.op('act', lambda e: e.activation(out=lscp[:, 0:1], in_=lmvp[:, 1:2], func=AF.Ln, bias=eps_t[:], scale=1.0),
                             reads=[B_ltmpp, B_const], writes=[B_ltmpp])
                        P.op('act', lambda e: e.activation(out=lscp[:, 1:2], in_=lscp[:, 0:1], func=AF.Exp, scale=-0.5),
                             reads=[B_ltmpp], writes=[B_ltmpp])
                        yield
                        P.op('dve', lambda e: e.tensor_scalar(out=lscp[:, 2:3], in0=lmvp[:, 0:1], scalar1=lscp[:, 1:2], scalar2=-1.0,
                                                              op0=ALU.mult, op1=ALU.mult), reads=[B_ltmpp], writes=[B_ltmpp])
                        P.op('act', lambda e, c=c: e.activation(out=xh_[:, c, :], in_=svg[:], func=AF.Identity, bias=lscp[:, 2:3], scale=lscp[:, 1:2]),
                             reads=[B_sv, B_ltmpp], writes=[Bxh_])
                        yield

                def gen_ln1(xi_, Bxi_, c_, t_, last):
                    yield
                    layernorm(xi_[:, c_, :], xi_[:, c_, :], lng, lnb, Bxi_, Bxi_, B_ln, ltmp, B_ltmp)
                    yield
                    if last:
                        tk = t_ * NT
                        P.dma('sp', dst_d[tk:tk + NT, :].rearrange("(c p) f -> p c f", p=128), xi_[:],
                              reads=[Bxi_], writes=[Bdst_tiles[t_]])

                tails = []
                for _ in gen_pro(0):
                    pass
                for t in range(ntile_a):
                    tok0 = t * NT
                    par = t % 2
                    xi, Bxi = xin[par], B_xin[par]
                    hT, B_hT = hT2[par], B_hT2[par]
                    qk_f, B_qk = qk2[par], B_qk2[par]
                    glrT, B_glr = glr2[par], B_glr2[par]
                    gu, B_gu = gu2[par], B_gu2[par]
                    xc, B_xc = xc2[par], B_xc2[par]
                    nxt = gen_pro(t + 1) if t + 1 < ntile_a else None
                    if tok0 % seqlen == 0:
                        P.op('pool', lambda e: e.memset(gst[:], 0.0), writes=[B_gst])
                        P.op('pool', lambda e: e.memset(gstb[:], 0.0), writes=[B_gstb])
                        P.op('pool', lambda e: e.memset(sst[:], 0.0), writes=[B_sst])
                        P.op('pool', lambda e: e.memset(sstb[:], 0.0), writes=[B_sstb])

                    for c in range(NCH):
                        cs = c * 128
                        vb, B_vb = vb2[par][:, c, :], B_vb2[par]
                        gs, B_gs = gs2[par][:, c, :], B_gs2[par]
                        zs, B_zs = zs2[par][:, c, :], B_zs2[par]
                        xhat, B_xhat = xhat2[par][:, c, :], B_xhat2[par]
                        dta, B_dta = dta2[par][:, c, :], B_dta2[par]
                        la, B_la = la2[par][:, c, :], B_la2[par]
                        def gen_sgu():
                            if 'sgu' not in mixers:
                                P.op('pool', lambda e: e.memset(mixT[:, 2:4, :], 0.0), writes=[B_mixT])
                                return
                            yield
                            for g in range(4):
                                hp, fc = (g % 2) * 64, g // 2
                                P.op('pe', lambda e, g=g, hp=hp, fc=fc: e.matmul(
                                    psum[5][hp:hp + 64, 256 + fc * 128:256 + (fc + 1) * 128], lhsT=xhat[:, g * 64:(g + 1) * 64], rhs=wmT[:, g, :],
                                    start=True, stop=True), reads=[B_xhat, Bc], writes=[B5b])
                            yield
                            for fc in range(2):
                                P.op('dve', lambda e, fc=fc: e.scalar_tensor_tensor(
                                    out=sgt[:], in0=psum[5][:, 256 + fc * 128:256 + (fc + 1) * 128], scalar=SGG(fc), in1=csg[:, fc, :],
                                    op0=ALU.mult, op1=ALU.add), reads=[B5b, Bc], writes=[B_sgt])
                                P.op('dve', lambda e, fc=fc: e.tensor_tensor(out=mixT[:, 2 + fc, :], in0=sgt[:], in1=gu[:, fc, cs:cs + 128],
                                                                              op=ALU.mult), reads=[B_sgt, B_gu], writes=[B_mixT])
                        def gen_gla():
                            if 'gla' not in mixers:
                                P.op('pool', lambda e: e.memset(mixT[:, 0:2, :], 0.0), writes=[B_mixT])
                                return
                            yield
                            for pr in range(2):
                                P.op('pe', lambda e, pr=pr: e.matmul(psum[3][0:64, 128 + pr * 128:256 + pr * 128], lhsT=la[:, pr * 64:(pr + 1) * 64],
                                                                     rhs=tri_f[:], start=True, stop=True), reads=[B_la, Bc], writes=[B_ps[3]])
                            yield
                            cum3 = psum[3][0:64, 128:384].rearrange("p (a t) -> p a t", a=2)
                            yield
                            P.op('act', lambda e: e.activation(out=ecp[:], in_=cum3, func=AF.Exp, scale=-1.0 / 16.0), reads=[B_ps[3]], writes=[B_ec])
                            yield
                            P.op('act', lambda e: e.activation(out=ecn[:], in_=cum3, func=AF.Exp, scale=1.0 / 16.0), reads=[B_ps[3]], writes=[B_ec])
                            yield
                            for h in range(4):
                                P.op('dve', lambda e, h=h: e.scalar_tensor_tensor(out=qt[:, h, :], in0=qk_f[:, h // 2, cs:cs + 128], scalar=hm[:, h:h + 1],
                                                                                  in1=ecp[:, h // 2, :], op0=ALU.mult, op1=ALU.mult),
                                     reads=[B_qk, B_ec, Bc], writes=[B_qkt])
                            yield
                            P.op('dve', lambda e: e.tensor_tensor(out=kt[:], in0=qk_f[:, 2:4, cs:cs + 128], in1=ecn[:], op=ALU.mult),
                                 reads=[B_qk, B_ec], writes=[B_qkt])
                            yield
                            for pr in range(2):
                                P.op('pe', lambda e, pr=pr: e.transpose(out=pbf[6][:, 768 + pr * 64:768 + (pr + 1) * 64], in_=kt[:, pr, :],
                                                                        identity=ident_b[0:64, 0:64]), reads=[B_qkt, B_const], writes=[B6c])
                            yield
                            P.op('act', lambda e: e.copy(out=ktm[:], in_=pbf[6][:, 768:896]), reads=[B6c], writes=[B_ktm])
                            yield
                            for h in range(4):
                                pr, hb = h // 2, (h % 2) * 32
                                P.op('pe', lambda e, h=h, pr=pr, hb=hb: e.matmul(psum[4][:, h * 128:(h + 1) * 128], lhsT=kt[:, pr, :],
                                                                                 rhs=qt[:, h, :], start=True, stop=True),
                                     reads=[B_qkt], writes=[B_ps[4]])
                            yield
                            P.op('dve', lambda e: e.tensor_tensor(out=sm[:], in0=psum[4][:].rearrange("p (h t) -> p h t", h=4),
                                                                  in1=maskge_b[:].unsqueeze(1).broadcast_to([128, 4, 128]), op=ALU.mult),
                                 reads=[B_ps[4], Bc], writes=[B_sm])
                            yield
                            for h in range(4):
                                pr, hb = h // 2, (h % 2) * 32
                                P.op('pe', lambda e, h=h: e.matmul(psum[5][:, h * 64:(h + 1) * 64], lhsT=sm[:, h, :], rhs=vb[:, h * 64:(h + 1) * 64],
                                                                   start=True, stop=False), reads=[B_sm, B_vb], writes=[B5a])
                                P.op('pe', lambda e, h=h, pr=pr, hb=hb: e.matmul(psum[5][:, h * 64:(h + 1) * 64], lhsT=qt[:, h, :],
                                                                                 rhs=gstb[:, pr, (h % 2) * 64:(h % 2 + 1) * 64], start=False, stop=True),
                                     reads=[B_qkt, B_gstb], writes=[B5a])
                            yield
                            for pr in range(2):
                                P.op('pe', lambda e, pr=pr: e.matmul(psum[3][0:64, 128 + pr * 128:256 + pr * 128],
                                                                     lhsT=ktm[:, pr * 64:(pr + 1) * 64], rhs=vb[:, pr * 128:(pr + 1) * 128],
                                                                     start=True, stop=True), reads=[B_ktm, B_vb], writes=[B_ps[3]])
                            yield
                            P.op('dve', lambda e: e.tensor_tensor(out=gstt[:], in0=psum[3][0:64, 128:384].rearrange("p (a v) -> p a v", a=2),
                                                                  in1=gst[:], op=ALU.add), reads=[B_ps[3], B_gst], writes=[B_gst])
                            yield
                            for pr in range(2):
                                P.op('act', lambda e, pr=pr: e.activation(out=gst[:, pr, :], in_=gstt[:, pr, :], func=AF.Identity,
                                                                          scale=ecp[:, pr, 127:128]), reads=[B_gst, B_ec], writes=[B_gst])
                            yield
                            P.op('dve', lambda e: e.tensor_copy(out=gstb[:], in_=gst[:]), reads=[B_gst], writes=[B_gstb])
                            yield
                            P.op('act', lambda e: e.activation(out=osq[:, 0:256], in_=psum[5][:, 0:256], func=AF.Square), reads=[B5a], writes=[B_osq])
                            yield
                            P.op('dve', lambda e: e.tensor_reduce(out=gsc[:, 0:4], in_=osq[:, 0:256].rearrange("p (h v) -> p h v", h=4),
                                                                  axis=AX.X, op=ALU.add), reads=[B_osq], writes=[B_gsc])
                            yield
                            P.op('act', lambda e: e.activation(out=gsc[:, 4:8], in_=gsc[:, 0:4], func=AF.Ln, bias=eps_t[:], scale=1.0 / 64.0),
                                 reads=[B_gsc, B_const], writes=[B_gsc])
                            yield
                            P.op('act', lambda e: e.activation(out=gsc[:, 8:12], in_=gsc[:, 4:8], func=AF.Exp, scale=-0.5), reads=[B_gsc], writes=[B_gsc])
                            yield
                            P.op('dve', lambda e: e.tensor_tensor(out=og[:].rearrange("p (h v) -> p h v", h=4),
                                                                  in0=psum[5][:, 0:256].rearrange("p (h v) -> p h v", h=4),
                                                                  in1=gsc[:, 8:12].unsqueeze(2).broadcast_to([128, 4, 64]), op=ALU.mult),
                                 reads=[B5a, B_gsc], writes=[B_og])
                            yield
                            P.op('dve', lambda e: e.tensor_tensor(out=ogl[:], in0=og[:], in1=gs[:], op=ALU.mult), reads=[B_og, B_gs], writes=[B_ogl], cost=0.35)
                            yield
                            for q in range(2):
                                P.op('pe', lambda e, q=q: e.transpose(out=pbf[3][:, 768 + q * 128:768 + (q + 1) * 128], in_=ogl[:, q * 128:(q + 1) * 128],
                                                                      identity=ident_b[:]), reads=[B_ogl, B_const], writes=[B_ps[3]])
                            yield
                            P.op('act', lambda e: e.copy(out=mixT[:, 0:2, :], in_=pbf[3][:, 768:1024].rearrange("p (q t) -> p q t", q=2)),
                                 reads=[B_ps[3]], writes=[B_mixT])
                        def gen_ssd():
                            if 'ssd' not in mixers:
                                P.op('pool', lambda e: e.memset(mixT[:, 4:8, :], 0.0), writes=[B_mixT])
                                return
                            yield
                            for q in range(4):
                                P.op('pe', lambda e, q=q: e.transpose(out=pbf[6][:, q * 128:(q + 1) * 128], in_=xc[:, q, cs:cs + 128], identity=ident_b[:]),
                                     reads=[B_xc, B_const], writes=[B6a])
                            yield
                            for q in range(2):
                                P.op('pe', lambda e, q=q: e.transpose(out=pbf[6][:, 512 + q * 128:512 + (q + 1) * 128], in_=xc[:, 4 + q, cs:cs + 128],
                                                                      identity=ident_b[:]), reads=[B_xc, B_const], writes=[B6a])
                            yield
                            P.op('pe', lambda e: e.matmul(psum[2][:, 264:272], lhsT=tri_f[:], rhs=dta[:, 8:16], start=True, stop=True),
                                 reads=[B_dts, B_dta, Bc], writes=[B_ps[2]])
                            yield
                            P.op('pe', lambda e: e.matmul(psum[2][:, 272:280], lhsT=ones_f[:], rhs=dta[:, 8:16], start=True, stop=True),
                                 reads=[B_dts, B_dta, Bc], writes=[B_ps[2]])
                            yield
                            P.op('act', lambda e: e.copy(out=dts[:, 16:32], in_=psum[2][:, 264:280]), reads=[B_ps[2]], writes=[B_dts])
                            yield
                            P.op('act', lambda e: e.activation(out=dts[:, 32:48], in_=dts[:, 16:32], func=AF.Exp), reads=[B_dts, B_dta], writes=[B_dts])
                            yield
                            P.op('dve', lambda e: e.tensor_tensor(out=dts[:, 48:56], in0=dts[:, 24:32], in1=dts[:, 16:24], op=ALU.subtract),
                                 reads=[B_dts, B_dta], writes=[B_dts])
                            yield
                            P.op('act', lambda e: e.activation(out=dts[:, 48:56], in_=dts[:, 48:56], func=AF.Exp), reads=[B_dts, B_dta], writes=[B_dts])
                            yield
                            xsT3 = pbf[6][:, 0:512].rearrange("p (h v) -> p h v", h=8)
                            yield
                            P.op('dve', lambda e: e.tensor_tensor(out=xdt[:].rearrange("p (h v) -> p h v", h=8), in0=xsT3,
                                                                  in1=dta[:, 0:8].unsqueeze(2).broadcast_to([128, 8, 64]), op=ALU.mult),
                                 reads=[B6a, B_dts, B_dta], writes=[B_xdt])
                            yield
                            P.op('dve', lambda e: e.tensor_tensor(out=xsd[:].rearrange("p (h v) -> p h v", h=8), in0=xsT3,
                                                                  in1=dsk_bc[:].unsqueeze(2).broadcast_to([128, 8, 64]), op=ALU.mult),
                                 reads=[B6a, B_bc], writes=[B_xsd])
                            yield
                            P.op('dve', lambda e: e.tensor_tensor(out=xdd[:].rearrange("p (h v) -> p h v", h=8),
                                                                   in0=xdt[:].rearrange("p (h v) -> p h v", h=8),
                                                                   in1=dts[:, 48:56].unsqueeze(2).broadcast_to([128, 8, 64]), op=ALU.mult),
                                 reads=[B_xdt, B_dts, B_dta], writes=[B_xdd])
                            yield
                            P.op('act', lambda e: e.copy(out=bmtm[:], in_=pbf[6][:, 512:768]), reads=[B6a], writes=[B_bmtm])
                            yield
                            P.op('dve', lambda e: e.tensor_tensor(out=ldec[:], in0=maskgt_f[:].unsqueeze(1).broadcast_to([128, 8, 128]),
                                                                  in1=dta[:, 8:16].unsqueeze(2).broadcast_to([128, 8, 128]), op=ALU.mult),
                                 reads=[Bc, B_dts, B_dta], writes=[B_ldec])
                            yield
                            for h in range(8):
                                bk = h // 4
                                P.op('pe', lambda e, h=h, bk=bk: e.matmul(psum[bk][:, (h % 4) * 128:(h % 4 + 1) * 128], lhsT=ldec[:, h, :], rhs=tri_f[:],
                                                                          start=True, stop=True), reads=[B_ldec, Bc], writes=[B_ps[bk]])
                            yield
                            for bk in range(2):
                                P.op('act', lambda e, bk=bk: e.activation(out=dm[:, bk * 4:(bk + 1) * 4, :],
                                                                          in_=psum[bk][:].rearrange("p (h t) -> p h t", h=4), func=AF.Exp),
                                     reads=[B_ps[bk]], writes=[B_dm])
                            yield
                            for g in range(2):
                                P.op('pe', lambda e, g=g: e.matmul(psum[2][:, g * 128:(g + 1) * 128], lhsT=xc[:, 4 + g, cs:cs + 128],
                                                                   rhs=xc[:, 6 + g, cs:cs + 128], start=True, stop=True), reads=[B_xc], writes=[B_ps[2]])
                            yield
                            P.op('dve', lambda e: e.tensor_tensor(out=cbm[:], in0=psum[2][:, 0:256].rearrange("p (g t) -> p g t", g=2),
                                                                  in1=maskge_b[:].unsqueeze(1).broadcast_to([128, 2, 128]), op=ALU.mult),
                                 reads=[B_ps[2], Bc], writes=[B_cbm])
                            yield
                            for g in range(2):
                                P.op('dve', lambda e, g=g: e.tensor_tensor(out=mm_[:, g * 4:(g + 1) * 4, :], in0=dm[:, g * 4:(g + 1) * 4, :],
                                                                            in1=cbm[:, g, :].unsqueeze(1).broadcast_to([128, 4, 128]), op=ALU.mult),
                                     reads=[B_dm, B_cbm], writes=[B_mm])
                            yield
                            for h in range(8):
                                P.op('pe', lambda e, h=h: e.matmul(psum[0][:, h * 64:(h + 1) * 64], lhsT=mm_[:, h, :], rhs=xdt[:, h * 64:(h + 1) * 64],
                                                                   start=True, stop=True), reads=[B_mm, B_xdt], writes=[B_ps[0]])
                            yield
                            for g in range(2):
                                P.op('pe', lambda e, g=g: e.matmul(psum[1][:, g * 256:(g + 1) * 256], lhsT=xc[:, 6 + g, cs:cs + 128],
                                                                   rhs=sstb[:, g * 256:(g + 1) * 256], start=True, stop=True),
                                     reads=[B_xc, B_sstb], writes=[B_ps[1]])
                            yield
                            P.op('dve', lambda e: e.tensor_tensor(out=y1[:].rearrange("p (h v) -> p h v", h=8),
                                                                  in0=psum[1][:].rearrange("p (h v) -> p h v", h=8),
                                                                  in1=dts[:, 32:40].unsqueeze(2).broadcast_to([128, 8, 64]), op=ALU.mult),
                                 reads=[B_ps[1], B_dts, B_dta], writes=[B_y1])
                            yield
                            P.op('dve', lambda e: e.tensor_tensor(out=y1[:], in0=y1[:], in1=psum[0][:], op=ALU.add), reads=[B_y1, B_ps[0]], writes=[B_y1])
                            yield
                            P.op('dve', lambda e: e.tensor_tensor(out=y1[:], in0=y1[:], in1=xsd[:], op=ALU.add), reads=[B_y1, B_xsd], writes=[B_y1])
                            yield
                            P.op('dve', lambda e: e.tensor_tensor(out=y1[:], in0=y1[:], in1=zs[:], op=ALU.mult), reads=[B_y1, B_zs], writes=[B_y1])
                            yield
                            for g in range(2):
                                P.op('pe', lambda e, g=g: e.matmul(psum[2][:, g * 256:(g + 1) * 256], lhsT=bmtm[:, g * 128:(g + 1) * 128],
                                                                   rhs=xdd[:, g * 256:(g + 1) * 256], start=True, stop=True),
                                     reads=[B_bmtm, B_xdd], writes=[B_ps[2]])
                            yield
                            P.op('pool', lambda e: e.tensor_tensor(out=sst[:].rearrange("p (h v) -> p h v", h=8),
                                                                   in0=sst[:].rearrange("p (h v) -> p h v", h=8),
                                                                   in1=dts[:, 40:48].unsqueeze(2).broadcast_to([128, 8, 64]), op=ALU.mult),
                                 reads=[B_sst, B_dts, B_dta], writes=[B_sst])
                            yield
                            P.op('dve', lambda e: e.tensor_tensor(out=sst[:], in0=sst[:], in1=psum[2][:], op=ALU.add), reads=[B_sst, B_ps[2]], writes=[B_sst])
                            yield
                            P.op('act', lambda e: e.copy(out=sstb[:], in_=sst[:]), reads=[B_sst], writes=[B_sstb])
                            yield
                            P.op('act', lambda e: e.activation(out=osq2[:], in_=y1[:], func=AF.Square), reads=[B_y1], writes=[B_osq2])
                            yield
                            P.op('dve', lambda e: e.tensor_reduce(out=gsc[:, 12:14], in_=osq2[:].rearrange("p (g v) -> p g v", g=2), axis=AX.X, op=ALU.add),
                                 reads=[B_osq2], writes=[B_gsc])
                            yield
                            P.op('act', lambda e: e.activation(out=gsc[:, 14:16], in_=gsc[:, 12:14], func=AF.Ln, bias=eps_t[:], scale=1.0 / 256.0),
                                 reads=[B_gsc, B_const], writes=[B_gsc])
                            yield
                            P.op('act', lambda e: e.activation(out=gsc[:, 12:14], in_=gsc[:, 14:16], func=AF.Exp, scale=-0.5), reads=[B_gsc], writes=[B_gsc])
                            yield
                            P.op('dve', lambda e: e.tensor_tensor(out=y1[:].rearrange("p (g v) -> p g v", g=2),
                                                                  in0=y1[:].rearrange("p (g v) -> p g v", g=2),
                                                                  in1=gsc[:, 12:14].unsqueeze(2).broadcast_to([128, 2, 256]), op=ALU.mult),
                                 reads=[B_y1, B_gsc], writes=[B_y1])
                            yield
                            P.op('pool', lambda e: e.tensor_tensor(out=yb[:], in0=y1[:], in1=sng_bc[:], op=ALU.mult), reads=[B_y1, B_bc], writes=[B_yb])
                            yield
                            for q in range(4):
                                P.op('pe', lambda e, q=q: e.transpose(out=pbf[6][:, q * 128:(q + 1) * 128], in_=yb[:, q * 128:(q + 1) * 128],
                                                                      identity=ident_b[:]), reads=[B_yb, B_const], writes=[B6a])
                            yield
                            P.op('act', lambda e: e.copy(out=mixT[:, 4:8, :], in_=pbf[6][:, 0:512].rearrange("p (q t) -> p q t", q=4)),
                                 reads=[B6a], writes=[B_mixT])
                        g_ssd = gen_ssd()
                        pend = list(tails)
                        gens = [g_ssd, gen_gla(), g_ssd, gen_sgu()] + tails
                        tails = []
                        while gens:
                            for g_ in list(gens):
                                if g_ not in gens:
                                    continue
                                try:
                                    next(g_)
                                except StopIteration:
                                    while g_ in gens:
                                        gens.remove(g_)
                            for _rep in range(2):
                                if nxt is not None and not any(p_ in gens for p_ in pend):
                                    try:
                                        next(nxt)
                                    except StopIteration:
                                        nxt = None
                        for hf in range(2):
                            for fc in range(8):
                                P.op('pe', lambda e, hf=hf, fc=fc: e.matmul(psum[hf][:], lhsT=mixT[:, fc, :], rhs=w_out[:, fc, hf * 512:(hf + 1) * 512],
                                                                            start=(fc == 0), stop=(fc == 7)), reads=[B_mixT, B_wout], writes=[B_ps[hf]])
                        for hf in range(2):
                            P.op('dve', lambda e, hf=hf: e.scalar_tensor_tensor(
                                out=xi[:, c, hf * 512:(hf + 1) * 512], in0=xi[:, c, hf * 512:(hf + 1) * 512], scalar=ALPHA,
                                in1=psum[hf][:], op0=ALU.mult, op1=ALU.add), reads=[Bxi, B_ps[hf]], writes=[Bxi])
                        tails = [gen_ln1(xi, Bxi, c, t, c == NCH - 1)]
                    if nxt is not None:
                        for _ in nxt:
                            pass
                for g_ in tails:
                    for _ in g_:
                        pass

        B_x = [Buf(f"x{i}") for i in range(ntile)]
        cur_d, cur_B = x_d, B_x
        first = True
        for pi, (kind, layer) in enumerate(phases):
            last = (pi == len(phases) - 1)
            if kind == 'B':
                dst_d, dst_B = (y_d, B_y) if last else (hB_d, B_hB)
                phase_b(layer, cur_d, cur_B, dst_d, dst_B, pre_ln=first)
            else:
                dst_d, dst_B = (y_d, B_y) if last else (hA_d, B_hA)
                phase_a(layer, cur_d, cur_B, dst_d, dst_B, pre_ln=first)
            P.barrier()
            cur_d, cur_B = dst_d, dst_B
            first = False
        P.wait_all('sp', B_y)
        print(f"[build] instructions={P.nins} waits={P.nwaits} sems={len(P.sems)}")
    return nc


_W_NAMES = ["ln_in_g", "ln_in_b", "w_in", "gla_w_gate", "gla_b_gate", "gla_norm_g", "sgu_norm_g", "sgu_norm_b",
            "sgu_w", "sgu_b", "ssd_conv_w", "ssd_conv_b", "ssd_dt_bias", "ssd_a_log", "ssd_d", "ssd_norm_g",
            "w_out", "ln1_g", "ln1_b", "ffn_w_up", "ffn_conv_w", "ffn_conv_b", "ffn_w_down", "ln2_g", "ln2_b"]


def run(inputs, ntok, seqlen, phases=None, ncores=NCORES, **kw):
    nc = build_program(ntok=ntok, seqlen=seqlen, phases=phases, **kw)
    x = np.ascontiguousarray(np.asarray(inputs["x"], dtype=np.float32)).reshape(-1, D)
    assert x.shape[0] == ntok * ncores
    wmap = {k: np.ascontiguousarray(np.asarray(inputs[k], dtype=np.float32)) for k in _W_NAMES}
    in_maps = []
    for c in range(ncores):
        m = dict(wmap)
        m["x"] = x[c * ntok:(c + 1) * ntok]
        in_maps.append(m)
    res = run_bass_kernel_spmd(nc, in_maps, core_ids=list(range(ncores)))
    return np.concatenate([r["y"] for r in res.results], axis=0)


def kernel(**inputs):
    x = inputs["x"]
    B, S, _ = x.shape
    y = run(inputs, ntok=(B * S) // NCORES, seqlen=S)
    return y.reshape(B, S, D).astype(np.float32)
```

```python
import types
import numpy as np
from contextlib import ExitStack
import concourse.bass as bass
import concourse.mybir as mybir
from concourse.bass_utils import run_bass_kernel_spmd

F32 = mybir.dt.float32
BF16 = mybir.dt.bfloat16
AF = mybir.ActivationFunctionType
ALU = mybir.AluOpType
AX = mybir.AxisListType

D = 1024
DEPTH = 2
DIN = 2840
DFF = 2816
NJ = DFF // 128
ALPHA = float((2 * DEPTH) ** 0.25)
EPS = 1e-5
NCORES = 8

O_Q, O_K, O_V, O_G, O_GLR, O_SU, O_SV, O_Z, O_XBC, O_DT = 0, 128, 256, 512, 768, 784, 1040, 1296, 1808, 2832


class Buf:
    def __init__(self, name, psum=False):
        self.name = name
        self.psum = psum
        self.writer = None
        self.readers = {}
        self.aliases = []
        self.dcount = 0


class Prog:
    def __init__(self, nc, stack):
        self.nc = nc
        self.stack = stack
        self.eng = {'pe': nc.tensor, 'act': nc.scalar, 'dve': nc.vector, 'pool': nc.gpsimd, 'sp': nc.sync}
        self.sems = {}
        self.cnt = {}
        self.seen = {e: {} for e in self.eng}
        for e in ('pe', 'act', 'dve', 'pool'):
            self.sems[e] = stack.enter_context(nc.semaphore("s_" + e))
            self.cnt[e] = 0
        self.nwaits = 0
        self.nins = 0
        self.dtot = {}
        self.nodes = []

    def _sem(self, key):
        if key not in self.sems:
            self.sems[key] = self.stack.enter_context(self.nc.semaphore("d_" + key))
        return self.sems[key]

    def _deps(self, eng, reads, writes):
        raw = {}
        oth = {}

        def add(d, ev):
            if ev is None:
                return
            k, v = ev
            if d.get(k, 0) < v:
                d[k] = v
        for b in reads:
            add(raw, b.writer)
            for a in b.aliases:
                add(raw, a.writer)
            if b.psum:
                for k, v in b.readers.items():
                    add(oth, (k, v))
        for b in writes:
            for bb in [b] + b.aliases:
                add(oth, bb.writer)
                for k, v in bb.readers.items():
                    add(oth, (k, v))
        need = {}
        for k, v in raw.items():
            if k == eng:
                if eng == 'pe':
                    continue
                if eng != 'pool' and self.cnt[eng] - v >= 8:
                    continue
            if need.get(k, 0) < v:
                need[k] = v
        for k, v in oth.items():
            if k == eng:
                continue
            if need.get(k, 0) < v:
                need[k] = v
        e = self.eng[eng]
        for k, v in need.items():
            if self.seen[eng].get(k, 0) < v:
                e.wait_ge(self._sem(k), v)
                self.seen[eng][k] = v
                self.nwaits += 1

    def _emit_op(self, eng, fn, reads=(), writes=()):
        self._deps(eng, reads, writes)
        ins = fn(self.eng[eng])
        self.cnt[eng] += 1
        ins.then_inc(self.sems[eng], 1)
        ev = (eng, self.cnt[eng])
        for b in reads:
            if b.readers.get(eng, 0) < ev[1]:
                b.readers[eng] = ev[1]
        for b in writes:
            b.writer = ev
            b.readers = {}
        self.nins += 1
        return ins

    def _emit_dma(self, eng, out, in_, reads, writes, **kw):
        self._deps(eng, reads, writes)
        dst = writes[0]
        key = "dma_" + dst.name
        sem = self._sem(key)
        ins = self.eng[eng].dma_start(out=out, in_=in_, **kw)
        cnt = self.dtot.get(key, 0) + 16
        self.dtot[key] = cnt
        ins.then_inc(sem, 16)
        ev = (key, cnt)
        for b in reads:
            if b.readers.get(key, 0) < ev[1]:
                b.readers[key] = ev[1]
        for b in writes:
            b.writer = ev
            b.readers = {}
        return ins

    COST = {'pe': 0.21, 'act': 0.5, 'dve': 0.5, 'pool': 1.05, 'sp': 0.1}
    WINDOW = 600
    SLACK = 0.4

    @staticmethod
    def _freeze(fn):
        if not fn.__closure__:
            return fn
        cells = []
        for c in fn.__closure__:
            try:
                cells.append(types.CellType(c.cell_contents))
            except ValueError:
                cells.append(c)
        return types.FunctionType(fn.__code__, fn.__globals__, fn.__name__, fn.__defaults__, tuple(cells))

    def op(self, eng, fn, reads=(), writes=(), cost=None):
        tbl = None
        if eng == 'act':
            names = fn.__code__.co_names
            if 'Silu' in names:
                tbl = 'silu'
            elif 'Sigmoid' in names:
                tbl = 'sigmoid'
            elif 'Exp' in names or 'Ln' in names:
                tbl = 'exp'
        self.nodes.append(('op', eng, self._freeze(fn), list(reads), list(writes), tbl, cost))

    def dma(self, eng, out, in_, reads, writes, **kw):
        self.nodes.append(('dma', eng, (out, in_), list(reads), list(writes), kw, None))

    def flush(self):
        nodes = self.nodes
        self.nodes = []
        n = len(nodes)
        if n == 0:
            return
        lastw = {}
        readers = {}
        deps = [None] * n
        succ = [[] for _ in range(n)]
        for i, (kind, eng, fn, reads, writes, kw, cost) in enumerate(nodes):
            d = set()
            for b in reads:
                for bb in [b] + b.aliases:
                    w = lastw.get(id(bb))
                    if w is not None:
                        d.add(w)
            for b in writes:
                for bb in [b] + b.aliases:
                    w = lastw.get(id(bb))
                    if w is not None:
                        d.add(w)
                    for r in readers.get(id(bb), ()):
                        d.add(r)
            d.discard(i)
            deps[i] = d
            for j in d:
                succ[j].append(i)
            for b in reads:
                readers.setdefault(id(b), []).append(i)
            for b in writes:
                lastw[id(b)] = i
                readers[id(b)] = []
        ndep = [len(d) for d in deps]
        lp = [0.0] * n
        for i in range(n - 1, -1, -1):
            kind_, eng_, _, _, _, _, cost_ = nodes[i]
            c_ = (cost_ if cost_ is not None else self.COST[eng_]) if kind_ == 'op' else 3.0
            m_ = 0.0
            for k in succ[i]:
                if lp[k] > m_:
                    m_ = lp[k]
            lp[i] = c_ + m_ + 0.35
        finish = [0.0] * n
        etime = {e: 0.0 for e in self.eng}
        ready = [i for i in range(n) if ndep[i] == 0]
        cur_tbl = None
        done = [False] * n
        lo = 0
        nsched = 0
        while nsched < n:
            while lo < n and done[lo]:
                lo += 1
            cands = []
            tmin = None
            for i in ready:
                if i > lo + self.WINDOW:
                    continue
                eng = nodes[i][1]
                st = etime[eng]
                for j in deps[i]:
                    f = finish[j] + (0.0 if nodes[j][1] == eng else 0.35)
                    if f > st:
                        st = f
                if eng == 'act' and nodes[i][0] == 'op' and nodes[i][5] is not None and nodes[i][5] != cur_tbl:
                    st += 1.3
                cands.append((st, i))
                if tmin is None or st < tmin:
                    tmin = st
            best, bkey = None, None
            for (st, i) in cands:
                if st <= tmin + self.SLACK:
                    key = (-lp[i], i)
                    if bkey is None or key < bkey:
                        best, bkey, bst = i, key, st
            i = best
            kind, eng, fn, reads, writes, kw, cost = nodes[i]
            st = bst
            if kind == 'op':
                c = cost if cost is not None else self.COST[eng]
                if eng == 'act' and kw is not None:
                    if kw != cur_tbl:
                        c += 1.3
                    cur_tbl = kw
                self._emit_op(eng, fn, reads, writes)
                etime[eng] = st + c
                finish[i] = st + c
            else:
                self._emit_dma(eng, fn[0], fn[1], reads, writes, **kw)
                etime[eng] = st + 0.1
                finish[i] = st + 3.0
            done[i] = True
            nsched += 1
            ready.remove(i)
            for k in succ[i]:
                ndep[k] -= 1
                if ndep[k] == 0:
                    ready.append(k)

    def barrier(self):
        self.flush()
        tot = dict(self.cnt)
        tot.update(self.dtot)
        for eng in self.eng:
            e = self.eng[eng]
            for k, v in tot.items():
                if v == 0:
                    continue
                if self.seen[eng].get(k, 0) < v:
                    e.wait_ge(self._sem(k), v)
                    self.seen[eng][k] = v
                    self.nwaits += 1

    def wait_all(self, eng, bufs):
        self.flush()
        self._deps(eng, bufs, ())


def build_program(ntok=4096, seqlen=2048, phases=None, dbg=False, mixers=('gla', 'sgu', 'ssd')):
    if phases is None:
        phases = []
        for l in range(DEPTH):
            phases += [('A', l), ('B', l)]
    nc = bass.Bass("TRN2", target_bir_lowering=False)
    dt_in = lambda name, shape: nc.dram_tensor(name, shape, F32, kind="ExternalInput").ap()
    x_d = dt_in("x", [ntok, D])
    ln_in_g = dt_in("ln_in_g", [D])
    ln_in_b = dt_in("ln_in_b", [D])
    w_in_d = dt_in("w_in", [DEPTH, D, DIN])
    gla_w_gate = dt_in("gla_w_gate", [DEPTH, 16, 128])
    gla_b_gate = dt_in("gla_b_gate", [DEPTH, 128])
    gla_norm_g = dt_in("gla_norm_g", [DEPTH, 64])
    sgu_norm_g = dt_in("sgu_norm_g", [DEPTH, 256])
    sgu_norm_b = dt_in("sgu_norm_b", [DEPTH, 256])
    sgu_w = dt_in("sgu_w", [DEPTH, 4, 128, 128])
    sgu_b = dt_in("sgu_b", [DEPTH, 4, 128])
    ssd_conv_w = dt_in("ssd_conv_w", [DEPTH, 4, 1024])
    ssd_conv_b = dt_in("ssd_conv_b", [DEPTH, 1024])
    ssd_dt_bias = dt_in("ssd_dt_bias", [DEPTH, 8])
    ssd_a_log = dt_in("ssd_a_log", [DEPTH, 8])
    ssd_d = dt_in("ssd_d", [DEPTH, 8])
    ssd_norm_g = dt_in("ssd_norm_g", [DEPTH, 512])
    w_out_d = dt_in("w_out", [DEPTH, D, D])
    ln1_g = dt_in("ln1_g", [DEPTH, D])
    ln1_b = dt_in("ln1_b", [DEPTH, D])
    w_up_d = dt_in("ffn_w_up", [DEPTH, D, 2 * DFF])
    ffn_conv_w = dt_in("ffn_conv_w", [DEPTH, 3, 2 * DFF])
    ffn_conv_b = dt_in("ffn_conv_b", [DEPTH, 2 * DFF])
    w_down_d = dt_in("ffn_w_down", [DEPTH, DFF, D])
    ln2_g = dt_in("ln2_g", [DEPTH, D])
    ln2_b = dt_in("ln2_b", [DEPTH, D])
    y_d = nc.dram_tensor("y", [ntok, D], F32, kind="ExternalOutput").ap()
    hA_d = nc.dram_tensor("hA", [ntok, D], F32, kind="Internal").ap()
    hB_d = nc.dram_tensor("hB", [ntok, D], F32, kind="Internal").ap()

    stack = ExitStack()
    with stack:
        P = Prog(nc, stack)
        sb = lambda name, shape, dt=F32: stack.enter_context(nc.sbuf_tensor(name, shape, dt))

        B_wdown, B_wup, B_wout, B_win = Buf("wdown"), Buf("wup"), Buf("wout"), Buf("win")

        ident_f = sb("ident_f", [128, 128], F32)
        ident_b = sb("ident_b", [128, 128], BF16)
        B_const = Buf("const")
        lng = sb("lng", [128, D], F32)
        lnb = sb("lnb", [128, D], F32)
        B_ln = Buf("ln")
        B_ln0 = Buf("ln0")
        eps_t = sb("eps_t", [128, 1], F32)

        psum = [stack.enter_context(nc.psum_tensor(f"ps{i}", [128, 512], F32)) for i in range(8)]
        B_ps = [Buf(f"ps{i}", psum=True) for i in range(8)]

        P.op('pool', lambda e: e.memset(ident_f[:], 1.0), writes=[B_const])
        P.op('pool', lambda e: e.affine_select(out=ident_f[:], in_=ident_f[:], pattern=[[-1, 128]],
                                               compare_op=ALU.is_equal, fill=0.0, base=0, channel_multiplier=1),
             reads=[B_const], writes=[B_const])
        P.op('pool', lambda e: e.tensor_copy(out=ident_b[:], in_=ident_f[:]), reads=[B_const], writes=[B_const])
        P.op('pool', lambda e: e.memset(eps_t[:], EPS), writes=[B_const])

        def layernorm(src, dst, g_t, b_t, Bsrc, Bdst, Bg, tmp, Btmp):
            st, mv, sc = tmp
            for hh in range(2):
                P.op('dve', lambda e, hh=hh: e.bn_stats(out=st[:, hh, :], in_=src[:, hh * 512:(hh + 1) * 512]),
                     reads=[Bsrc], writes=[Btmp])
            P.op('dve', lambda e: e.bn_aggr(out=mv[:], in_=st[:].rearrange("p a b -> p (a b)")), reads=[Btmp], writes=[Btmp])
            P.op('act', lambda e: e.activation(out=sc[:, 0:1], in_=mv[:, 1:2], func=AF.Ln, bias=eps_t[:], scale=1.0),
                 reads=[Btmp, B_const], writes=[Btmp])
            P.op('act', lambda e: e.activation(out=sc[:, 1:2], in_=sc[:, 0:1], func=AF.Exp, scale=-0.5),
                 reads=[Btmp], writes=[Btmp])
            P.op('dve', lambda e: e.tensor_scalar(out=sc[:, 2:3], in0=mv[:, 0:1], scalar1=sc[:, 1:2], scalar2=-1.0,
                                                  op0=ALU.mult, op1=ALU.mult), reads=[Btmp], writes=[Btmp])
            P.op('act', lambda e: e.activation(out=dst, in_=src, func=AF.Identity, bias=sc[:, 2:3], scale=sc[:, 1:2]),
                 reads=[Bsrc, Btmp], writes=[Bdst], cost=0.95)
            P.op('dve', lambda e: e.tensor_tensor(out=dst, in0=dst, in1=g_t[:], op=ALU.mult), reads=[Bdst, Bg], writes=[Bdst], cost=1.15)
            P.op('pool', lambda e: e.tensor_tensor(out=dst, in0=dst, in1=b_t[:], op=ALU.add), reads=[Bdst, Bg], writes=[Bdst], cost=2.0)

        def load_ln_consts(g_row, b_row, g_t, b_t, Bg):
            P.dma('sp', g_t[:], g_row.partition_broadcast(128), reads=[], writes=[Bg])
            P.dma('sp', b_t[:], b_row.partition_broadcast(128), reads=[], writes=[Bg])

        def load_weight(dst3, src2, nk, Bw, rows_per=128):
            N = src2.shape[1]
            for k in range(nk):
                c0 = 0
                while c0 < N:
                    c1 = min(N, c0 + 2048)
                    P.dma('pool', dst3[:, k, c0:c1], src2[k * 128:(k + 1) * 128, c0:c1], reads=[], writes=[Bw])
                    c0 = c1

        TB = 256
        ntile = ntok // TB
        B_hA = [Buf(f"hA{i}") for i in range(ntile)]
        B_hB = [Buf(f"hB{i}") for i in range(ntile)]
        B_y = [Buf(f"y{i}") for i in range(ntile)]

        def phase_b(layer, src_d, Bsrc_tiles, dst_d, Bdst_tiles, pre_ln):
            bstack = ExitStack()
            with bstack:
                sbb = lambda name, shape, dt=F32: bstack.enter_context(nc.sbuf_tensor(f"b{layer}_{name}", shape, dt))
                w_up = sbb("w_up", [128, 8, 2 * DFF], BF16)
                w_down = sbb("w_down", [128, NJ, D], BF16)
                NG = 4
                JG = 6
                B_wupg = [Buf(f"wup{g}") for g in range(NG)]
                B_wdng = [Buf(f"wdn{g}") for g in range(NG)]
                for g in range(NG):
                    j0, j1 = g * JG, min(NJ, (g + 1) * JG)
                    for a in range(2):
                        c0, c1 = a * DFF + j0 * 128, a * DFF + j1 * 128
                        for k in range(8):
                            P.dma('pool', w_up[:, k, c0:c1], w_up_d[layer][k * 128:(k + 1) * 128, c0:c1], reads=[], writes=[B_wupg[g]])
                    for j in range(j0, j1):
                        P.dma('pool', w_down[:, j, :], w_down_d[layer][j * 128:(j + 1) * 128, :], reads=[], writes=[B_wdng[g]])
                load_ln_consts(ln2_g[layer], ln2_b[layer], lng, lnb, B_ln)
                if pre_ln:
                    lng0 = sbb("lng0", [128, D], F32)
                    lnb0 = sbb("lnb0", [128, D], F32)
                    load_ln_consts(ln_in_g, ln_in_b, lng0, lnb0, B_ln0)
                cwraw = sbb("cwraw", [44, 4, 128])
                cw = sbb("cw", [128, 4, 44])
                B_cwraw, B_cw = Buf("cwraw"), Buf("cw")
                for k in range(3):
                    P.dma('sp', cwraw[:, k, :], ffn_conv_w[layer, k].rearrange("(c p) -> c p", p=128), reads=[], writes=[B_cwraw])
                P.dma('sp', cwraw[:, 3, :], ffn_conv_b[layer].rearrange("(c p) -> c p", p=128), reads=[], writes=[B_cwraw])
                for k in range(4):
                    P.op('pe', lambda e, k=k: e.transpose(out=psum[6][:, 0:44], in_=cwraw[:, k, :], identity=ident_f[0:44, 0:44]),
                         reads=[B_cwraw, B_const], writes=[B_ps[6]])
                    P.op('dve', lambda e, k=k: e.tensor_copy(out=cw[:, k, :], in_=psum[6][:, 0:44]), reads=[B_ps[6]], writes=[B_cw])

                xin = [sbb(f"xin{i}", [128, 2, D]) for i in range(2)]
                B_xin = [Buf(f"xin{i}") for i in range(2)]
                xbf = sbb("xbf", [128, 2, D], BF16)
                B_xbf = Buf("xbf")
                hT = [sbb(f"hT{i}", [128, 8, TB], BF16) for i in range(2)]
                B_hT = [Buf(f"hT{i}") for i in range(2)]
                halo = sbb("halo", [128, NJ, 2, 2])
                B_halo = Buf("halo")
                NP = 3
                xs = [sbb(f"xs{i}", [128, 2, 2 + TB]) for i in range(NP)]
                B_xs = [Buf(f"xs{i}") for i in range(NP)]
                cg = [sbb(f"cg{i}", [128, TB]) for i in range(NP)]
                cv = [sbb(f"cv{i}", [128, TB]) for i in range(NP)]
                B_cg = [Buf(f"cg{i}") for i in range(NP)]
                B_cv = [Buf(f"cv{i}") for i in range(NP)]
                sg = [sbb(f"sg{i}", [128, TB]) for i in range(NP)]
                B_sg = [Buf(f"sg{i}") for i in range(NP)]
                NA = 4
                aT = [sbb(f"aT{i}", [128, TB], BF16) for i in range(NA)]
                B_aT = [Buf(f"aT{i}") for i in range(NA)]
                rr = [sbb(f"rr{i}", [128, D]) for i in range(2)]
                B_rr = [Buf(f"rr{i}") for i in range(2)]
                lst = sbb("lst", [128, 2, 6]); lmv = sbb("lmv", [128, 2]); lsc = sbb("lsc", [128, 4])
                B_ltmp = Buf("ltmp")
                ltmp = (lst, lmv, lsc)
                pT = psum[7][:].bitcast(BF16)

                def load(t):
                    tok0 = t * TB
                    xi, Bxi = xin[t % 2], B_xin[t % 2]
                    P.dma('sp', xi[:], src_d[tok0:tok0 + TB, :].rearrange("(c p) f -> p c f", p=128),
                          reads=[Bsrc_tiles[t]], writes=[Bxi])
                    if pre_ln:
                        for c in range(2):
                            layernorm(xi[:, c, :], xi[:, c, :], lng0, lnb0, Bxi, Bxi, B_ln0, ltmp, B_ltmp)

                def prologue(t):
                    xi, Bxi = xin[t % 2], B_xin[t % 2]
                    h_, Bh_ = hT[t % 2], B_hT[t % 2]
                    P.op('act', lambda e: e.copy(out=xbf[:], in_=xi[:]), reads=[Bxi], writes=[B_xbf])
                    for c in range(2):
                        for fc in range(8):
                            P.op('pe', lambda e, c=c, fc=fc: e.transpose(
                                out=pT[:, fc * 128:(fc + 1) * 128], in_=xbf[:, c, fc * 128:(fc + 1) * 128], identity=ident_b[:]),
                                reads=[B_xbf, B_const], writes=[B_ps[7]])
                        P.op('dve', lambda e, c=c: e.tensor_copy(
                            out=h_[:, :, c * 128:(c + 1) * 128], in_=pT[:, :].rearrange("p (q t) -> p q t", q=8)),
                            reads=[B_ps[7]], writes=[Bh_])

                def up(t, j):
                    h_, Bh_ = hT[t % 2], B_hT[t % 2]
                    bk = 4 + j % NP
                    pu3 = psum[bk][:].rearrange("p (a t) -> p a t", a=2)
                    for a in range(2):
                        col0 = a * DFF + j * 128
                        for kc in range(8):
                            P.op('pe', lambda e, a=a, kc=kc, col0=col0: e.matmul(
                                pu3[:, a, :], lhsT=w_up[:, kc, col0:col0 + 128], rhs=h_[:, kc, :],
                                start=(kc == 0), stop=(kc == 7)),
                                reads=[B_wupg[j // JG], Bh_], writes=[B_ps[bk]], cost=0.18)

                def ew(t, j):
                    bk = 4 + j % NP
                    Bpu = B_ps[bk]
                    pu3 = psum[bk][:].rearrange("p (a t) -> p a t", a=2)
                    x_, Bx_ = xs[j % NP], B_xs[j % NP]
                    P.op('act', lambda e: e.copy(out=x_[:, :, 2:2 + TB], in_=pu3), reads=[Bpu], writes=[Bx_])
                    P.op('pool', lambda e: e.tensor_copy(out=x_[:, :, 0:2], in_=halo[:, j, :, :]), reads=[B_halo], writes=[Bx_])
                    P.op('pool', lambda e: e.tensor_copy(out=halo[:, j, :, :], in_=x_[:, :, TB:TB + 2]), reads=[Bx_], writes=[B_halo])
                    for a, (ct, Bct) in enumerate(((cg[j % NP], B_cg[j % NP]), (cv[j % NP], B_cv[j % NP]))):
                        ch = a * NJ + j
                        P.op('act', lambda e, a=a, ch=ch, ct=ct: e.activation(
                            out=ct[:], in_=pu3[:, a, :], func=AF.Identity, bias=cw[:, 3, ch:ch + 1], scale=cw[:, 2, ch:ch + 1]),
                            reads=[Bpu, B_cw], writes=[Bct])
                        P.op('dve', lambda e, a=a, ch=ch, ct=ct: e.scalar_tensor_tensor(
                            out=ct[:], in0=x_[:, a, 1:1 + TB], scalar=cw[:, 1, ch:ch + 1], in1=ct[:], op0=ALU.mult, op1=ALU.add),
                            reads=[Bx_, B_cw, Bct], writes=[Bct])
                        P.op('dve', lambda e, a=a, ch=ch, ct=ct: e.scalar_tensor_tensor(
                            out=ct[:], in0=x_[:, a, 0:TB], scalar=cw[:, 0, ch:ch + 1], in1=ct[:], op0=ALU.mult, op1=ALU.add),
                            reads=[Bx_, B_cw, Bct], writes=[Bct])
                    s_, Bs_ = sg[j % NP], B_sg[j % NP]
                    P.op('act', lambda e: e.activation(out=s_[:], in_=cg[j % NP][:], func=AF.Silu),
                         reads=[B_cg[j % NP]], writes=[Bs_])
                    a_, Ba_ = aT[j % NA], B_aT[j % NA]
                    P.op('pool', lambda e: e.tensor_tensor(out=a_[:], in0=s_[:], in1=cv[j % NP][:], op=ALU.mult),
                         reads=[Bs_, B_cv[j % NP]], writes=[Ba_])

                def down(t, j):
                    a_, Ba_ = aT[j % NA], B_aT[j % NA]
                    for c in range(2):
                        for hf in range(2):
                            bk = c * 2 + hf
                            P.op('pe', lambda e, c=c, hf=hf, bk=bk: e.matmul(
                                psum[bk][:], lhsT=a_[:, c * 128:(c + 1) * 128], rhs=w_down[:, j, hf * 512:(hf + 1) * 512],
                                start=(j == 0), stop=(j == NJ - 1)),
                                reads=[Ba_, B_wdng[j // JG]], writes=[B_ps[bk]], cost=0.39)

                def epilogue(t):
                    tok0 = t * TB
                    xi, Bxi = xin[t % 2], B_xin[t % 2]
                    for c in range(2):
                        r_, Br_ = rr[c], B_rr[c]
                        for hf in range(2):
                            bk = c * 2 + hf
                            P.op('dve', lambda e, c=c, hf=hf, bk=bk, r_=r_: e.scalar_tensor_tensor(
                                out=r_[:, hf * 512:(hf + 1) * 512], in0=xi[:, c, hf * 512:(hf + 1) * 512], scalar=ALPHA,
                                in1=psum[bk][:], op0=ALU.mult, op1=ALU.add),
                                reads=[Bxi, B_ps[bk]], writes=[Br_])
                    for c in range(2):
                        layernorm(rr[c][:], xi[:, c, :], lng, lnb, B_rr[c], Bxi, B_ln, ltmp, B_ltmp)
                    P.dma('sp', dst_d[tok0:tok0 + TB, :].rearrange("(c p) f -> p c f", p=128), xi[:],
                          reads=[Bxi], writes=[Bdst_tiles[t]])

                load(0)
                prologue(0)
                for t in range(ntile):
                    tok0 = t * TB
                    if t + 1 < ntile:
                        load(t + 1)
                    if tok0 % seqlen == 0:
                        P.op('pool', lambda e: e.memset(halo[:], 0.0), writes=[B_halo])
                    for j in range(min(NP - 1, NJ)):
                        up(t, j)
                    for j in range(NJ):
                        if j + NP - 1 < NJ:
                            up(t, j + NP - 1)
                        ew(t, j)
                        down(t, j)
                        if j == NJ - 4 and t + 1 < ntile:
                            prologue(t + 1)
                    epilogue(t)

        def phase_a(layer, src_d, Bsrc_tiles, dst_d, Bdst_tiles, pre_ln):
            NT = TB
            NCH = NT // 128
            astack = ExitStack()
            with astack:
                sba = lambda name, shape, dt=F32: astack.enter_context(nc.sbuf_tensor(f"a{layer}_{name}", shape, dt))
                w_in = sba("w_in", [128, 8, DIN], BF16)
                w_out = sba("w_out", [128, 8, D], BF16)
                load_weight(w_in, w_in_d[layer], 8, B_win)
                load_weight(w_out, w_out_d[layer], 8, B_wout)
                load_ln_consts(ln1_g[layer], ln1_b[layer], lng, lnb, B_ln)
                if pre_ln:
                    lng0 = sba("lng0", [128, D], F32)
                    lnb0 = sba("lnb0", [128, D], F32)
                    load_ln_consts(ln_in_g, ln_in_b, lng0, lnb0, B_ln0)
                Bc = Buf("aconst")
                tri_f = sba("tri_f", [128, 128])
                maskge_b = sba("maskge_b", [128, 128], BF16)
                maskgt_f = sba("maskgt_f", [128, 128])
                ones_f = sba("ones_f", [128, 128])
                ones_b = sba("ones_b", [128, 128], BF16)
                one_t = sba("one_t", [128, 1])
                P.op('pool', lambda e: e.memset(ones_f[:], 1.0), writes=[Bc])
                P.op('pool', lambda e: e.memset(ones_b[:], 1.0), writes=[Bc])
                P.op('pool', lambda e: e.memset(one_t[:], 1.0), writes=[Bc])
                hm = sba("hm", [64, 4])
                P.op('pool', lambda e: e.memset(hm[:], 0.0), writes=[Bc])
                for h in range(4):
                    hb = (h % 2) * 32
                    P.op('pool', lambda e, h=h, hb=hb: e.memset(hm[hb:hb + 32, h:h + 1], 32.0 ** -0.5), writes=[Bc])
                P.op('pool', lambda e: e.affine_select(out=tri_f[:], in_=ones_f[:], pattern=[[1, 128]], compare_op=ALU.is_ge,
                                                       fill=0.0, base=0, channel_multiplier=-1), reads=[Bc], writes=[Bc])
                P.op('pool', lambda e: e.tensor_copy(out=maskge_b[:], in_=tri_f[:]), reads=[Bc], writes=[Bc])
                P.op('pool', lambda e: e.affine_select(out=maskgt_f[:], in_=ones_f[:], pattern=[[-1, 128]], compare_op=ALU.is_gt,
                                                       fill=0.0, base=0, channel_multiplier=1), reads=[Bc], writes=[Bc])
                craw = sba("craw", [44, 128])
                ccol = sba("ccol", [128, 44])
                B_craw = Buf("craw")
                P.dma('sp', craw[0:32, :], ssd_conv_w[layer].rearrange("k (c p) -> (k c) p", p=128), reads=[], writes=[B_craw])
                P.dma('sp', craw[32:40, :], ssd_conv_b[layer].rearrange("(c p) -> c p", p=128), reads=[], writes=[B_craw])
                P.dma('sp', craw[40:42, :], sgu_norm_g[layer].rearrange("(c p) -> c p", p=128), reads=[], writes=[B_craw])
                P.dma('sp', craw[42:44, :], sgu_norm_b[layer].rearrange("(c p) -> c p", p=128), reads=[], writes=[B_craw])
                P.op('pe', lambda e: e.transpose(out=psum[0][:, 0:44], in_=craw[:, :], identity=ident_f[0:44, 0:44]),
                     reads=[B_craw, B_const], writes=[B_ps[0]])
                P.op('dve', lambda e: e.tensor_copy(out=ccol[:], in_=psum[0][:, 0:44]), reads=[B_ps[0]], writes=[Bc])
                CW = lambda k, fc: ccol[:, k * 8 + fc:k * 8 + fc + 1]
                CB_ = lambda fc: ccol[:, 32 + fc:33 + fc]
                SGG = lambda fc: ccol[:, 40 + fc:41 + fc]
                SGB = lambda fc: ccol[:, 42 + fc:43 + fc]
                dtb_bc = sba("dtb_bc", [128, 8]); acont_bc = sba("acont_bc", [128, 8]); dsk_bc = sba("dsk_bc", [128, 8])
                sng_bc = sba("sng_bc", [128, 512]); gng_bc = sba("gng_bc", [128, 64])
                B_bc = Buf("bcast")
                P.dma('sp', dtb_bc[:], ssd_dt_bias[layer].partition_broadcast(128), reads=[], writes=[B_bc])
                P.dma('sp', acont_bc[:], ssd_a_log[layer].partition_broadcast(128), reads=[], writes=[B_bc])
                P.dma('sp', dsk_bc[:], ssd_d[layer].partition_broadcast(128), reads=[], writes=[B_bc])
                P.dma('sp', sng_bc[:], ssd_norm_g[layer].partition_broadcast(128), reads=[], writes=[B_bc])
                P.dma('sp', gng_bc[:], gla_norm_g[layer].partition_broadcast(128), reads=[], writes=[B_bc])
                P.op('act', lambda e: e.activation(out=acont_bc[:], in_=acont_bc[:], func=AF.Exp), reads=[B_bc], writes=[B_bc])
                P.op('act', lambda e: e.mul(acont_bc[:], acont_bc[:], -1.0), reads=[B_bc], writes=[B_bc])
                wg_f = sba("wg_f", [17, 128]); wg_b = sba("wg_b", [17, 128], BF16)
                B_wg = Buf("wg")
                P.dma('sp', wg_f[0:16, :], gla_w_gate[layer], reads=[], writes=[B_wg])
                P.dma('sp', wg_f[16:17, :], gla_b_gate[layer].rearrange("(a n) -> a n", a=1), reads=[], writes=[B_wg])
                P.op('dve', lambda e: e.tensor_copy(out=wg_b[:], in_=wg_f[:]), reads=[B_wg], writes=[Bc])
                wsg = sba("wsg", [128, 4, 128]); wmT = sba("wmT", [128, 4, 128], BF16)
                bsbc = sba("bsbc", [128, 2, 128]); csg = sba("csg", [128, 2, 128])
                B_wsg = Buf("wsg")
                P.dma('sp', wsg[:], sgu_w[layer].rearrange("g t s -> t g s"), reads=[], writes=[B_wsg])
                for g in range(4):
                    hp = (g % 2) * 64
                    P.dma('sp', bsbc[hp:hp + 64, g // 2, :], sgu_b[layer, g].partition_broadcast(64), reads=[], writes=[B_bc])
                P.op('pool', lambda e: e.affine_select(out=wsg[:], in_=wsg[:], pattern=[[0, 4], [-1, 128]], compare_op=ALU.is_ge,
                                                       fill=0.0, base=0, channel_multiplier=1), reads=[B_wsg], writes=[B_wsg])
                for g in range(4):
                    P.op('pe', lambda e, g=g: e.transpose(out=psum[1][:, g * 128:(g + 1) * 128], in_=wsg[:, g, :], identity=ident_f[:]),
                         reads=[B_wsg, B_const], writes=[B_ps[1]])
                P.op('dve', lambda e: e.tensor_copy(out=wmT[:], in_=psum[1][:].rearrange("p (g t) -> p g t", g=4)),
                     reads=[B_ps[1]], writes=[Bc])
                for g in range(4):
                    P.op('pe', lambda e, g=g: e.matmul(psum[2][:, g * 128:(g + 1) * 128], lhsT=ones_b[:], rhs=wmT[:, g, :],
                                                       start=True, stop=True), reads=[Bc], writes=[B_ps[2]])
                for g in range(4):
                    hp, fc = (g % 2) * 64, g // 2
                    P.op('dve', lambda e, g=g, hp=hp, fc=fc: e.scalar_tensor_tensor(
                        out=csg[hp:hp + 64, fc, :], in0=psum[2][hp:hp + 64, g * 128:(g + 1) * 128], scalar=ccol[hp:hp + 64, 42 + fc:43 + fc],
                        in1=bsbc[hp:hp + 64, fc, :], op0=ALU.mult, op1=ALU.add), reads=[B_ps[2], Bc, B_bc], writes=[Bc])

                xin = [sba(f"xin{i}", [128, NCH, D]) for i in range(2)]
                B_xin = [Buf(f"axin{i}") for i in range(2)]
                xbf = sba("xbf", [128, NCH, D], BF16); B_xbf = Buf("axbf")
                hT2 = [sba(f"hT{i}", [128, 8, NT], BF16) for i in range(2)]; B_hT2 = [Buf(f"ahT{i}") for i in range(2)]
                qk2 = [sba(f"qk_f{i}", [64, 4, NT]) for i in range(2)]; B_qk2 = [Buf(f"qk{i}") for i in range(2)]
                glr2 = [sba(f"glrT{i}", [32, NT], BF16) for i in range(2)]; B_glr2 = [Buf(f"glr{i}") for i in range(2)]
                gu2 = [sba(f"gu{i}", [128, 2, NT]) for i in range(2)]; B_gu2 = [Buf(f"gu{i}") for i in range(2)]
                gtp1 = sba("gtp1", [128, 512]); gtp2 = sba("gtp2", [128, 512]); B_gtp = Buf("gtp")
                lstp = sba("lstp", [128, 2, 6]); lmvp = sba("lmvp", [128, 2]); lscp = sba("lscp", [128, 4])
                B_ltmpp = Buf("altmpp")
                ltmpp = (lstp, lmvp, lscp)
                B7 = [B_ps[7], B_ps[7]]
                xr = sba("xr", [128, 8, 3 + NT]); B_xr = Buf("xr")
                xhalo = sba("xhalo", [128, 8, 3]); B_xhalo = Buf("xhalo")
                ct = [sba(f"ct{i}", [128, NT]) for i in range(2)]; B_ct = [Buf(f"ct{i}") for i in range(2)]
                xc2 = [sba(f"xc{i}", [128, 8, NT], BF16) for i in range(2)]; B_xc2 = [Buf(f"xc{i}") for i in range(2)]
                mixT = sba("mixT", [128, 8, 128], BF16); B_mixT = Buf("mixT")
                vb2 = [sba(f"vb{i}", [128, NCH, 256], BF16) for i in range(2)]; B_vb2 = [Buf(f"vb{i}") for i in range(2)]
                gs2 = [sba(f"gs{i}", [128, NCH, 256]) for i in range(2)]; B_gs2 = [Buf(f"gs{i}") for i in range(2)]
                svg = sba("svg", [128, 256]); B_sv = Buf("sv")
                xhat2 = [sba(f"xhat{i}", [128, NCH, 256], BF16) for i in range(2)]; B_xhat2 = [Buf(f"xhat{i}") for i in range(2)]
                zs2 = [sba(f"zs{i}", [128, NCH, 512]) for i in range(2)]; B_zs2 = [Buf(f"zs{i}") for i in range(2)]
                dta2 = [sba(f"dta{i}", [128, NCH, 16]) for i in range(2)]; B_dta2 = [Buf(f"dta{i}") for i in range(2)]
                la2 = [sba(f"la{i}", [128, NCH, 128]) for i in range(2)]; B_la2 = [Buf(f"la{i}") for i in range(2)]
                dts = sba("dts", [128, 64]); B_dts = Buf("dts")
                e1 = sba("e1", [128, 128]); B_e1 = Buf("e1")
                ecp = sba("ecp", [64, 2, 128]); ecn = sba("ecn", [64, 2, 128]); B_ec = Buf("ec")
                qt = sba("qt", [64, 4, 128], BF16); kt = sba("kt", [64, 2, 128], BF16); B_qkt = Buf("qkt")
                ktm = sba("ktm", [128, 128], BF16); B_ktm = Buf("ktm")
                sm = sba("sm", [128, 4, 128], BF16); B_sm = Buf("sm")
                gst = sba("gst", [64, 2, 128]); gstt = sba("gstt", [64, 2, 128]); gstb = sba("gstb", [64, 2, 128], BF16)
                B_gst = Buf("gst"); B_gstb = Buf("gstb")
                osq = sba("osq", [128, 512]); B_osq = Buf("osq")
                osq2 = sba("osq2", [128, 512]); B_osq2 = Buf("osq2")
                B5a = B5b = B_ps[5]
                B6a = B6c = B_ps[6]
                og = sba("og", [128, 256]); B_og = Buf("og")
                ogl = sba("ogl", [128, 256], BF16); B_ogl = Buf("ogl")
                gsc = sba("gsc", [128, 16]); B_gsc = Buf("gsc")
                sgt = sba("sgt", [128, 128]); B_sgt = Buf("sgt")
                ldec = sba("ldec", [128, 8, 128]); B_ldec = Buf("ldec")
                dm = sba("dm", [128, 8, 128], BF16); B_dm = Buf("dm")
                mm_ = sba("mm_", [128, 8, 128], BF16); B_mm = Buf("mm")
                cbm = sba("cbm", [128, 2, 128], BF16); B_cbm = Buf("cbm")
                xdt = sba("xdt", [128, 512], BF16); xdd = sba("xdd", [128, 512], BF16); B_xdt = Buf("xdt"); B_xdd = Buf("xdd")
                xsd = sba("xsd", [128, 512]); B_xsd = Buf("xsd")
                bmtm = sba("bmtm", [128, 256], BF16); B_bmtm = Buf("bmtm")
                y1 = sba("y1", [128, 512]); B_y1 = Buf("y1")
                yb = sba("yb", [128, 512], BF16); B_yb = Buf("yb")
                sst = sba("sst", [128, 512]); sstb = sba("sstb", [128, 512], BF16); B_sst = Buf("sst"); B_sstb = Buf("sstb")
                lst = sba("lst", [128, 2, 6]); lmv = sba("lmv", [128, 2]); lsc = sba("lsc", [128, 4])
                B_ltmp = Buf("altmp")
                ltmp = (lst, lmv, lsc)
                pbf = [psum[i][:].bitcast(BF16) for i in range(8)]

                for i in range(2):
                    P.op('pool', lambda e, i=i: e.memset(glr2[i][:], 1.0), writes=[B_glr2[i]])

                def gelu(dst, x_sb, n, Bx, Bdst, gt1=None, gt2=None, B_gt=None):
                    P.op('dve', lambda e: e.scalar_tensor_tensor(out=gt1[:, 0:n], in0=x_sb, scalar=0.044715, in1=x_sb,
                                                                 op0=ALU.mult, op1=ALU.mult), reads=[Bx], writes=[B_gt])
                    P.op('dve', lambda e: e.scalar_tensor_tensor(out=gt2[:, 0:n], in0=gt1[:, 0:n], scalar=1.0, in1=x_sb,
                                                                 op0=ALU.add, op1=ALU.mult), reads=[Bx, B_gt], writes=[B_gt])
                    P.op('act', lambda e: e.activation(out=gt1[:, 0:n], in_=gt2[:, 0:n], func=AF.Sigmoid, scale=1.5957691216057308),
                         reads=[B_gt], writes=[B_gt])
                    P.op('pool', lambda e: e.tensor_tensor(out=dst, in0=gt1[:, 0:n], in1=x_sb, op=ALU.mult),
                         reads=[B_gt, Bx], writes=[Bdst])

                def inproj_tm(bank, c, c0, N, o0=0):
                    for kc in range(8):
                        P.op('pe', lambda e, kc=kc: e.matmul(
                            psum[bank][:, o0:o0 + N], lhsT=hT[:, kc, c * 128:(c + 1) * 128], rhs=w_in[:, kc, c0:c0 + N],
                            start=(kc == 0), stop=(kc == 7)), reads=[B_win, B_hT], writes=[B_ps[bank]])

                ntile_a = ntok // NT

                def gen_pro(t):
                    par = t % 2
                    tok0 = t * NT
                    xi, Bxi = xin[par], B_xin[par]
                    h_, Bh_ = hT2[par], B_hT2[par]
                    qk_, Bqk_ = qk2[par], B_qk2[par]
                    gl_, Bgl_ = glr2[par], B_glr2[par]
                    gu_, Bgu_ = gu2[par], B_gu2[par]
                    xc_, Bxc_ = xc2[par], B_xc2[par]
                    P.dma('sp', xi[:], src_d[tok0:tok0 + NT, :].rearrange("(c p) f -> p c f", p=128),
                          reads=[Bsrc_tiles[t]], writes=[Bxi])
                    yield
                    if pre_ln:
                        for c in range(NCH):
                            layernorm(xi[:, c, :], xi[:, c, :], lng0, lnb0, Bxi, Bxi, B_ln0, ltmpp, B_ltmpp)
                            yield
                    if tok0 % seqlen == 0:
                        P.op('pool', lambda e: e.memset(xhalo[:], 0.0), writes=[B_xhalo])
                    P.op('act', lambda e: e.copy(out=xbf[:], in_=xi[:]), reads=[Bxi], writes=[B_xbf])
                    yield
                    hh = 0
                    for c in range(NCH):
                        for half in range(2):
                            pt, Bpt = pbf[7][:, hh * 512:(hh + 1) * 512], B7[hh]
                            for q in range(4):
                                fc = half * 4 + q
                                P.op('pe', lambda e, c=c, fc=fc, q=q, pt=pt: e.transpose(
                                    out=pt[:, q * 128:(q + 1) * 128], in_=xbf[:, c, fc * 128:(fc + 1) * 128], identity=ident_b[:]),
                                    reads=[B_xbf, B_const], writes=[Bpt])
                            P.op('dve', lambda e, c=c, half=half, pt=pt: e.tensor_copy(
                                out=h_[:, half * 4:(half + 1) * 4, c * 128:(c + 1) * 128],
                                in_=pt.rearrange("p (q t) -> p q t", q=4)),
                                reads=[Bpt], writes=[Bh_])
                            hh ^= 1
                            yield
                    def fm(specs):
                        for (c0, M, o0) in specs:
                            for kc in range(8):
                                P.op('pe', lambda e, c0=c0, M=M, o0=o0, kc=kc: e.matmul(
                                    psum[7][0:M, o0:o0 + NT], lhsT=w_in[:, kc, c0:c0 + M], rhs=h_[:, kc, :],
                                    start=(kc == 0), stop=(kc == 7)), reads=[B_win, Bh_], writes=[B_ps[7]], cost=0.18)
                    for gi, c0 in enumerate((O_Q, O_K)):
                        fm([(c0, 64, 0), (c0 + 64, 64, NT)])
                        P.op('act', lambda e, gi=gi: e.copy(out=qk_[:, 2 * gi:2 * gi + 2, :],
                                                            in_=psum[7][0:64, :].rearrange("p (a t) -> p a t", a=2)),
                             reads=[B_ps[7]], writes=[Bqk_])
                        yield
                    fm([(O_GLR, 16, 0)])
                    P.op('act', lambda e: e.copy(out=gl_[0:16, :], in_=psum[7][0:16, 0:NT]), reads=[B_ps[7]], writes=[Bgl_])
                    yield
                    fm([(O_SU, 128, 0), (O_SU + 128, 128, NT)])
                    P.op('act', lambda e: e.copy(out=gu_[:], in_=psum[7][:].rearrange("p (a t) -> p a t", a=2)),
                         reads=[B_ps[7]], writes=[Bgu_])
                    yield
                    gelu(gu_[:].rearrange("p a t -> p (a t)"), gu_[:].rearrange("p a t -> p (a t)"), 2 * NT, Bgu_, Bgu_,
                         gt1=gtp1, gt2=gtp2, B_gt=B_gtp)
                    yield
                    for pr in range(4):
                        fm([(O_XBC + (2 * pr) * 128, 128, 0), (O_XBC + (2 * pr + 1) * 128, 128, NT)])
                        P.op('act', lambda e, pr=pr: e.copy(out=xr[:, 2 * pr:2 * pr + 2, 3:3 + NT],
                                                            in_=psum[7][:].rearrange("p (a t) -> p a t", a=2)),
                             reads=[B_ps[7]], writes=[B_xr])
                        yield
                    P.op('pool', lambda e: e.tensor_copy(out=xr[:, :, 0:3], in_=xhalo[:]), reads=[B_xhalo], writes=[B_xr])
                    P.op('pool', lambda e: e.tensor_copy(out=xhalo[:], in_=xr[:, :, NT:NT + 3]), reads=[B_xr], writes=[B_xhalo])
                    yield
                    for fc in range(8):
                        c_, Bc_ = ct[fc % 2], B_ct[fc % 2]
                        P.op('act', lambda e, fc=fc, c_=c_: e.activation(out=c_[:], in_=xr[:, fc, 3:3 + NT], func=AF.Identity,
                                                                         bias=CB_(fc), scale=CW(3, fc)), reads=[B_xr, Bc], writes=[Bc_])
                        for k in (2, 1, 0):
                            P.op('dve', lambda e, fc=fc, k=k, c_=c_: e.scalar_tensor_tensor(
                                out=c_[:], in0=xr[:, fc, k:k + NT], scalar=CW(k, fc), in1=c_[:], op0=ALU.mult, op1=ALU.add),
                                reads=[B_xr, Bc, Bc_], writes=[Bc_])
                        P.op('act', lambda e, fc=fc, c_=c_: e.activation(out=xc_[:, fc, :], in_=c_[:], func=AF.Silu),
                             reads=[Bc_], writes=[Bxc_])
                        yield
                    vb_, gs_, zs_, xh_, dta_, la_ = vb2[par], gs2[par], zs2[par], xhat2[par], dta2[par], la2[par]
                    Bvb_, Bgs_, Bzs_, Bxh_, Bdta_, Bla_ = B_vb2[par], B_gs2[par], B_zs2[par], B_xhat2[par], B_dta2[par], B_la2[par]

                    def tm(c, c0, N, o0=0):
                        for kc in range(8):
                            P.op('pe', lambda e, kc=kc: e.matmul(
                                psum[7][:, o0:o0 + N], lhsT=h_[:, kc, c * 128:(c + 1) * 128], rhs=w_in[:, kc, c0:c0 + N],
                                start=(kc == 0), stop=(kc == 7)), reads=[B_win, Bh_], writes=[B_ps[7]], cost=0.1 + N * 0.00057)
                    for c in range(NCH):
                        cs_ = c * 128
                        tm(c, O_V, 512)
                        P.op('act', lambda e, c=c: e.copy(out=vb_[:, c, :], in_=psum[7][:, 0:256]), reads=[B_ps[7]], writes=[Bvb_])
                        P.op('act', lambda e, c=c: e.activation(out=gs_[:, c, :], in_=psum[7][:, 256:512], func=AF.Silu),
                             reads=[B_ps[7]], writes=[Bgs_])
                        yield
                        P.op('pool', lambda e, c=c: e.tensor_tensor(out=gs_[:, c, :].rearrange("p (h v) -> p h v", h=4),
                                                                    in0=gs_[:, c, :].rearrange("p (h v) -> p h v", h=4),
                                                                    in1=gng_bc[:].unsqueeze(1).broadcast_to([128, 4, 64]), op=ALU.mult),
                             reads=[Bgs_, B_bc], writes=[Bgs_])
                        tm(c, O_Z, 512)
                        P.op('act', lambda e, c=c: e.activation(out=zs_[:, c, :], in_=psum[7][:], func=AF.Silu), reads=[B_ps[7]], writes=[Bzs_])
                        yield
                        tm(c, O_SV, 256)
                        tm(c, O_DT, 8, o0=256)
                        P.op('act', lambda e: e.copy(out=svg[:], in_=psum[7][:, 0:256]), reads=[B_ps[7]], writes=[B_sv])
                        P.op('dve', lambda e, c=c: e.tensor_tensor(out=dta_[:, c, 0:8], in0=psum[7][:, 256:264], in1=dtb_bc[:], op=ALU.add),
                             reads=[B_ps[7], B_bc], writes=[Bdta_])
                        yield
                        P.op('act', lambda e, c=c: e.activation(out=dta_[:, c, 0:8], in_=dta_[:, c, 0:8], func=AF.Exp), reads=[Bdta_], writes=[Bdta_])
                        P.op('act', lambda e, c=c: e.activation(out=dta_[:, c, 0:8], in_=dta_[:, c, 0:8], func=AF.Ln, bias=one_t[:], scale=1.0),
                             reads=[Bdta_, Bc], writes=[Bdta_])
                        P.op('dve', lambda e, c=c: e.tensor_tensor(out=dta_[:, c, 8:16], in0=dta_[:, c, 0:8], in1=acont_bc[:], op=ALU.mult),
                             reads=[Bdta_, B_bc], writes=[Bdta_])
                        yield
                        P.op('pe', lambda e, cs_=cs_: e.matmul(psum[7][:, 0:128], lhsT=gl_[0:17, cs_:cs_ + 128], rhs=wg_b[:, :], start=True, stop=True),
                             reads=[Bgl_, Bc], writes=[B_ps[7]])
                        P.op('act', lambda e: e.activation(out=e1[:], in_=psum[7][:, 0:128], func=AF.Exp, scale=-1.0),
                             reads=[B_ps[7]], writes=[B_e1])
                        P.op('act', lambda e, c=c: e.activation(out=la_[:, c, :], in_=e1[:], func=AF.Ln, bias=one_t[:], scale=1.0),
                             reads=[B_e1, Bc], writes=[Bla_])
                        yield
                        gelu(svg[:], svg[:], 256, B_sv, B_sv, gt1=gtp1, gt2=gtp2, B_gt=B_gtp)
                        yield
                        P.op('dve', lambda e: e.bn_stats(out=lstp[:, 0, :], in_=svg[:]), reads=[B_sv], writes=[B_ltmpp])
                        P.op('dve', lambda e: e.bn_aggr(out=lmvp[:], in_=lstp[:, 0, :]), reads=[B_ltmpp], writes=[B_ltmpp])
                        P.op('act', lambda e: e.activation(out=lscp[:, 0:1], in_=lmvp[:, 1:2], func=AF.Ln, bias=eps_t[:], scale=1.0),
                             reads=[B_ltmpp, B_const], writes=[B_ltmpp])
                        P.op('act', lambda e: e.activation(out=lscp[:, 1:2], in_=lscp[:, 0:1], func=AF.Exp, scale=-0.5),
                             reads=[B_ltmpp], writes=[B_ltmpp])
                        yield
                        P.op('dve', lambda e: e.tensor_scalar(out=lscp[:, 2:3], in0=lmvp[:, 0:1], scalar1=lscp[:, 1:2], scalar2=-1.0,
                                                              op0=ALU.mult, op1=ALU.mult), reads=[B_ltmpp], writes=[B_ltmpp])
                        P.op('act', lambda e, c=c: e.activation(out=xh_[:, c, :], in_=svg[:], func=AF.Identity, bias=lscp[:, 2:3], scale=lscp[:, 1:2]),
                             reads=[B_sv, B_ltmpp], writes=[Bxh_])
                        yield

                def gen_ln1(xi_, Bxi_, c_, t_, last):
                    yield
                    layernorm(xi_[:, c_, :], xi_[:, c_, :], lng, lnb, Bxi_, Bxi_, B_ln, ltmp, B_ltmp)
                    yield
                    if last:
                        tk = t_ * NT
                        P.dma('sp', dst_d[tk:tk + NT, :].rearrange("(c p) f -> p c f", p=128), xi_[:],
                              reads=[Bxi_], writes=[Bdst_tiles[t_]])

                tails = []
                for _ in gen_pro(0):
                    pass
                for t in range(ntile_a):
                    tok0 = t * NT
                    par = t % 2
                    xi, Bxi = xin[par], B_xin[par]
                    hT, B_hT = hT2[par], B_hT2[par]
                    qk_f, B_qk = qk2[par], B_qk2[par]
                    glrT, B_glr = glr2[par], B_glr2[par]
                    gu, B_gu = gu2[par], B_gu2[par]
                    xc, B_xc = xc2[par], B_xc2[par]
                    nxt = gen_pro(t + 1) if t + 1 < ntile_a else None
                    if tok0 % seqlen == 0:
                        P.op('pool', lambda e: e.memset(gst[:], 0.0), writes=[B_gst])
                        P.op('pool', lambda e: e.memset(gstb[:], 0.0), writes=[B_gstb])
                        P.op('pool', lambda e: e.memset(sst[:], 0.0), writes=[B_sst])
                        P.op('pool', lambda e: e.memset(sstb[:], 0.0), writes=[B_sstb])

                    for c in range(NCH):
                        cs = c * 128
                        vb, B_vb = vb2[par][:, c, :], B_vb2[par]
                        gs, B_gs = gs2[par][:, c, :], B_gs2[par]
                        zs, B_zs = zs2[par][:, c, :], B_zs2[par]
                        xhat, B_xhat = xhat2[par][:, c, :], B_xhat2[par]
                        dta, B_dta = dta2[par][:, c, :], B_dta2[par]
                        la, B_la = la2[par][:, c, :], B_la2[par]
                        def gen_sgu():
                            if 'sgu' not in mixers:
                                P.op('pool', lambda e: e.memset(mixT[:, 2:4, :], 0.0), writes=[B_mixT])
                                return
                            yield
                            for g in range(4):
                                hp, fc = (g % 2) * 64, g // 2
                                P.op('pe', lambda e, g=g, hp=hp, fc=fc: e.matmul(
                                    psum[5][hp:hp + 64, 256 + fc * 128:256 + (fc + 1) * 128], lhsT=xhat[:, g * 64:(g + 1) * 64], rhs=wmT[:, g, :],
                                    start=True, stop=True), reads=[B_xhat, Bc], writes=[B5b])
                            yield
                            for fc in range(2):
                                P.op('dve', lambda e, fc=fc: e.scalar_tensor_tensor(
                                    out=sgt[:], in0=psum[5][:, 256 + fc * 128:256 + (fc + 1) * 128], scalar=SGG(fc), in1=csg[:, fc, :],
                                    op0=ALU.mult, op1=ALU.add), reads=[B5b, Bc], writes=[B_sgt])
                                P.op('dve', lambda e, fc=fc: e.tensor_tensor(out=mixT[:, 2 + fc, :], in0=sgt[:], in1=gu[:, fc, cs:cs + 128],
                                                                              op=ALU.mult), reads=[B_sgt, B_gu], writes=[B_mixT])
                        def gen_gla():
                            if 'gla' not in mixers:
                                P.op('pool', lambda e: e.memset(mixT[:, 0:2, :], 0.0), writes=[B_mixT])
                                return
                            yield
                            for pr in range(2):
                                P.op('pe', lambda e, pr=pr: e.matmul(psum[3][0:64, 128 + pr * 128:256 + pr * 128], lhsT=la[:, pr * 64:(pr + 1) * 64],
                                                                     rhs=tri_f[:], start=True, stop=True), reads=[B_la, Bc], writes=[B_ps[3]])
                            yield
                            cum3 = psum[3][0:64, 128:384].rearrange("p (a t) -> p a t", a=2)
                            yield
                            P.op('act', lambda e: e.activation(out=ecp[:], in_=cum3, func=AF.Exp, scale=-1.0 / 16.0), reads=[B_ps[3]], writes=[B_ec])
                            yield
                            P.op('act', lambda e: e.activation(out=ecn[:], in_=cum3, func=AF.Exp, scale=1.0 / 16.0), reads=[B_ps[3]], writes=[B_ec])
                            yield
                            for h in range(4):
                                P.op('dve', lambda e, h=h: e.scalar_tensor_tensor(out=qt[:, h, :], in0=qk_f[:, h // 2, cs:cs + 128], scalar=hm[:, h:h + 1],
                                                                                  in1=ecp[:, h // 2, :], op0=ALU.mult, op1=ALU.mult),
                                     reads=[B_qk, B_ec, Bc], writes=[B_qkt])
                            yield
                            P.op('dve', lambda e: e.tensor_tensor(out=kt[:], in0=qk_f[:, 2:4, cs:cs + 128], in1=ecn[:], op=ALU.mult),
                                 reads=[B_qk, B_ec], writes=[B_qkt])
                            yield
                            for pr in range(2):
                                P.op('pe', lambda e, pr=pr: e.transpose(out=pbf[6][:, 768 + pr * 64:768 + (pr + 1) * 64], in_=kt[:, pr, :],
                                                                        identity=ident_b[0:64, 0:64]), reads=[B_qkt, B_const], writes=[B6c])
                            yield
                            P.op('act', lambda e: e.copy(out=ktm[:], in_=pbf[6][:, 768:896]), reads=[B6c], writes=[B_ktm])
                            yield
                            for h in range(4):
                                pr, hb = h // 2, (h % 2) * 32
                                P.op('pe', lambda e, h=h, pr=pr, hb=hb: e.matmul(psum[4][:, h * 128:(h + 1) * 128], lhsT=kt[:, pr, :],
                                                                                 rhs=qt[:, h, :], start=True, stop=True),
                                     reads=[B_qkt], writes=[B_ps[4]])
                            yield
                            P.op('dve', lambda e: e.tensor_tensor(out=sm[:], in0=psum[4][:].rearrange("p (h t) -> p h t", h=4),
                                                                  in1=maskge_b[:].unsqueeze(1).broadcast_to([128, 4, 128]), op=ALU.mult),
                                 reads=[B_ps[4], Bc], writes=[B_sm])
                            yield
                            for h in range(4):
                                pr, hb = h // 2, (h % 2) * 32
                                P.op('pe', lambda e, h=h: e.matmul(psum[5][:, h * 64:(h + 1) * 64], lhsT=sm[:, h, :], rhs=vb[:, h * 64:(h + 1) * 64],
                                                                   start=True, stop=False), reads=[B_sm, B_vb], writes=[B5a])
                                P.op('pe', lambda e, h=h, pr=pr, hb=hb: e.matmul(psum[5][:, h * 64:(h + 1) * 64], lhsT=qt[:, h, :],
                                                                                 rhs=gstb[:, pr, (h % 2) * 64:(h % 2 + 1) * 64], start=False, stop=True),
                                     reads=[B_qkt, B_gstb], writes=[B5a])
                            yield
                            for pr in range(2):
                                P.op('pe', lambda e, pr=pr: e.matmul(psum[3][0:64, 128 + pr * 128:256 + pr * 128],
                                                                     lhsT=ktm[:, pr * 64:(pr + 1) * 64], rhs=vb[:, pr * 128:(pr + 1) * 128],
                                                                     start=True, stop=True), reads=[B_ktm, B_vb], writes=[B_ps[3]])
                            yield
                            P.op('dve', lambda e: e.tensor_tensor(out=gstt[:], in0=psum[3][0:64, 128:384].rearrange("p (a v) -> p a v", a=2),
                                                                  in1=gst[:], op=ALU.add), reads=[B_ps[3], B_gst], writes=[B_gst])
                            yield
                            for pr in range(2):
                                P.op('act', lambda e, pr=pr: e.activation(out=gst[:, pr, :], in_=gstt[:, pr, :], func=AF.Identity,
                                                                          scale=ecp[:, pr, 127:128]), reads=[B_gst, B_ec], writes=[B_gst])
                            yield
                            P.op('dve', lambda e: e.tensor_copy(out=gstb[:], in_=gst[:]), reads=[B_gst], writes=[B_gstb])
                            yield
                            P.op('act', lambda e: e.activation(out=osq[:, 0:256], in_=psum[5][:, 0:256], func=AF.Square), reads=[B5a], writes=[B_osq])
                            yield
                            P.op('dve', lambda e: e.tensor_reduce(out=gsc[:, 0:4], in_=osq[:, 0:256].rearrange("p (h v) -> p h v", h=4),
                                                                  axis=AX.X, op=ALU.add), reads=[B_osq], writes=[B_gsc])
                            yield
                            P.op('act', lambda e: e.activation(out=gsc[:, 4:8], in_=gsc[:, 0:4], func=AF.Ln, bias=eps_t[:], scale=1.0 / 64.0),
                                 reads=[B_gsc, B_const], writes=[B_gsc])
                            yield
                            P.op('act', lambda e: e.activation(out=gsc[:, 8:12], in_=gsc[:, 4:8], func=AF.Exp, scale=-0.5), reads=[B_gsc], writes=[B_gsc])
                            yield
                            P.op('dve', lambda e: e.tensor_tensor(out=og[:].rearrange("p (h v) -> p h v", h=4),
                                                                  in0=psum[5][:, 0:256].rearrange("p (h v) -> p h v", h=4),
                                                                  in1=gsc[:, 8:12].unsqueeze(2).broadcast_to([128, 4, 64]), op=ALU.mult),
                                 reads=[B5a, B_gsc], writes=[B_og])
                            yield
                            P.op('dve', lambda e: e.tensor_tensor(out=ogl[:], in0=og[:], in1=gs[:], op=ALU.mult), reads=[B_og, B_gs], writes=[B_ogl], cost=0.35)
                            yield
                            for q in range(2):
                                P.op('pe', lambda e, q=q: e.transpose(out=pbf[3][:, 768 + q * 128:768 + (q + 1) * 128], in_=ogl[:, q * 128:(q + 1) * 128],
                                                                      identity=ident_b[:]), reads=[B_ogl, B_const], writes=[B_ps[3]])
                            yield
                            P.op('act', lambda e: e.copy(out=mixT[:, 0:2, :], in_=pbf[3][:, 768:1024].rearrange("p (q t) -> p q t", q=2)),
                                 reads=[B_ps[3]], writes=[B_mixT])
                        def gen_ssd():
                            if 'ssd' not in mixers:
                                P.op('pool', lambda e: e.memset(mixT[:, 4:8, :], 0.0), writes=[B_mixT])
                                return
                            yield
                            for q in range(4):
                                P.op('pe', lambda e, q=q: e.transpose(out=pbf[6][:, q * 128:(q + 1) * 128], in_=xc[:, q, cs:cs + 128], identity=ident_b[:]),
                                     reads=[B_xc, B_const], writes=[B6a])
                            yield
                            for q in range(2):
                                P.op('pe', lambda e, q=q: e.transpose(out=pbf[6][:, 512 + q * 128:512 + (q + 1) * 128], in_=xc[:, 4 + q, cs:cs + 128],
                                                                      identity=ident_b[:]), reads=[B_xc, B_const], writes=[B6a])
                            yield
                            P.op('pe', lambda e: e.matmul(psum[2][:, 264:272], lhsT=tri_f[:], rhs=dta[:, 8:16], start=True, stop=True),
                                 reads=[B_dts, B_dta, Bc], writes=[B_ps[2]])
                            yield
                            P.op('pe', lambda e: e.matmul(psum[2][:, 272:280], lhsT=ones_f[:], rhs=dta[:, 8:16], start=True, stop=True),
                                 reads=[B_dts, B_dta, Bc], writes=[B_ps[2]])
                            yield
                            P.op('act', lambda e: e.copy(out=dts[:, 16:32], in_=psum[2][:, 264:280]), reads=[B_ps[2]], writes=[B_dts])
                            yield
                            P.op('act', lambda e: e.activation(out=dts[:, 32:48], in_=dts[:, 16:32], func=AF.Exp), reads=[B_dts, B_dta], writes=[B_dts])
                            yield
                            P.op('dve', lambda e: e.tensor_tensor(out=dts[:, 48:56], in0=dts[:, 24:32], in1=dts[:, 16:24], op=ALU.subtract),
                                 reads=[B_dts, B_dta], writes=[B_dts])
                            yield
                            P.op('act', lambda e: e.activation(out=dts[:, 48:56], in_=dts[:, 48:56], func=AF.Exp), reads=[B_dts, B_dta], writes=[B_dts])
                            yield
                            xsT3 = pbf[6][:, 0:512].rearrange("p (h v) -> p h v", h=8)
                            yield
                            P.op('dve', lambda e: e.tensor_tensor(out=xdt[:].rearrange("p (h v) -> p h v", h=8), in0=xsT3,
                                                                  in1=dta[:, 0:8].unsqueeze(2).broadcast_to([128, 8, 64]), op=ALU.mult),
                                 reads=[B6a, B_dts, B_dta], writes=[B_xdt])
                            yield
                            P.op('dve', lambda e: e.tensor_tensor(out=xsd[:].rearrange("p (h v) -> p h v", h=8), in0=xsT3,
                                                                  in1=dsk_bc[:].unsqueeze(2).broadcast_to([128, 8, 64]), op=ALU.mult),
                                 reads=[B6a, B_bc], writes=[B_xsd])
                            yield
                            P.op('dve', lambda e: e.tensor_tensor(out=xdd[:].rearrange("p (h v) -> p h v", h=8),
                                                                   in0=xdt[:].rearrange("p (h v) -> p h v", h=8),
                                                                   in1=dts[:, 48:56].unsqueeze(2).broadcast_to([128, 8, 64]), op=ALU.mult),
                                 reads=[B_xdt, B_dts, B_dta], writes=[B_xdd])
                            yield
                            P.op('act', lambda e: e.copy(out=bmtm[:], in_=pbf[6][:, 512:768]), reads=[B6a], writes=[B_bmtm])
                            yield
                            P.op('dve', lambda e: e.tensor_tensor(out=ldec[:], in0=maskgt_f[:].unsqueeze(1).broadcast_to([128, 8, 128]),
                                                                  in1=dta[:, 8:16].unsqueeze(2).broadcast_to([128, 8, 128]), op=ALU.mult),
                                 reads=[Bc, B_dts, B_dta], writes=[B_ldec])
                            yield
                            for h in range(8):
                                bk = h // 4
                                P.op('pe', lambda e, h=h, bk=bk: e.matmul(psum[bk][:, (h % 4) * 128:(h % 4 + 1) * 128], lhsT=ldec[:, h, :], rhs=tri_f[:],
                                                                          start=True, stop=True), reads=[B_ldec, Bc], writes=[B_ps[bk]])
                            yield
                            for bk in range(2):
                                P.op('act', lambda e, bk=bk: e.activation(out=dm[:, bk * 4:(bk + 1) * 4, :],
                                                                          in_=psum[bk][:].rearrange("p (h t) -> p h t", h=4), func=AF.Exp),
                                     reads=[B_ps[bk]], writes=[B_dm])
                            yield
                            for g in range(2):
                                P.op('pe', lambda e, g=g: e.matmul(psum[2][:, g * 128:(g + 1) * 128], lhsT=xc[:, 4 + g, cs:cs + 128],
                                                                   rhs=xc[:, 6 + g, cs:cs + 128], start=True, stop=True), reads=[B_xc], writes=[B_ps[2]])
                            yield
                            P.op('dve', lambda e: e.tensor_tensor(out=cbm[:], in0=psum[2][:, 0:256].rearrange("p (g t) -> p g t", g=2),
                                                                  in1=maskge_b[:].unsqueeze(1).broadcast_to([128, 2, 128]), op=ALU.mult),
                                 reads=[B_ps[2], Bc], writes=[B_cbm])
                            yield
                            for g in range(2):
                                P.op('dve', lambda e, g=g: e.tensor_tensor(out=mm_[:, g * 4:(g + 1) * 4, :], in0=dm[:, g * 4:(g + 1) * 4, :],
                                                                            in1=cbm[:, g, :].unsqueeze(1).broadcast_to([128, 4, 128]), op=ALU.mult),
                                     reads=[B_dm, B_cbm], writes=[B_mm])
                            yield
                            for h in range(8):
                                P.op('pe', lambda e, h=h: e.matmul(psum[0][:, h * 64:(h + 1) * 64], lhsT=mm_[:, h, :], rhs=xdt[:, h * 64:(h + 1) * 64],
                                                                   start=True, stop=True), reads=[B_mm, B_xdt], writes=[B_ps[0]])
                            yield
                            for g in range(2):
                                P.op('pe', lambda e, g=g: e.matmul(psum[1][:, g * 256:(g + 1) * 256], lhsT=xc[:, 6 + g, cs:cs + 128],
                                                                   rhs=sstb[:, g * 256:(g + 1) * 256], start=True, stop=True),
                                     reads=[B_xc, B_sstb], writes=[B_ps[1]])
                            yield
                            P.op('dve', lambda e: e.tensor_tensor(out=y1[:].rearrange("p (h v) -> p h v", h=8),
                                                                  in0=psum[1][:].rearrange("p (h v) -> p h v", h=8),
                                                                  in1=dts[:, 32:40].unsqueeze(2).broadcast_to([128, 8, 64]), op=ALU.mult),
                                 reads=[B_ps[1], B_dts, B_dta], writes=[B_y1])
                            yield
                            P.op('dve', lambda e: e.tensor_tensor(out=y1[:], in0=y1[:], in1=psum[0][:], op=ALU.add), reads=[B_y1, B_ps[0]], writes=[B_y1])
                            yield
                            P.op('dve', lambda e: e.tensor_tensor(out=y1[:], in0=y1[:], in1=xsd[:], op=ALU.add), reads=[B_y1, B_xsd], writes=[B_y1])
                            yield
                            P.op('dve', lambda e: e.tensor_tensor(out=y1[:], in0=y1[:], in1=zs[:], op=ALU.mult), reads=[B_y1, B_zs], writes=[B_y1])
                            yield
                            for g in range(2):
                                P.op('pe', lambda e, g=g: e.matmul(psum[2][:, g * 256:(g + 1) * 256], lhsT=bmtm[:, g * 128:(g + 1) * 128],
                                                                   rhs=xdd[:, g * 256:(g + 1) * 256], start=True, stop=True),
                                     reads=[B_bmtm, B_xdd], writes=[B_ps[2]])
                            yield
                            P.op('pool', lambda e: e.tensor_tensor(out=sst[:].rearrange("p (h v) -> p h v", h=8),
                                                                   in0=sst[:].rearrange("p (h v) -> p h v", h=8),
                                                                   in1=dts[:, 40:48].unsqueeze(2).broadcast_to([128, 8, 64]), op=ALU.mult),
                                 reads=[B_sst, B_dts, B_dta], writes=[B_sst])
                            yield
                            P.op('dve', lambda e: e.tensor_tensor(out=sst[:], in0=sst[:], in1=psum[2][:], op=ALU.add), reads=[B_sst, B_ps[2]], writes=[B_sst])
                            yield
                            P.op('act', lambda e: e.copy(out=sstb[:], in_=sst[:]), reads=[B_sst], writes=[B_sstb])
                            yield
                            P.op('act', lambda e: e.activation(out=osq2[:], in_=y1[:], func=AF.Square), reads=[B_y1], writes=[B_osq2])
                            yield
                            P.op('dve', lambda e: e.tensor_reduce(out=gsc[:, 12:14], in_=osq2[:].rearrange("p (g v) -> p g v", g=2), axis=AX.X, op=ALU.add),
                                 reads=[B_osq2], writes=[B_gsc])
                            yield
                            P.op('act', lambda e: e.activation(out=gsc[:, 14:16], in_=gsc[:, 12:14], func=AF.Ln, bias=eps_t[:], scale=1.0 / 256.0),
                                 reads=[B_gsc, B_const], writes=[B_gsc])
                            yield
                            P.op('act', lambda e: e.activation(out=gsc[:, 12:14], in_=gsc[:, 14:16], func=AF.Exp, scale=-0.5), reads=[B_gsc], writes=[B_gsc])
                            yield
                            P.op('dve', lambda e: e.tensor_tensor(out=y1[:].rearrange("p (g v) -> p g v", g=2),
                                                                  in0=y1[:].rearrange("p (g v) -> p g v", g=2),
                                                                  in1=gsc[:, 12:14].unsqueeze(2).broadcast_to([128, 2, 256]), op=ALU.mult),
                                 reads=[B_y1, B_gsc], writes=[B_y1])
                            yield
                            P.op('pool', lambda e: e.tensor_tensor(out=yb[:], in0=y1[:], in1=sng_bc[:], op=ALU.mult), reads=[B_y1, B_bc], writes=[B_yb])
                            yield
                            for q in range(4):
                                P.op('pe', lambda e, q=q: e.transpose(out=pbf[6][:, q * 128:(q + 1) * 128], in_=yb[:, q * 128:(q + 1) * 128],
                                                                      identity=ident_b[:]), reads=[B_yb, B_const], writes=[B6a])
                            yield
                            P.op('act', lambda e: e.copy(out=mixT[:, 4:8, :], in_=pbf[6][:, 0:512].rearrange("p (q t) -> p q t", q=4)),
                                 reads=[B6a], writes=[B_mixT])
                        g_ssd = gen_ssd()
                        pend = list(tails)
                        gens = [g_ssd, gen_gla(), g_ssd, gen_sgu()] + tails
                        tails = []
                        while gens:
                            for g_ in list(gens):
                                if g_ not in gens:
                                    continue
                                try:
                                    next(g_)
                                except StopIteration:
                                    while g_ in gens:
                                        gens.remove(g_)
                            for _rep in range(2):
                                if nxt is not None and not any(p_ in gens for p_ in pend):
                                    try:
                                        next(nxt)
                                    except StopIteration:
                                        nxt = None
                        for hf in range(2):
                            for fc in range(8):
                                P.op('pe', lambda e, hf=hf, fc=fc: e.matmul(psum[hf][:], lhsT=mixT[:, fc, :], rhs=w_out[:, fc, hf * 512:(hf + 1) * 512],
                                                                            start=(fc == 0), stop=(fc == 7)), reads=[B_mixT, B_wout], writes=[B_ps[hf]], cost=0.39)
                        for hf in range(2):
                            P.op('dve', lambda e, hf=hf: e.scalar_tensor_tensor(
                                out=xi[:, c, hf * 512:(hf + 1) * 512], in0=xi[:, c, hf * 512:(hf + 1) * 512], scalar=ALPHA,
                                in1=psum[hf][:], op0=ALU.mult, op1=ALU.add), reads=[Bxi, B_ps[hf]], writes=[Bxi])
                        tails = [gen_ln1(xi, Bxi, c, t, c == NCH - 1)]
                    if nxt is not None:
                        for _ in nxt:
                            pass
                for g_ in tails:
                    for _ in g_:
                        pass

        B_x = [Buf(f"x{i}") for i in range(ntile)]
        cur_d, cur_B = x_d, B_x
        first = True
        for pi, (kind, layer) in enumerate(phases):
            last = (pi == len(phases) - 1)
            if kind == 'B':
                dst_d, dst_B = (y_d, B_y) if last else (hB_d, B_hB)
                phase_b(layer, cur_d, cur_B, dst_d, dst_B, pre_ln=first)
            else:
                dst_d, dst_B = (y_d, B_y) if last else (hA_d, B_hA)
                phase_a(layer, cur_d, cur_B, dst_d, dst_B, pre_ln=first)
            P.barrier()
            cur_d, cur_B = dst_d, dst_B
            first = False
        P.wait_all('sp', B_y)
        print(f"[build] instructions={P.nins} waits={P.nwaits} sems={len(P.sems)}")
    return nc


_W_NAMES = ["ln_in_g", "ln_in_b", "w_in", "gla_w_gate", "gla_b_gate", "gla_norm_g", "sgu_norm_g", "sgu_norm_b",
            "sgu_w", "sgu_b", "ssd_conv_w", "ssd_conv_b", "ssd_dt_bias", "ssd_a_log", "ssd_d", "ssd_norm_g",
            "w_out", "ln1_g", "ln1_b", "ffn_w_up", "ffn_conv_w", "ffn_conv_b", "ffn_w_down", "ln2_g", "ln2_b"]


def run(inputs, ntok, seqlen, phases=None, ncores=NCORES, **kw):
    nc = build_program(ntok=ntok, seqlen=seqlen, phases=phases, **kw)
    x = np.ascontiguousarray(np.asarray(inputs["x"], dtype=np.float32)).reshape(-1, D)
    assert x.shape[0] == ntok * ncores
    wmap = {k: np.ascontiguousarray(np.asarray(inputs[k], dtype=np.float32)) for k in _W_NAMES}
    in_maps = []
    for c in range(ncores):
        m = dict(wmap)
        m["x"] = x[c * ntok:(c + 1) * ntok]
        in_maps.append(m)
    res = run_bass_kernel_spmd(nc, in_maps, core_ids=list(range(ncores)))
    return np.concatenate([r["y"] for r in res.results], axis=0)


def kernel(**inputs):
    x = inputs["x"]
    B, S, _ = x.shape
    y = run(inputs, ntok=(B * S) // NCORES, seqlen=S)
    return y.reshape(B, S, D).astype(np.float32)
```

```python
import types
import numpy as np
from contextlib import ExitStack
import concourse.bass as bass
import concourse.mybir as mybir
from concourse.bass_utils import run_bass_kernel_spmd

F32 = mybir.dt.float32
BF16 = mybir.dt.bfloat16
AF = mybir.ActivationFunctionType
ALU = mybir.AluOpType
AX = mybir.AxisListType

D = 1024
DEPTH = 2
DIN = 2840
DFF = 2816
NJ = DFF // 128
ALPHA = float((2 * DEPTH) ** 0.25)
EPS = 1e-5
NCORES = 8

O_Q, O_K, O_V, O_G, O_GLR, O_SU, O_SV, O_Z, O_XBC, O_DT = 0, 128, 256, 512, 768, 784, 1040, 1296, 1808, 2832


class Buf:
    def __init__(self, name, psum=False):
        self.name = name
        self.psum = psum
        self.writer = None
        self.readers = {}
        self.aliases = []
        self.dcount = 0


class Prog:
    def __init__(self, nc, stack):
        self.nc = nc
        self.stack = stack
        self.eng = {'pe': nc.tensor, 'act': nc.scalar, 'dve': nc.vector, 'pool': nc.gpsimd, 'sp': nc.sync}
        self.sems = {}
        self.cnt = {}
        self.seen = {e: {} for e in self.eng}
        for e in ('pe', 'act', 'dve', 'pool'):
            self.sems[e] = stack.enter_context(nc.semaphore("s_" + e))
            self.cnt[e] = 0
        self.nwaits = 0
        self.nins = 0
        self.dtot = {}
        self.nodes = []

    def _sem(self, key):
        if key not in self.sems:
            self.sems[key] = self.stack.enter_context(self.nc.semaphore("d_" + key))
        return self.sems[key]

    def _deps(self, eng, reads, writes):
        raw = {}
        oth = {}

        def add(d, ev):
            if ev is None:
                return
            k, v = ev
            if d.get(k, 0) < v:
                d[k] = v
        for b in reads:
            add(raw, b.writer)
            for a in b.aliases:
                add(raw, a.writer)
            if b.psum:
                for k, v in b.readers.items():
                    add(oth, (k, v))
        for b in writes:
            for bb in [b] + b.aliases:
                add(oth, bb.writer)
                for k, v in bb.readers.items():
                    add(oth, (k, v))
        need = {}
        for k, v in raw.items():
            if k == eng:
                if eng == 'pe':
                    continue
                if eng != 'pool' and self.cnt[eng] - v >= 8:
                    continue
            if need.get(k, 0) < v:
                need[k] = v
        for k, v in oth.items():
            if k == eng:
                continue
            if need.get(k, 0) < v:
                need[k] = v
        e = self.eng[eng]
        for k, v in need.items():
            if self.seen[eng].get(k, 0) < v:
                e.wait_ge(self._sem(k), v)
                self.seen[eng][k] = v
                self.nwaits += 1

    def _emit_op(self, eng, fn, reads=(), writes=()):
        self._deps(eng, reads, writes)
        ins = fn(self.eng[eng])
        self.cnt[eng] += 1
        ins.then_inc(self.sems[eng], 1)
        ev = (eng, self.cnt[eng])
        for b in reads:
            if b.readers.get(eng, 0) < ev[1]:
                b.readers[eng] = ev[1]
        for b in writes:
            b.writer = ev
            b.readers = {}
        self.nins += 1
        return ins

    def _emit_dma(self, eng, out, in_, reads, writes, **kw):
        self._deps(eng, reads, writes)
        dst = writes[0]
        key = "dma_" + dst.name
        sem = self._sem(key)
        ins = self.eng[eng].dma_start(out=out, in_=in_, **kw)
        cnt = self.dtot.get(key, 0) + 16
        self.dtot[key] = cnt
        ins.then_inc(sem, 16)
        ev = (key, cnt)
        for b in reads:
            if b.readers.get(key, 0) < ev[1]:
                b.readers[key] = ev[1]
        for b in writes:
            b.writer = ev
            b.readers = {}
        return ins

    COST = {'pe': 0.13, 'act': 0.5, 'dve': 0.5, 'pool': 1.05, 'sp': 0.1}
    WINDOW = 600
    SLACK = 0.4

    @staticmethod
    def _freeze(fn):
        if not fn.__closure__:
            return fn
        cells = []
        for c in fn.__closure__:
            try:
                cells.append(types.CellType(c.cell_contents))
            except ValueError:
                cells.append(c)
        return types.FunctionType(fn.__code__, fn.__globals__, fn.__name__, fn.__defaults__, tuple(cells))

    def op(self, eng, fn, reads=(), writes=(), cost=None):
        tbl = None
        if eng == 'act':
            names = fn.__code__.co_names
            if 'Silu' in names:
                tbl = 'silu'
            elif 'Sigmoid' in names:
                tbl = 'sigmoid'
            elif 'Exp' in names or 'Ln' in names:
                tbl = 'exp'
        self.nodes.append(('op', eng, self._freeze(fn), list(reads), list(writes), tbl, cost))

    def dma(self, eng, out, in_, reads, writes, **kw):
        self.nodes.append(('dma', eng, (out, in_), list(reads), list(writes), kw, None))

    def flush(self):
        nodes = self.nodes
        self.nodes = []
        n = len(nodes)
        if n == 0:
            return
        lastw = {}
        readers = {}
        deps = [None] * n
        succ = [[] for _ in range(n)]
        for i, (kind, eng, fn, reads, writes, kw, cost) in enumerate(nodes):
            d = set()
            for b in reads:
                for bb in [b] + b.aliases:
                    w = lastw.get(id(bb))
                    if w is not None:
                        d.add(w)
            for b in writes:
                for bb in [b] + b.aliases:
                    w = lastw.get(id(bb))
                    if w is not None:
                        d.add(w)
                    for r in readers.get(id(bb), ()):
                        d.add(r)
            d.discard(i)
            deps[i] = d
            for j in d:
                succ[j].append(i)
            for b in reads:
                readers.setdefault(id(b), []).append(i)
            for b in writes:
                lastw[id(b)] = i
                readers[id(b)] = []
        ndep = [len(d) for d in deps]
        lp = [0.0] * n
        for i in range(n - 1, -1, -1):
            kind_, eng_, _, _, _, _, cost_ = nodes[i]
            c_ = (cost_ if cost_ is not None else self.COST[eng_]) if kind_ == 'op' else 3.0
            m_ = 0.0
            for k in succ[i]:
                if lp[k] > m_:
                    m_ = lp[k]
            lp[i] = c_ + m_ + 0.35
        finish = [0.0] * n
        etime = {e: 0.0 for e in self.eng}
        ready = [i for i in range(n) if ndep[i] == 0]
        cur_tbl = None
        done = [False] * n
        lo = 0
        nsched = 0
        while nsched < n:
            while lo < n and done[lo]:
                lo += 1
            cands = []
            tmin = None
            for i in ready:
                if i > lo + self.WINDOW:
                    continue
                eng = nodes[i][1]
                st = etime[eng]
                for j in deps[i]:
                    f = finish[j] + (0.0 if nodes[j][1] == eng else 0.35)
                    if f > st:
                        st = f
                if eng == 'act' and nodes[i][0] == 'op' and nodes[i][5] is not None and nodes[i][5] != cur_tbl:
                    st += 1.3
                cands.append((st, i))
                if tmin is None or st < tmin:
                    tmin = st
            best, bkey = None, None
            for (st, i) in cands:
                if st <= tmin + self.SLACK:
                    key = (-lp[i], i)
                    if bkey is None or key < bkey:
                        best, bkey, bst = i, key, st
            i = best
            kind, eng, fn, reads, writes, kw, cost = nodes[i]
            st = bst
            if kind == 'op':
                c = cost if cost is not None else self.COST[eng]
                if eng == 'act' and kw is not None:
                    if kw != cur_tbl:
                        c += 1.3
                    cur_tbl = kw
                self._emit_op(eng, fn, reads, writes)
                etime[eng] = st + c
                finish[i] = st + c
            else:
                self._emit_dma(eng, fn[0], fn[1], reads, writes, **kw)
                etime[eng] = st + 0.1
                finish[i] = st + 3.0
            done[i] = True
            nsched += 1
            ready.remove(i)
            for k in succ[i]:
                ndep[k] -= 1
                if ndep[k] == 0:
                    ready.append(k)

    def barrier(self):
        self.flush()
        tot = dict(self.cnt)
        tot.update(self.dtot)
        for eng in self.eng:
            e = self.eng[eng]
            for k, v in tot.items():
                if v == 0:
                    continue
                if self.seen[eng].get(k, 0) < v:
                    e.wait_ge(self._sem(k), v)
                    self.seen[eng][k] = v
                    self.nwaits += 1

    def wait_all(self, eng, bufs):
        self.flush()
        self._deps(eng, bufs, ())


def build_program(ntok=4096, seqlen=2048, phases=None, dbg=False, mixers=('gla', 'sgu', 'ssd')):
    if phases is None:
        phases = []
        for l in range(DEPTH):
            phases += [('A', l), ('B', l)]
    nc = bass.Bass("TRN2", target_bir_lowering=False)
    dt_in = lambda name, shape: nc.dram_tensor(name, shape, F32, kind="ExternalInput").ap()
    x_d = dt_in("x", [ntok, D])
    ln_in_g = dt_in("ln_in_g", [D])
    ln_in_b = dt_in("ln_in_b", [D])
    w_in_d = dt_in("w_in", [DEPTH, D, DIN])
    gla_w_gate = dt_in("gla_w_gate", [DEPTH, 16, 128])
    gla_b_gate = dt_in("gla_b_gate", [DEPTH, 128])
    gla_norm_g = dt_in("gla_norm_g", [DEPTH, 64])
    sgu_norm_g = dt_in("sgu_norm_g", [DEPTH, 256])
    sgu_norm_b = dt_in("sgu_norm_b", [DEPTH, 256])
    sgu_w = dt_in("sgu_w", [DEPTH, 4, 128, 128])
    sgu_b = dt_in("sgu_b", [DEPTH, 4, 128])
    ssd_conv_w = dt_in("ssd_conv_w", [DEPTH, 4, 1024])
    ssd_conv_b = dt_in("ssd_conv_b", [DEPTH, 1024])
    ssd_dt_bias = dt_in("ssd_dt_bias", [DEPTH, 8])
    ssd_a_log = dt_in("ssd_a_log", [DEPTH, 8])
    ssd_d = dt_in("ssd_d", [DEPTH, 8])
    ssd_norm_g = dt_in("ssd_norm_g", [DEPTH, 512])
    w_out_d = dt_in("w_out", [DEPTH, D, D])
    ln1_g = dt_in("ln1_g", [DEPTH, D])
    ln1_b = dt_in("ln1_b", [DEPTH, D])
    w_up_d = dt_in("ffn_w_up", [DEPTH, D, 2 * DFF])
    ffn_conv_w = dt_in("ffn_conv_w", [DEPTH, 3, 2 * DFF])
    ffn_conv_b = dt_in("ffn_conv_b", [DEPTH, 2 * DFF])
    w_down_d = dt_in("ffn_w_down", [DEPTH, DFF, D])
    ln2_g = dt_in("ln2_g", [DEPTH, D])
    ln2_b = dt_in("ln2_b", [DEPTH, D])
    y_d = nc.dram_tensor("y", [ntok, D], F32, kind="ExternalOutput").ap()
    hA_d = nc.dram_tensor("hA", [ntok, D], F32, kind="Internal").ap()
    hB_d = nc.dram_tensor("hB", [ntok, D], F32, kind="Internal").ap()

    stack = ExitStack()
    with stack:
        P = Prog(nc, stack)
        sb = lambda name, shape, dt=F32: stack.enter_context(nc.sbuf_tensor(name, shape, dt))

        B_wdown, B_wup, B_wout, B_win = Buf("wdown"), Buf("wup"), Buf("wout"), Buf("win")

        ident_f = sb("ident_f", [128, 128], F32)
        ident_b = sb("ident_b", [128, 128], BF16)
        B_const = Buf("const")
        lng = sb("lng", [128, D], F32)
        lnb = sb("lnb", [128, D], F32)
        B_ln = Buf("ln")
        B_ln0 = Buf("ln0")
        eps_t = sb("eps_t", [128, 1], F32)

        psum = [stack.enter_context(nc.psum_tensor(f"ps{i}", [128, 512], F32)) for i in range(8)]
        B_ps = [Buf(f"ps{i}", psum=True) for i in range(8)]

        P.op('pool', lambda e: e.memset(ident_f[:], 1.0), writes=[B_const])
        P.op('pool', lambda e: e.affine_select(out=ident_f[:], in_=ident_f[:], pattern=[[-1, 128]],
                                               compare_op=ALU.is_equal, fill=0.0, base=0, channel_multiplier=1),
             reads=[B_const], writes=[B_const])
        P.op('pool', lambda e: e.tensor_copy(out=ident_b[:], in_=ident_f[:]), reads=[B_const], writes=[B_const])
        P.op('pool', lambda e: e.memset(eps_t[:], EPS), writes=[B_const])

        def layernorm(src, dst, g_t, b_t, Bsrc, Bdst, Bg, tmp, Btmp):
            st, mv, sc = tmp
            for hh in range(2):
                P.op('dve', lambda e, hh=hh: e.bn_stats(out=st[:, hh, :], in_=src[:, hh * 512:(hh + 1) * 512]),
                     reads=[Bsrc], writes=[Btmp])
            P.op('dve', lambda e: e.bn_aggr(out=mv[:], in_=st[:].rearrange("p a b -> p (a b)")), reads=[Btmp], writes=[Btmp])
            P.op('act', lambda e: e.activation(out=sc[:, 0:1], in_=mv[:, 1:2], func=AF.Ln, bias=eps_t[:], scale=1.0),
                 reads=[Btmp, B_const], writes=[Btmp])
            P.op('act', lambda e: e.activation(out=sc[:, 1:2], in_=sc[:, 0:1], func=AF.Exp, scale=-0.5),
                 reads=[Btmp], writes=[Btmp])
            P.op('dve', lambda e: e.tensor_scalar(out=sc[:, 2:3], in0=mv[:, 0:1], scalar1=sc[:, 1:2], scalar2=-1.0,
                                                  op0=ALU.mult, op1=ALU.mult), reads=[Btmp], writes=[Btmp])
            P.op('act', lambda e: e.activation(out=dst, in_=src, func=AF.Identity, bias=sc[:, 2:3], scale=sc[:, 1:2]),
                 reads=[Bsrc, Btmp], writes=[Bdst], cost=0.95)
            P.op('dve', lambda e: e.tensor_tensor(out=dst, in0=dst, in1=g_t[:], op=ALU.mult), reads=[Bdst, Bg], writes=[Bdst], cost=1.15)
            P.op('pool', lambda e: e.tensor_tensor(out=dst, in0=dst, in1=b_t[:], op=ALU.add), reads=[Bdst, Bg], writes=[Bdst], cost=2.0)

        def load_ln_consts(g_row, b_row, g_t, b_t, Bg):
            P.dma('sp', g_t[:], g_row.partition_broadcast(128), reads=[], writes=[Bg])
            P.dma('sp', b_t[:], b_row.partition_broadcast(128), reads=[], writes=[Bg])

        def load_weight(dst3, src2, nk, Bw, rows_per=128):
            N = src2.shape[1]
            for k in range(nk):
                c0 = 0
                while c0 < N:
                    c1 = min(N, c0 + 2048)
                    P.dma('pool', dst3[:, k, c0:c1], src2[k * 128:(k + 1) * 128, c0:c1], reads=[], writes=[Bw])
                    c0 = c1

        TB = 256
        ntile = ntok // TB
        B_hA = [Buf(f"hA{i}") for i in range(ntile)]
        B_hB = [Buf(f"hB{i}") for i in range(ntile)]
        B_y = [Buf(f"y{i}") for i in range(ntile)]

        def phase_b(layer, src_d, Bsrc_tiles, dst_d, Bdst_tiles, pre_ln):
            bstack = ExitStack()
            with bstack:
                sbb = lambda name, shape, dt=F32: bstack.enter_context(nc.sbuf_tensor(f"b{layer}_{name}", shape, dt))
                w_up = sbb("w_up", [128, 8, 2 * DFF], BF16)
                w_down = sbb("w_down", [128, NJ, D], BF16)
                NG = 4
                JG = 6
                B_wupg = [Buf(f"wup{g}") for g in range(NG)]
                B_wdng = [Buf(f"wdn{g}") for g in range(NG)]
                for g in range(NG):
                    j0, j1 = g * JG, min(NJ, (g + 1) * JG)
                    for a in range(2):
                        c0, c1 = a * DFF + j0 * 128, a * DFF + j1 * 128
                        for k in range(8):
                            P.dma('pool', w_up[:, k, c0:c1], w_up_d[layer][k * 128:(k + 1) * 128, c0:c1], reads=[], writes=[B_wupg[g]])
                    for j in range(j0, j1):
                        P.dma('pool', w_down[:, j, :], w_down_d[layer][j * 128:(j + 1) * 128, :], reads=[], writes=[B_wdng[g]])
                load_ln_consts(ln2_g[layer], ln2_b[layer], lng, lnb, B_ln)
                if pre_ln:
                    lng0 = sbb("lng0", [128, D], F32)
                    lnb0 = sbb("lnb0", [128, D], F32)
                    load_ln_consts(ln_in_g, ln_in_b, lng0, lnb0, B_ln0)
                cwraw = sbb("cwraw", [44, 4, 128])
                cw = sbb("cw", [128, 4, 44])
                B_cwraw, B_cw = Buf("cwraw"), Buf("cw")
                for k in range(3):
                    P.dma('sp', cwraw[:, k, :], ffn_conv_w[layer, k].rearrange("(c p) -> c p", p=128), reads=[], writes=[B_cwraw])
                P.dma('sp', cwraw[:, 3, :], ffn_conv_b[layer].rearrange("(c p) -> c p", p=128), reads=[], writes=[B_cwraw])
                for k in range(4):
                    P.op('pe', lambda e, k=k: e.transpose(out=psum[6][:, 0:44], in_=cwraw[:, k, :], identity=ident_f[0:44, 0:44]),
                         reads=[B_cwraw, B_const], writes=[B_ps[6]])
                    P.op('dve', lambda e, k=k: e.tensor_copy(out=cw[:, k, :], in_=psum[6][:, 0:44]), reads=[B_ps[6]], writes=[B_cw])

                xin = [sbb(f"xin{i}", [128, 2, D]) for i in range(2)]
                B_xin = [Buf(f"xin{i}") for i in range(2)]
                xbf = sbb("xbf", [128, 2, D], BF16)
                B_xbf = Buf("xbf")
                hT = [sbb(f"hT{i}", [128, 8, TB], BF16) for i in range(2)]
                B_hT = [Buf(f"hT{i}") for i in range(2)]
                halo = sbb("halo", [128, NJ, 2, 2])
                B_halo = Buf("halo")
                NP = 3
                xs = [sbb(f"xs{i}", [128, 2, 2 + TB]) for i in range(NP)]
                B_xs = [Buf(f"xs{i}") for i in range(NP)]
                cg = [sbb(f"cg{i}", [128, TB]) for i in range(NP)]
                cv = [sbb(f"cv{i}", [128, TB]) for i in range(NP)]
                B_cg = [Buf(f"cg{i}") for i in range(NP)]
                B_cv = [Buf(f"cv{i}") for i in range(NP)]
                sg = [sbb(f"sg{i}", [128, TB]) for i in range(NP)]
                B_sg = [Buf(f"sg{i}") for i in range(NP)]
                NA = 4
                aT = [sbb(f"aT{i}", [128, TB], BF16) for i in range(NA)]
                B_aT = [Buf(f"aT{i}") for i in range(NA)]
                rr = [sbb(f"rr{i}", [128, D]) for i in range(2)]
                B_rr = [Buf(f"rr{i}") for i in range(2)]
                lst = sbb("lst", [128, 2, 6]); lmv = sbb("lmv", [128, 2]); lsc = sbb("lsc", [128, 4])
                B_ltmp = Buf("ltmp")
                ltmp = (lst, lmv, lsc)
                pT = psum[7][:].bitcast(BF16)

                def load(t):
                    tok0 = t * TB
                    xi, Bxi = xin[t % 2], B_xin[t % 2]
                    P.dma('sp', xi[:], src_d[tok0:tok0 + TB, :].rearrange("(c p) f -> p c f", p=128),
                          reads=[Bsrc_tiles[t]], writes=[Bxi])
                    if pre_ln:
                        for c in range(2):
                            layernorm(xi[:, c, :], xi[:, c, :], lng0, lnb0, Bxi, Bxi, B_ln0, ltmp, B_ltmp)

                def prologue(t):
                    xi, Bxi = xin[t % 2], B_xin[t % 2]
                    h_, Bh_ = hT[t % 2], B_hT[t % 2]
                    P.op('act', lambda e: e.copy(out=xbf[:], in_=xi[:]), reads=[Bxi], writes=[B_xbf])
                    for c in range(2):
                        for fc in range(8):
                            P.op('pe', lambda e, c=c, fc=fc: e.transpose(
                                out=pT[:, fc * 128:(fc + 1) * 128], in_=xbf[:, c, fc * 128:(fc + 1) * 128], identity=ident_b[:]),
                                reads=[B_xbf, B_const], writes=[B_ps[7]])
                        P.op('dve', lambda e, c=c: e.tensor_copy(
                            out=h_[:, :, c * 128:(c + 1) * 128], in_=pT[:, :].rearrange("p (q t) -> p q t", q=8)),
                            reads=[B_ps[7]], writes=[Bh_])

                def up(t, j):
                    h_, Bh_ = hT[t % 2], B_hT[t % 2]
                    bk = 4 + j % NP
                    pu3 = psum[bk][:].rearrange("p (a t) -> p a t", a=2)
                    for a in range(2):
                        col0 = a * DFF + j * 128
                        for kc in range(8):
                            P.op('pe', lambda e, a=a, kc=kc, col0=col0: e.matmul(
                                pu3[:, a, :], lhsT=w_up[:, kc, col0:col0 + 128], rhs=h_[:, kc, :],
                                start=(kc == 0), stop=(kc == 7)),
                                reads=[B_wupg[j // JG], Bh_], writes=[B_ps[bk]], cost=0.18)

                def ew(t, j):
                    bk = 4 + j % NP
                    Bpu = B_ps[bk]
                    pu3 = psum[bk][:].rearrange("p (a t) -> p a t", a=2)
                    x_, Bx_ = xs[j % NP], B_xs[j % NP]
                    P.op('act', lambda e: e.copy(out=x_[:, :, 2:2 + TB], in_=pu3), reads=[Bpu], writes=[Bx_])
                    P.op('pool', lambda e: e.tensor_copy(out=x_[:, :, 0:2], in_=halo[:, j, :, :]), reads=[B_halo], writes=[Bx_])
                    P.op('pool', lambda e: e.tensor_copy(out=halo[:, j, :, :], in_=x_[:, :, TB:TB + 2]), reads=[Bx_], writes=[B_halo])
                    for a, (ct, Bct) in enumerate(((cg[j % NP], B_cg[j % NP]), (cv[j % NP], B_cv[j % NP]))):
                        ch = a * NJ + j
                        P.op('act', lambda e, a=a, ch=ch, ct=ct: e.activation(
                            out=ct[:], in_=pu3[:, a, :], func=AF.Identity, bias=cw[:, 3, ch:ch + 1], scale=cw[:, 2, ch:ch + 1]),
                            reads=[Bpu, B_cw], writes=[Bct])
                        P.op('dve', lambda e, a=a, ch=ch, ct=ct: e.scalar_tensor_tensor(
                            out=ct[:], in0=x_[:, a, 1:1 + TB], scalar=cw[:, 1, ch:ch + 1], in1=ct[:], op0=ALU.mult, op1=ALU.add),
                            reads=[Bx_, B_cw, Bct], writes=[Bct])
                        P.op('dve', lambda e, a=a, ch=ch, ct=ct: e.scalar_tensor_tensor(
                            out=ct[:], in0=x_[:, a, 0:TB], scalar=cw[:, 0, ch:ch + 1], in1=ct[:], op0=ALU.mult, op1=ALU.add),
                            reads=[Bx_, B_cw, Bct], writes=[Bct])
                    s_, Bs_ = sg[j % NP], B_sg[j % NP]
                    P.op('act', lambda e: e.activation(out=s_[:], in_=cg[j % NP][:], func=AF.Silu),
                         reads=[B_cg[j % NP]], writes=[Bs_])
                    a_, Ba_ = aT[j % NA], B_aT[j % NA]
                    P.op('pool', lambda e: e.tensor_tensor(out=a_[:], in0=s_[:], in1=cv[j % NP][:], op=ALU.mult),
                         reads=[Bs_, B_cv[j % NP]], writes=[Ba_])

                def down(t, j):
                    a_, Ba_ = aT[j % NA], B_aT[j % NA]
                    for c in range(2):
                        for hf in range(2):
                            bk = c * 2 + hf
                            P.op('pe', lambda e, c=c, hf=hf, bk=bk: e.matmul(
                                psum[bk][:], lhsT=a_[:, c * 128:(c + 1) * 128], rhs=w_down[:, j, hf * 512:(hf + 1) * 512],
                                start=(j == 0), stop=(j == NJ - 1)),
                                reads=[Ba_, B_wdng[j // JG]], writes=[B_ps[bk]], cost=0.39)

                def epilogue(t):
                    tok0 = t * TB
                    xi, Bxi = xin[t % 2], B_xin[t % 2]
                    for c in range(2):
                        r_, Br_ = rr[c], B_rr[c]
                        for hf in range(2):
                            bk = c * 2 + hf
                            P.op('dve', lambda e, c=c, hf=hf, bk=bk, r_=r_: e.scalar_tensor_tensor(
                                out=r_[:, hf * 512:(hf + 1) * 512], in0=xi[:, c, hf * 512:(hf + 1) * 512], scalar=ALPHA,
                                in1=psum[bk][:], op0=ALU.mult, op1=ALU.add),
                                reads=[Bxi, B_ps[bk]], writes=[Br_])
                    for c in range(2):
                        layernorm(rr[c][:], xi[:, c, :], lng, lnb, B_rr[c], Bxi, B_ln, ltmp, B_ltmp)
                    P.dma('sp', dst_d[tok0:tok0 + TB, :].rearrange("(c p) f -> p c f", p=128), xi[:],
                          reads=[Bxi], writes=[Bdst_tiles[t]])

                load(0)
                prologue(0)
                for t in range(ntile):
                    tok0 = t * TB
                    if t + 1 < ntile:
                        load(t + 1)
                    if tok0 % seqlen == 0:
                        P.op('pool', lambda e: e.memset(halo[:], 0.0), writes=[B_halo])
                    for j in range(min(NP - 1, NJ)):
                        up(t, j)
                    for j in range(NJ):
                        if j + NP - 1 < NJ:
                            up(t, j + NP - 1)
                        ew(t, j)
                        down(t, j)
                        if j == NJ - 4 and t + 1 < ntile:
                            prologue(t + 1)
                    epilogue(t)

        def phase_a(layer, src_d, Bsrc_tiles, dst_d, Bdst_tiles, pre_ln):
            NT = TB
            NCH = NT // 128
            astack = ExitStack()
            with astack:
                sba = lambda name, shape, dt=F32: astack.enter_context(nc.sbuf_tensor(f"a{layer}_{name}", shape, dt))
                w_in = sba("w_in", [128, 8, DIN], BF16)
                w_out = sba("w_out", [128, 8, D], BF16)
                load_weight(w_in, w_in_d[layer], 8, B_win)
                load_weight(w_out, w_out_d[layer], 8, B_wout)
                load_ln_consts(ln1_g[layer], ln1_b[layer], lng, lnb, B_ln)
                if pre_ln:
                    lng0 = sba("lng0", [128, D], F32)
                    lnb0 = sba("lnb0", [128, D], F32)
                    load_ln_consts(ln_in_g, ln_in_b, lng0, lnb0, B_ln0)
                Bc = Buf("aconst")
                tri_f = sba("tri_f", [128, 128])
                maskge_b = sba("maskge_b", [128, 128], BF16)
                maskgt_f = sba("maskgt_f", [128, 128])
                ones_f = sba("ones_f", [128, 128])
                ones_b = sba("ones_b", [128, 128], BF16)
                one_t = sba("one_t", [128, 1])
                P.op('pool', lambda e: e.memset(ones_f[:], 1.0), writes=[Bc])
                P.op('pool', lambda e: e.memset(ones_b[:], 1.0), writes=[Bc])
                P.op('pool', lambda e: e.memset(one_t[:], 1.0), writes=[Bc])
                hm = sba("hm", [64, 4])
                P.op('pool', lambda e: e.memset(hm[:], 0.0), writes=[Bc])
                for h in range(4):
                    hb = (h % 2) * 32
                    P.op('pool', lambda e, h=h, hb=hb: e.memset(hm[hb:hb + 32, h:h + 1], 32.0 ** -0.5), writes=[Bc])
                P.op('pool', lambda e: e.affine_select(out=tri_f[:], in_=ones_f[:], pattern=[[1, 128]], compare_op=ALU.is_ge,
                                                       fill=0.0, base=0, channel_multiplier=-1), reads=[Bc], writes=[Bc])
                P.op('pool', lambda e: e.tensor_copy(out=maskge_b[:], in_=tri_f[:]), reads=[Bc], writes=[Bc])
                P.op('pool', lambda e: e.affine_select(out=maskgt_f[:], in_=ones_f[:], pattern=[[-1, 128]], compare_op=ALU.is_gt,
                                                       fill=0.0, base=0, channel_multiplier=1), reads=[Bc], writes=[Bc])
                craw = sba("craw", [44, 128])
                ccol = sba("ccol", [128, 44])
                B_craw = Buf("craw")
                P.dma('sp', craw[0:32, :], ssd_conv_w[layer].rearrange("k (c p) -> (k c) p", p=128), reads=[], writes=[B_craw])
                P.dma('sp', craw[32:40, :], ssd_conv_b[layer].rearrange("(c p) -> c p", p=128), reads=[], writes=[B_craw])
                P.dma('sp', craw[40:42, :], sgu_norm_g[layer].rearrange("(c p) -> c p", p=128), reads=[], writes=[B_craw])
                P.dma('sp', craw[42:44, :], sgu_norm_b[layer].rearrange("(c p) -> c p", p=128), reads=[], writes=[B_craw])
                P.op('pe', lambda e: e.transpose(out=psum[0][:, 0:44], in_=craw[:, :], identity=ident_f[0:44, 0:44]),
                     reads=[B_craw, B_const], writes=[B_ps[0]])
                P.op('dve', lambda e: e.tensor_copy(out=ccol[:], in_=psum[0][:, 0:44]), reads=[B_ps[0]], writes=[Bc])
                CW = lambda k, fc: ccol[:, k * 8 + fc:k * 8 + fc + 1]
                CB_ = lambda fc: ccol[:, 32 + fc:33 + fc]
                SGG = lambda fc: ccol[:, 40 + fc:41 + fc]
                SGB = lambda fc: ccol[:, 42 + fc:43 + fc]
                dtb_bc = sba("dtb_bc", [128, 8]); acont_bc = sba("acont_bc", [128, 8]); dsk_bc = sba("dsk_bc", [128, 8])
                sng_bc = sba("sng_bc", [128, 512]); gng_bc = sba("gng_bc", [128, 64])
                B_bc = Buf("bcast")
                P.dma('sp', dtb_bc[:], ssd_dt_bias[layer].partition_broadcast(128), reads=[], writes=[B_bc])
                P.dma('sp', acont_bc[:], ssd_a_log[layer].partition_broadcast(128), reads=[], writes=[B_bc])
                P.dma('sp', dsk_bc[:], ssd_d[layer].partition_broadcast(128), reads=[], writes=[B_bc])
                P.dma('sp', sng_bc[:], ssd_norm_g[layer].partition_broadcast(128), reads=[], writes=[B_bc])
                P.dma('sp', gng_bc[:], gla_norm_g[layer].partition_broadcast(128), reads=[], writes=[B_bc])
                P.op('act', lambda e: e.activation(out=acont_bc[:], in_=acont_bc[:], func=AF.Exp), reads=[B_bc], writes=[B_bc])
                P.op('act', lambda e: e.mul(acont_bc[:], acont_bc[:], -1.0), reads=[B_bc], writes=[B_bc])
                wg_f = sba("wg_f", [17, 128]); wg_b = sba("wg_b", [17, 128], BF16)
                B_wg = Buf("wg")
                P.dma('sp', wg_f[0:16, :], gla_w_gate[layer], reads=[], writes=[B_wg])
                P.dma('sp', wg_f[16:17, :], gla_b_gate[layer].rearrange("(a n) -> a n", a=1), reads=[], writes=[B_wg])
                P.op('dve', lambda e: e.tensor_copy(out=wg_b[:], in_=wg_f[:]), reads=[B_wg], writes=[Bc])
                wsg = sba("wsg", [128, 4, 128]); wmT = sba("wmT", [128, 4, 128], BF16)
                bsbc = sba("bsbc", [128, 2, 128]); csg = sba("csg", [128, 2, 128])
                B_wsg = Buf("wsg")
                P.dma('sp', wsg[:], sgu_w[layer].rearrange("g t s -> t g s"), reads=[], writes=[B_wsg])
                for g in range(4):
                    hp = (g % 2) * 64
                    P.dma('sp', bsbc[hp:hp + 64, g // 2, :], sgu_b[layer, g].partition_broadcast(64), reads=[], writes=[B_bc])
                P.op('pool', lambda e: e.affine_select(out=wsg[:], in_=wsg[:], pattern=[[0, 4], [-1, 128]], compare_op=ALU.is_ge,
                                                       fill=0.0, base=0, channel_multiplier=1), reads=[B_wsg], writes=[B_wsg])
                for g in range(4):
                    P.op('pe', lambda e, g=g: e.transpose(out=psum[1][:, g * 128:(g + 1) * 128], in_=wsg[:, g, :], identity=ident_f[:]),
                         reads=[B_wsg, B_const], writes=[B_ps[1]])
                P.op('dve', lambda e: e.tensor_copy(out=wmT[:], in_=psum[1][:].rearrange("p (g t) -> p g t", g=4)),
                     reads=[B_ps[1]], writes=[Bc])
                for g in range(4):
                    P.op('pe', lambda e, g=g: e.matmul(psum[2][:, g * 128:(g + 1) * 128], lhsT=ones_b[:], rhs=wmT[:, g, :],
                                                       start=True, stop=True), reads=[Bc], writes=[B_ps[2]])
                for g in range(4):
                    hp, fc = (g % 2) * 64, g // 2
                    P.op('dve', lambda e, g=g, hp=hp, fc=fc: e.scalar_tensor_tensor(
                        out=csg[hp:hp + 64, fc, :], in0=psum[2][hp:hp + 64, g * 128:(g + 1) * 128], scalar=ccol[hp:hp + 64, 42 + fc:43 + fc],
                        in1=bsbc[hp:hp + 64, fc, :], op0=ALU.mult, op1=ALU.add), reads=[B_ps[2], Bc, B_bc], writes=[Bc])

                xin = [sba(f"xin{i}", [128, NCH, D]) for i in range(2)]
                B_xin = [Buf(f"axin{i}") for i in range(2)]
                xbf = sba("xbf", [128, NCH, D], BF16); B_xbf = Buf("axbf")
                hT2 = [sba(f"hT{i}", [128, 8, NT], BF16) for i in range(2)]; B_hT2 = [Buf(f"ahT{i}") for i in range(2)]
                qk2 = [sba(f"qk_f{i}", [64, 4, NT]) for i in range(2)]; B_qk2 = [Buf(f"qk{i}") for i in range(2)]
                glr2 = [sba(f"glrT{i}", [32, NT], BF16) for i in range(2)]; B_glr2 = [Buf(f"glr{i}") for i in range(2)]
                gu2 = [sba(f"gu{i}", [128, 2, NT]) for i in range(2)]; B_gu2 = [Buf(f"gu{i}") for i in range(2)]
                gtp1 = sba("gtp1", [128, 512]); gtp2 = sba("gtp2", [128, 512]); B_gtp = Buf("gtp")
                lstp = sba("lstp", [128, 2, 6]); lmvp = sba("lmvp", [128, 2]); lscp = sba("lscp", [128, 4])
                B_ltmpp = Buf("altmpp")
                ltmpp = (lstp, lmvp, lscp)
                B7 = [B_ps[7], B_ps[7]]
                xr = sba("xr", [128, 8, 3 + NT]); B_xr = Buf("xr")
                xhalo = sba("xhalo", [128, 8, 3]); B_xhalo = Buf("xhalo")
                ct = [sba(f"ct{i}", [128, NT]) for i in range(2)]; B_ct = [Buf(f"ct{i}") for i in range(2)]
                xc2 = [sba(f"xc{i}", [128, 8, NT], BF16) for i in range(2)]; B_xc2 = [Buf(f"xc{i}") for i in range(2)]
                mixT = sba("mixT", [128, 8, 128], BF16); B_mixT = Buf("mixT")
                vb2 = [sba(f"vb{i}", [128, NCH, 256], BF16) for i in range(2)]; B_vb2 = [Buf(f"vb{i}") for i in range(2)]
                gs2 = [sba(f"gs{i}", [128, NCH, 256]) for i in range(2)]; B_gs2 = [Buf(f"gs{i}") for i in range(2)]
                svg = sba("svg", [128, 256]); B_sv = Buf("sv")
                xhat2 = [sba(f"xhat{i}", [128, NCH, 256], BF16) for i in range(2)]; B_xhat2 = [Buf(f"xhat{i}") for i in range(2)]
                zs2 = [sba(f"zs{i}", [128, NCH, 512]) for i in range(2)]; B_zs2 = [Buf(f"zs{i}") for i in range(2)]
                dta2 = [sba(f"dta{i}", [128, NCH, 16]) for i in range(2)]; B_dta2 = [Buf(f"dta{i}") for i in range(2)]
                la2 = [sba(f"la{i}", [128, NCH, 128]) for i in range(2)]; B_la2 = [Buf(f"la{i}") for i in range(2)]
                dts = sba("dts", [128, 64]); B_dts = Buf("dts")
                e1 = sba("e1", [128, 128]); B_e1 = Buf("e1")
                ecp = sba("ecp", [64, 2, 128]); ecn = sba("ecn", [64, 2, 128]); B_ec = Buf("ec")
                qt = sba("qt", [64, 4, 128], BF16); kt = sba("kt", [64, 2, 128], BF16); B_qkt = Buf("qkt")
                ktm = sba("ktm", [128, 128], BF16); B_ktm = Buf("ktm")
                sm = sba("sm", [128, 4, 128], BF16); B_sm = Buf("sm")
                gst = sba("gst", [64, 2, 128]); gstt = sba("gstt", [64, 2, 128]); gstb = sba("gstb", [64, 2, 128], BF16)
                B_gst = Buf("gst"); B_gstb = Buf("gstb")
                osq = sba("osq", [128, 512]); B_osq = Buf("osq")
                osq2 = sba("osq2", [128, 512]); B_osq2 = Buf("osq2")
                B5a = B5b = B_ps[5]
                B6a = B6c = B_ps[6]
                og = sba("og", [128, 256]); B_og = Buf("og")
                ogl = sba("ogl", [128, 256], BF16); B_ogl = Buf("ogl")
                gsc = sba("gsc", [128, 16]); B_gsc = Buf("gsc")
                sgt = sba("sgt", [128, 128]); B_sgt = Buf("sgt")
                ldec = sba("ldec", [128, 8, 128]); B_ldec = Buf("ldec")
                dm = sba("dm", [128, 8, 128], BF16); B_dm = Buf("dm")
                mm_ = sba("mm_", [128, 8, 128], BF16); B_mm = Buf("mm")
                cbm = sba("cbm", [128, 2, 128], BF16); B_cbm = Buf("cbm")
                xdt = sba("xdt", [128, 512], BF16); xdd = sba("xdd", [128, 512], BF16); B_xdt = Buf("xdt"); B_xdd = Buf("xdd")
                xsd = sba("xsd", [128, 512]); B_xsd = Buf("xsd")
                bmtm = sba("bmtm", [128, 256], BF16); B_bmtm = Buf("bmtm")
                y1 = sba("y1", [128, 512]); B_y1 = Buf("y1")
                yb = sba("yb", [128, 512], BF16); B_yb = Buf("yb")
                sst = sba("sst", [128, 512]); sstb = sba("sstb", [128, 512], BF16); B_sst = Buf("sst"); B_sstb = Buf("sstb")
                lst = sba("lst", [128, 2, 6]); lmv = sba("lmv", [128, 2]); lsc = sba("lsc", [128, 4])
                B_ltmp = Buf("altmp")
                ltmp = (lst, lmv, lsc)
                pbf = [psum[i][:].bitcast(BF16) for i in range(8)]

                for i in range(2):
                    P.op('pool', lambda e, i=i: e.memset(glr2[i][:], 1.0), writes=[B_glr2[i]])

                def gelu(dst, x_sb, n, Bx, Bdst, gt1=None, gt2=None, B_gt=None):
                    P.op('dve', lambda e: e.scalar_tensor_tensor(out=gt1[:, 0:n], in0=x_sb, scalar=0.044715, in1=x_sb,
                                                                 op0=ALU.mult, op1=ALU.mult), reads=[Bx], writes=[B_gt])
                    P.op('dve', lambda e: e.scalar_tensor_tensor(out=gt2[:, 0:n], in0=gt1[:, 0:n], scalar=1.0, in1=x_sb,
                                                                 op0=ALU.add, op1=ALU.mult), reads=[Bx, B_gt], writes=[B_gt])
                    P.op('act', lambda e: e.activation(out=gt1[:, 0:n], in_=gt2[:, 0:n], func=AF.Sigmoid, scale=1.5957691216057308),
                         reads=[B_gt], writes=[B_gt])
                    P.op('pool', lambda e: e.tensor_tensor(out=dst, in0=gt1[:, 0:n], in1=x_sb, op=ALU.mult),
                         reads=[B_gt, Bx], writes=[Bdst])

                def inproj_tm(bank, c, c0, N, o0=0):
                    for kc in range(8):
                        P.op('pe', lambda e, kc=kc: e.matmul(
                            psum[bank][:, o0:o0 + N], lhsT=hT[:, kc, c * 128:(c + 1) * 128], rhs=w_in[:, kc, c0:c0 + N],
                            start=(kc == 0), stop=(kc == 7)), reads=[B_win, B_hT], writes=[B_ps[bank]])

                ntile_a = ntok // NT

                def gen_pro(t):
                    par = t % 2
                    tok0 = t * NT
                    xi, Bxi = xin[par], B_xin[par]
                    h_, Bh_ = hT2[par], B_hT2[par]
                    qk_, Bqk_ = qk2[par], B_qk2[par]
                    gl_, Bgl_ = glr2[par], B_glr2[par]
                    gu_, Bgu_ = gu2[par], B_gu2[par]
                    xc_, Bxc_ = xc2[par], B_xc2[par]
                    P.dma('sp', xi[:], src_d[tok0:tok0 + NT, :].rearrange("(c p) f -> p c f", p=128),
                          reads=[Bsrc_tiles[t]], writes=[Bxi])
                    yield
                    if pre_ln:
                        for c in range(NCH):
                            layernorm(xi[:, c, :], xi[:, c, :], lng0, lnb0, Bxi, Bxi, B_ln0, ltmpp, B_ltmpp)
                            yield
                    if tok0 % seqlen == 0:
                        P.op('pool', lambda e: e.memset(xhalo[:], 0.0), writes=[B_xhalo])
                    P.op('act', lambda e: e.copy(out=xbf[:], in_=xi[:]), reads=[Bxi], writes=[B_xbf])
                    yield
                    hh = 0
                    for c in range(NCH):
                        for half in range(2):
                            pt, Bpt = pbf[7][:, hh * 512:(hh + 1) * 512], B7[hh]
                            for q in range(4):
                                fc = half * 4 + q
                                P.op('pe', lambda e, c=c, fc=fc, q=q, pt=pt: e.transpose(
                                    out=pt[:, q * 128:(q + 1) * 128], in_=xbf[:, c, fc * 128:(fc + 1) * 128], identity=ident_b[:]),
                                    reads=[B_xbf, B_const], writes=[Bpt])
                            P.op('dve', lambda e, c=c, half=half, pt=pt: e.tensor_copy(
                                out=h_[:, half * 4:(half + 1) * 4, c * 128:(c + 1) * 128],
                                in_=pt.rearrange("p (q t) -> p q t", q=4)),
                                reads=[Bpt], writes=[Bh_])
                            hh ^= 1
                            yield
                    def fm(specs):
                        for (c0, M, o0) in specs:
                            for kc in range(8):
                                P.op('pe', lambda e, c0=c0, M=M, o0=o0, kc=kc: e.matmul(
                                    psum[7][0:M, o0:o0 + NT], lhsT=w_in[:, kc, c0:c0 + M], rhs=h_[:, kc, :],
                                    start=(kc == 0), stop=(kc == 7)), reads=[B_win, Bh_], writes=[B_ps[7]], cost=0.18)
                    for gi, c0 in enumerate((O_Q, O_K)):
                        fm([(c0, 64, 0), (c0 + 64, 64, NT)])
                        P.op('act', lambda e, gi=gi: e.copy(out=qk_[:, 2 * gi:2 * gi + 2, :],
                                                            in_=psum[7][0:64, :].rearrange("p (a t) -> p a t", a=2)),
                             reads=[B_ps[7]], writes=[Bqk_])
                        yield
                    fm([(O_GLR, 16, 0)])
                    P.op('act', lambda e: e.copy(out=gl_[0:16, :], in_=psum[7][0:16, 0:NT]), reads=[B_ps[7]], writes=[Bgl_])
                    yield
                    fm([(O_SU, 128, 0), (O_SU + 128, 128, NT)])
                    P.op('act', lambda e: e.copy(out=gu_[:], in_=psum[7][:].rearrange("p (a t) -> p a t", a=2)),
                         reads=[B_ps[7]], writes=[Bgu_])
                    yield
                    gelu(gu_[:].rearrange("p a t -> p (a t)"), gu_[:].rearrange("p a t -> p (a t)"), 2 * NT, Bgu_, Bgu_,
                         gt1=gtp1, gt2=gtp2, B_gt=B_gtp)
                    yield
                    for pr in range(4):
                        fm([(O_XBC + (2 * pr) * 128, 128, 0), (O_XBC + (2 * pr + 1) * 128, 128, NT)])
                        P.op('act', lambda e, pr=pr: e.copy(out=xr[:, 2 * pr:2 * pr + 2, 3:3 + NT],
                                                            in_=psum[7][:].rearrange("p (a t) -> p a t", a=2)),
                             reads=[B_ps[7]], writes=[B_xr])
                        yield
                    P.op('pool', lambda e: e.tensor_copy(out=xr[:, :, 0:3], in_=xhalo[:]), reads=[B_xhalo], writes=[B_xr])
                    P.op('pool', lambda e: e.tensor_copy(out=xhalo[:], in_=xr[:, :, NT:NT + 3]), reads=[B_xr], writes=[B_xhalo])
                    yield
                    for fc in range(8):
                        c_, Bc_ = ct[fc % 2], B_ct[fc % 2]
                        P.op('act', lambda e, fc=fc, c_=c_: e.activation(out=c_[:], in_=xr[:, fc, 3:3 + NT], func=AF.Identity,
                                                                         bias=CB_(fc), scale=CW(3, fc)), reads=[B_xr, Bc], writes=[Bc_])
                        for k in (2, 1, 0):
                            P.op('dve', lambda e, fc=fc, k=k, c_=c_: e.scalar_tensor_tensor(
                                out=c_[:], in0=xr[:, fc, k:k + NT], scalar=CW(k, fc), in1=c_[:], op0=ALU.mult, op1=ALU.add),
                                reads=[B_xr, Bc, Bc_], writes=[Bc_])
                        P.op('act', lambda e, fc=fc, c_=c_: e.activation(out=xc_[:, fc, :], in_=c_[:], func=AF.Silu),
                             reads=[Bc_], writes=[Bxc_])
                        yield
                    vb_, gs_, zs_, xh_, dta_, la_ = vb2[par], gs2[par], zs2[par], xhat2[par], dta2[par], la2[par]
                    Bvb_, Bgs_, Bzs_, Bxh_, Bdta_, Bla_ = B_vb2[par], B_gs2[par], B_zs2[par], B_xhat2[par], B_dta2[par], B_la2[par]

                    def tm(c, c0, N, o0=0):
                        for kc in range(8):
                            P.op('pe', lambda e, kc=kc: e.matmul(
                                psum[7][:, o0:o0 + N], lhsT=h_[:, kc, c * 128:(c + 1) * 128], rhs=w_in[:, kc, c0:c0 + N],
                                start=(kc == 0), stop=(kc == 7)), reads=[B_win, Bh_], writes=[B_ps[7]], cost=0.1 + N * 0.00057)
                    for c in range(NCH):
                        cs_ = c * 128
                        tm(c, O_V, 512)
                        P.op('act', lambda e, c=c: e.copy(out=vb_[:, c, :], in_=psum[7][:, 0:256]), reads=[B_ps[7]], writes=[Bvb_])
                        P.op('act', lambda e, c=c: e.activation(out=gs_[:, c, :], in_=psum[7][:, 256:512], func=AF.Silu),
                             reads=[B_ps[7]], writes=[Bgs_])
                        yield
                        P.op('pool', lambda e, c=c: e.tensor_tensor(out=gs_[:, c, :].rearrange("p (h v) -> p h v", h=4),
                                                                    in0=gs_[:, c, :].rearrange("p (h v) -> p h v", h=4),
                                                                    in1=gng_bc[:].unsqueeze(1).broadcast_to([128, 4, 64]), op=ALU.mult),
                             reads=[Bgs_, B_bc], writes=[Bgs_])
                        tm(c, O_Z, 512)
                        P.op('act', lambda e, c=c: e.activation(out=zs_[:, c, :], in_=psum[7][:], func=AF.Silu), reads=[B_ps[7]], writes=[Bzs_])
                        yield
                        tm(c, O_SV, 256)
                        tm(c, O_DT, 8, o0=256)
                        P.op('act', lambda e: e.copy(out=svg[:], in_=psum[7][:, 0:256]), reads=[B_ps[7]], writes=[B_sv])
                        P.op('dve', lambda e, c=c: e.tensor_tensor(out=dta_[:, c, 0:8], in0=psum[7][:, 256:264], in1=dtb_bc[:], op=ALU.add),
                             reads=[B_ps[7], B_bc], writes=[Bdta_])
                        yield
                        P.op('act', lambda e, c=c: e.activation(out=dta_[:, c, 0:8], in_=dta_[:, c, 0:8], func=AF.Exp), reads=[Bdta_], writes=[Bdta_])
                        P.op('act', lambda e, c=c: e.activation(out=dta_[:, c, 0:8], in_=dta_[:, c, 0:8], func=AF.Ln, bias=one_t[:], scale=1.0),
                             reads=[Bdta_, Bc], writes=[Bdta_])
                        P.op('dve', lambda e, c=c: e.tensor_tensor(out=dta_[:, c, 8:16], in0=dta_[:, c, 0:8], in1=acont_bc[:], op=ALU.mult),
                             reads=[Bdta_, B_bc], writes=[Bdta_])
                        yield
                        P.op('pe', lambda e, cs_=cs_: e.matmul(psum[7][:, 0:128], lhsT=gl_[0:17, cs_:cs_ + 128], rhs=wg_b[:, :], start=True, stop=True),
                             reads=[Bgl_, Bc], writes=[B_ps[7]])
                        P.op('act', lambda e: e.activation(out=e1[:], in_=psum[7][:, 0:128], func=AF.Exp, scale=-1.0),
                             reads=[B_ps[7]], writes=[B_e1])
                        P.op('act', lambda e, c=c: e.activation(out=la_[:, c, :], in_=e1[:], func=AF.Ln, bias=one_t[:], scale=1.0),
                             reads=[B_e1, Bc], writes=[Bla_])
                        yield
                        gelu(svg[:], svg[:], 256, B_sv, B_sv, gt1=gtp1, gt2=gtp2, B_gt=B_gtp)
                        yield
                        P.op('dve', lambda e: e.bn_stats(out=lstp[:, 0, :], in_=svg[:]), reads=[B_sv], writes=[B_ltmpp])
                        P.op('dve', lambda e: e.bn_aggr(out=lmvp[:], in_=lstp[:, 0, :]), reads=[B_ltmpp], writes=[B_ltmpp])
                        P.op('act', lambda e: e.activation(out=lscp[:, 0:1], in_=lmvp[:, 1:2], func=AF.Ln, bias=eps_t[:], scale=1.0),
                             reads=[B_ltmpp, B_const], writes=[B_ltmpp])
                        P.op('act', lambda e: e.activation(out=lscp[:, 1:2], in_=lscp[:, 0:1], func=AF.Exp, scale=-0.5),
                             reads=[B_ltmpp], writes=[B_ltmpp])
                        yield
                        P.op('dve', lambda e: e.tensor_scalar(out=lscp[:, 2:3], in0=lmvp[:, 0:1], scalar1=lscp[:, 1:2], scalar2=-1.0,
                                                              op0=ALU.mult, op1=ALU.mult), reads=[B_ltmpp], writes=[B_ltmpp])
                        P.op('act', lambda e, c=c: e.activation(out=xh_[:, c, :], in_=svg[:], func=AF.Identity, bias=lscp[:, 2:3], scale=lscp[:, 1:2]),
                             reads=[B_sv, B_ltmpp], writes=[Bxh_])
                        yield

                def gen_ln1(xi_, Bxi_, c_, t_, last):
                    yield
                    layernorm(xi_[:, c_, :], xi_[:, c_, :], lng, lnb, Bxi_, Bxi_, B_ln, ltmp, B_ltmp)
                    yield
                    if last:
                        tk = t_ * NT
                        P.dma('sp', dst_d[tk:tk + NT, :].rearrange("(c p) f -> p c f", p=128), xi_[:],
                              reads=[Bxi_], writes=[Bdst_tiles[t_]])

                tails = []
                for _ in gen_pro(0):
                    pass
                for t in range(ntile_a):
                    tok0 = t * NT
                    par = t % 2
                    xi, Bxi = xin[par], B_xin[par]
                    hT, B_hT = hT2[par], B_hT2[par]
                    qk_f, B_qk = qk2[par], B_qk2[par]
                    glrT, B_glr = glr2[par], B_glr2[par]
                    gu, B_gu = gu2[par], B_gu2[par]
                    xc, B_xc = xc2[par], B_xc2[par]
                    nxt = gen_pro(t + 1) if t + 1 < ntile_a else None
                    if tok0 % seqlen == 0:
                        P.op('pool', lambda e: e.memset(gst[:], 0.0), writes=[B_gst])
                        P.op('pool', lambda e: e.memset(gstb[:], 0.0), writes=[B_gstb])
                        P.op('pool', lambda e: e.memset(sst[:], 0.0), writes=[B_sst])
                        P.op('pool', lambda e: e.memset(sstb[:], 0.0), writes=[B_sstb])

                    for c in range(NCH):
                        cs = c * 128
                        vb, B_vb = vb2[par][:, c, :], B_vb2[par]
                        gs, B_gs = gs2[par][:, c, :], B_gs2[par]
                        zs, B_zs = zs2[par][:, c, :], B_zs2[par]
                        xhat, B_xhat = xhat2[par][:, c, :], B_xhat2[par]
                        dta, B_dta = dta2[par][:, c, :], B_dta2[par]
                        la, B_la = la2[par][:, c, :], B_la2[par]
                        def gen_sgu():
                            if 'sgu' not in mixers:
                                P.op('pool', lambda e: e.memset(mixT[:, 2:4, :], 0.0), writes=[B_mixT])
                                return
                            yield
                            for g in range(4):
                                hp, fc = (g % 2) * 64, g // 2
                                P.op('pe', lambda e, g=g, hp=hp, fc=fc: e.matmul(
                                    psum[5][hp:hp + 64, 256 + fc * 128:256 + (fc + 1) * 128], lhsT=xhat[:, g * 64:(g + 1) * 64], rhs=wmT[:, g, :],
                                    start=True, stop=True), reads=[B_xhat, Bc], writes=[B5b])
                            yield
                            for fc in range(2):
                                P.op('dve', lambda e, fc=fc: e.scalar_tensor_tensor(
                                    out=sgt[:], in0=psum[5][:, 256 + fc * 128:256 + (fc + 1) * 128], scalar=SGG(fc), in1=csg[:, fc, :],
                                    op0=ALU.mult, op1=ALU.add), reads=[B5b, Bc], writes=[B_sgt])
                                P.op('dve', lambda e, fc=fc: e.tensor_tensor(out=mixT[:, 2 + fc, :], in0=sgt[:], in1=gu[:, fc, cs:cs + 128],
                                                                              op=ALU.mult), reads=[B_sgt, B_gu], writes=[B_mixT])
                        def gen_gla():
                            if 'gla' not in mixers:
                                P.op('pool', lambda e: e.memset(mixT[:, 0:2, :], 0.0), writes=[B_mixT])
                                return
                            yield
                            for pr in range(2):
                                P.op('pe', lambda e, pr=pr: e.matmul(psum[3][0:64, 128 + pr * 128:256 + pr * 128], lhsT=la[:, pr * 64:(pr + 1) * 64],
                                                                     rhs=tri_f[:], start=True, stop=True), reads=[B_la, Bc], writes=[B_ps[3]])
                            yield
                            cum3 = psum[3][0:64, 128:384].rearrange("p (a t) -> p a t", a=2)
                            yield
                            P.op('act', lambda e: e.activation(out=ecp[:], in_=cum3, func=AF.Exp, scale=-1.0 / 16.0), reads=[B_ps[3]], writes=[B_ec])
                            yield
                            P.op('act', lambda e: e.activation(out=ecn[:], in_=cum3, func=AF.Exp, scale=1.0 / 16.0), reads=[B_ps[3]], writes=[B_ec])
                            yield
                            for h in range(4):
                                P.op('dve', lambda e, h=h: e.scalar_tensor_tensor(out=qt[:, h, :], in0=qk_f[:, h // 2, cs:cs + 128], scalar=hm[:, h:h + 1],
                                                                                  in1=ecp[:, h // 2, :], op0=ALU.mult, op1=ALU.mult),
                                     reads=[B_qk, B_ec, Bc], writes=[B_qkt])
                            yield
                            P.op('dve', lambda e: e.tensor_tensor(out=kt[:], in0=qk_f[:, 2:4, cs:cs + 128], in1=ecn[:], op=ALU.mult),
                                 reads=[B_qk, B_ec], writes=[B_qkt])
                            yield
                            for pr in range(2):
                                P.op('pe', lambda e, pr=pr: e.transpose(out=pbf[6][:, 768 + pr * 64:768 + (pr + 1) * 64], in_=kt[:, pr, :],
                                                                        identity=ident_b[0:64, 0:64]), reads=[B_qkt, B_const], writes=[B6c])
                            yield
                            P.op('act', lambda e: e.copy(out=ktm[:], in_=pbf[6][:, 768:896]), reads=[B6c], writes=[B_ktm])
                            yield
                            for h in range(4):
                                pr, hb = h // 2, (h % 2) * 32
                                P.op('pe', lambda e, h=h, pr=pr, hb=hb: e.matmul(psum[4][:, h * 128:(h + 1) * 128], lhsT=kt[:, pr, :],
                                                                                 rhs=qt[:, h, :], start=True, stop=True),
                                     reads=[B_qkt], writes=[B_ps[4]])
                            yield
                            P.op('dve', lambda e: e.tensor_tensor(out=sm[:], in0=psum[4][:].rearrange("p (h t) -> p h t", h=4),
                                                                  in1=maskge_b[:].unsqueeze(1).broadcast_to([128, 4, 128]), op=ALU.mult),
                                 reads=[B_ps[4], Bc], writes=[B_sm])
                            yield
                            for h in range(4):
                                pr, hb = h // 2, (h % 2) * 32
                                P.op('pe', lambda e, h=h: e.matmul(psum[5][:, h * 64:(h + 1) * 64], lhsT=sm[:, h, :], rhs=vb[:, h * 64:(h + 1) * 64],
                                                                   start=True, stop=False), reads=[B_sm, B_vb], writes=[B5a])
                                P.op('pe', lambda e, h=h, pr=pr, hb=hb: e.matmul(psum[5][:, h * 64:(h + 1) * 64], lhsT=qt[:, h, :],
                                                                                 rhs=gstb[:, pr, (h % 2) * 64:(h % 2 + 1) * 64], start=False, stop=True),
                                     reads=[B_qkt, B_gstb], writes=[B5a])
                            yield
                            for pr in range(2):
                                P.op('pe', lambda e, pr=pr: e.matmul(psum[3][0:64, 128 + pr * 128:256 + pr * 128],
                                                                     lhsT=ktm[:, pr * 64:(pr + 1) * 64], rhs=vb[:, pr * 128:(pr + 1) * 128],
                                                                     start=True, stop=True), reads=[B_ktm, B_vb], writes=[B_ps[3]])
                            yield
                            P.op('dve', lambda e: e.tensor_tensor(out=gstt[:], in0=psum[3][0:64, 128:384].rearrange("p (a v) -> p a v", a=2),
                                                                  in1=gst[:], op=ALU.add), reads=[B_ps[3], B_gst], writes=[B_gst])
                            yield
                            for pr in range(2):
                                P.op('act', lambda e, pr=pr: e.activation(out=gst[:, pr, :], in_=gstt[:, pr, :], func=AF.Identity,
                                                                          scale=ecp[:, pr, 127:128]), reads=[B_gst, B_ec], writes=[B_gst])
                            yield
                            P.op('dve', lambda e: e.tensor_copy(out=gstb[:], in_=gst[:]), reads=[B_gst], writes=[B_gstb])
                            yield
                            P.op('act', lambda e: e.activation(out=osq[:, 0:256], in_=psum[5][:, 0:256], func=AF.Square), reads=[B5a], writes=[B_osq])
                            yield
                            P.op('dve', lambda e: e.tensor_reduce(out=gsc[:, 0:4], in_=osq[:, 0:256].rearrange("p (h v) -> p h v", h=4),
                                                                  axis=AX.X, op=ALU.add), reads=[B_osq], writes=[B_gsc])
                            yield
                            P.op('act', lambda e: e.activation(out=gsc[:, 4:8], in_=gsc[:, 0:4], func=AF.Ln, bias=eps_t[:], scale=1.0 / 64.0),
                                 reads=[B_gsc, B_const], writes=[B_gsc])
                            yield
                            P.op('act', lambda e: e.activation(out=gsc[:, 8:12], in_=gsc[:, 4:8], func=AF.Exp, scale=-0.5), reads=[B_gsc], writes=[B_gsc])
                            yield
                            P.op('dve', lambda e: e.tensor_tensor(out=og[:].rearrange("p (h v) -> p h v", h=4),
                                                                  in0=psum[5][:, 0:256].rearrange("p (h v) -> p h v", h=4),
                                                                  in1=gsc[:, 8:12].unsqueeze(2).broadcast_to([128, 4, 64]), op=ALU.mult),
                                 reads=[B5a, B_gsc], writes=[B_og])
                            yield
                            P.op('dve', lambda e: e.tensor_tensor(out=ogl[:], in0=og[:], in1=gs[:], op=ALU.mult), reads=[B_og, B_gs], writes=[B_ogl], cost=0.35)
                            yield
                            for q in range(2):
                                P.op('pe', lambda e, q=q: e.transpose(out=pbf[3][:, 768 + q * 128:768 + (q + 1) * 128], in_=ogl[:, q * 128:(q + 1) * 128],
                                                                      identity=ident_b[:]), reads=[B_ogl, B_const], writes=[B_ps[3]])
                            yield
                            P.op('act', lambda e: e.copy(out=mixT[:, 0:2, :], in_=pbf[3][:, 768:1024].rearrange("p (q t) -> p q t", q=2)),
                                 reads=[B_ps[3]], writes=[B_mixT])
                        def gen_ssd():
                            if 'ssd' not in mixers:
                                P.op('pool', lambda e: e.memset(mixT[:, 4:8, :], 0.0), writes=[B_mixT])
                                return
                            yield
                            for q in range(4):
                                P.op('pe', lambda e, q=q: e.transpose(out=pbf[6][:, q * 128:(q + 1) * 128], in_=xc[:, q, cs:cs + 128], identity=ident_b[:]),
                                     reads=[B_xc, B_const], writes=[B6a])
                            yield
                            for q in range(2):
                                P.op('pe', lambda e, q=q: e.transpose(out=pbf[6][:, 512 + q * 128:512 + (q + 1) * 128], in_=xc[:, 4 + q, cs:cs + 128],
                                                                      identity=ident_b[:]), reads=[B_xc, B_const], writes=[B6a])
                            yield
                            P.op('pe', lambda e: e.matmul(psum[2][:, 264:272], lhsT=tri_f[:], rhs=dta[:, 8:16], start=True, stop=True),
                                 reads=[B_dts, B_dta, Bc], writes=[B_ps[2]])
                            yield
                            P.op('pe', lambda e: e.matmul(psum[2][:, 272:280], lhsT=ones_f[:], rhs=dta[:, 8:16], start=True, stop=True),
                                 reads=[B_dts, B_dta, Bc], writes=[B_ps[2]])
                            yield
                            P.op('act', lambda e: e.copy(out=dts[:, 16:32], in_=psum[2][:, 264:280]), reads=[B_ps[2]], writes=[B_dts])
                            yield
                            P.op('act', lambda e: e.activation(out=dts[:, 32:48], in_=dts[:, 16:32], func=AF.Exp), reads=[B_dts, B_dta], writes=[B_dts])
                            yield
                            P.op('dve', lambda e: e.tensor_tensor(out=dts[:, 48:56], in0=dts[:, 24:32], in1=dts[:, 16:24], op=ALU.subtract),
                                 reads=[B_dts, B_dta], writes=[B_dts])
                            yield
                            P.op('act', lambda e: e.activation(out=dts[:, 48:56], in_=dts[:, 48:56], func=AF.Exp), reads=[B_dts, B_dta], writes=[B_dts])
                            yield
                            xsT3 = pbf[6][:, 0:512].rearrange("p (h v) -> p h v", h=8)
                            yield
                            P.op('dve', lambda e: e.tensor_tensor(out=xdt[:].rearrange("p (h v) -> p h v", h=8), in0=xsT3,
                                                                  in1=dta[:, 0:8].unsqueeze(2).broadcast_to([128, 8, 64]), op=ALU.mult),
                                 reads=[B6a, B_dts, B_dta], writes=[B_xdt])
                            yield
                            P.op('dve', lambda e: e.tensor_tensor(out=xsd[:].rearrange("p (h v) -> p h v", h=8), in0=xsT3,
                                                                  in1=dsk_bc[:].unsqueeze(2).broadcast_to([128, 8, 64]), op=ALU.mult),
                                 reads=[B6a, B_bc], writes=[B_xsd])
                            yield
                            P.op('dve', lambda e: e.tensor_tensor(out=xdd[:].rearrange("p (h v) -> p h v", h=8),
                                                                   in0=xdt[:].rearrange("p (h v) -> p h v", h=8),
                                                                   in1=dts[:, 48:56].unsqueeze(2).broadcast_to([128, 8, 64]), op=ALU.mult),
                                 reads=[B_xdt, B_dts, B_dta], writes=[B_xdd])
                            yield
                            P.op('act', lambda e: e.copy(out=bmtm[:], in_=pbf[6][:, 512:768]), reads=[B6a], writes=[B_bmtm])
                            yield
                            P.op('dve', lambda e: e.tensor_tensor(out=ldec[:], in0=maskgt_f[:].unsqueeze(1).broadcast_to([128, 8, 128]),
                                                                  in1=dta[:, 8:16].unsqueeze(2).broadcast_to([128, 8, 128]), op=ALU.mult),
                                 reads=[Bc, B_dts, B_dta], writes=[B_ldec])
                            yield
                            for h in range(8):
                                bk = h // 4
                                P.op('pe', lambda e, h=h, bk=bk: e.matmul(psum[bk][:, (h % 4) * 128:(h % 4 + 1) * 128], lhsT=ldec[:, h, :], rhs=tri_f[:],
                                                                          start=True, stop=True), reads=[B_ldec, Bc], writes=[B_ps[bk]])
                            yield
                            for bk in range(2):
                                P.op('act', lambda e, bk=bk: e.activation(out=dm[:, bk * 4:(bk + 1) * 4, :],
                                                                          in_=psum[bk][:].rearrange("p (h t) -> p h t", h=4), func=AF.Exp),
                                     reads=[B_ps[bk]], writes=[B_dm])
                            yield
                            for g in range(2):
                                P.op('pe', lambda e, g=g: e.matmul(psum[2][:, g * 128:(g + 1) * 128], lhsT=xc[:, 4 + g, cs:cs + 128],
                                                                   rhs=xc[:, 6 + g, cs:cs + 128], start=True, stop=True), reads=[B_xc], writes=[B_ps[2]])
                            yield
                            P.op('dve', lambda e: e.tensor_tensor(out=cbm[:], in0=psum[2][:, 0:256].rearrange("p (g t) -> p g t", g=2),
                                                                  in1=maskge_b[:].unsqueeze(1).broadcast_to([128, 2, 128]), op=ALU.mult),
                                 reads=[B_ps[2], Bc], writes=[B_cbm])
                            yield
                            for g in range(2):
                                P.op('dve', lambda e, g=g: e.tensor_tensor(out=mm_[:, g * 4:(g + 1) * 4, :], in0=dm[:, g * 4:(g + 1) * 4, :],
                                                                            in1=cbm[:, g, :].unsqueeze(1).broadcast_to([128, 4, 128]), op=ALU.mult),
                                     reads=[B_dm, B_cbm], writes=[B_mm])
                            yield
                            for h in range(8):
                                P.op('pe', lambda e, h=h: e.matmul(psum[0][:, h * 64:(h + 1) * 64], lhsT=mm_[:, h, :], rhs=xdt[:, h * 64:(h + 1) * 64],
                                                                   start=True, stop=True), reads=[B_mm, B_xdt], writes=[B_ps[0]])
                            yield
                            for g in range(2):
                                P.op('pe', lambda e, g=g: e.matmul(psum[1][:, g * 256:(g + 1) * 256], lhsT=xc[:, 6 + g, cs:cs + 128],
                                                                   rhs=sstb[:, g * 256:(g + 1) * 256], start=True, stop=True),
                                     reads=[B_xc, B_sstb], writes=[B_ps[1]])
                            yield
                            P.op('dve', lambda e: e.tensor_tensor(out=y1[:].rearrange("p (h v) -> p h v", h=8),
                                                                  in0=psum[1][:].rearrange("p (h v) -> p h v", h=8),
                                                                  in1=dts[:, 32:40].unsqueeze(2).broadcast_to([128, 8, 64]), op=ALU.mult),
                                 reads=[B_ps[1], B_dts, B_dta], writes=[B_y1])
                            yield
                            P.op('dve', lambda e: e.tensor_tensor(out=y1[:], in0=y1[:], in1=psum[0][:], op=ALU.add), reads=[B_y1, B_ps[0]], writes=[B_y1])
                            yield
                            P.op('dve', lambda e: e.tensor_tensor(out=y1[:], in0=y1[:], in1=xsd[:], op=ALU.add), reads=[B_y1, B_xsd], writes=[B_y1])
                            yield
                            P.op('dve', lambda e: e.tensor_tensor(out=y1[:], in0=y1[:], in1=zs[:], op=ALU.mult), reads=[B_y1, B_zs], writes=[B_y1])
                            yield
                            for g in range(2):
                                P.op('pe', lambda e, g=g: e.matmul(psum[2][:, g * 256:(g + 1) * 256], lhsT=bmtm[:, g * 128:(g + 1) * 128],
                                                                   rhs=xdd[:, g * 256:(g + 1) * 256], start=True, stop=True),
                                     reads=[B_bmtm, B_xdd], writes=[B_ps[2]])
                            yield
                            P.op('pool', lambda e: e.tensor_tensor(out=sst[:].rearrange("p (h v) -> p h v", h=8),
                                                                   in0=sst[:].rearrange("p (h v) -> p h v", h=8),
                                                                   in1=dts[:, 40:48].unsqueeze(2).broadcast_to([128, 8, 64]), op=ALU.mult),
                                 reads=[B_sst, B_dts, B_dta], writes=[B_sst])
                            yield
                            P.op('dve', lambda e: e.tensor_tensor(out=sst[:], in0=sst[:], in1=psum[2][:], op=ALU.add), reads=[B_sst, B_ps[2]], writes=[B_sst])
                            yield
                            P.op('act', lambda e: e.copy(out=sstb[:], in_=sst[:]), reads=[B_sst], writes=[B_sstb])
                            yield
                            P.op('act', lambda e: e.activation(out=osq2[:], in_=y1[:], func=AF.Square), reads=[B_y1], writes=[B_osq2])
                            yield
                            P.op('dve', lambda e: e.tensor_reduce(out=gsc[:, 12:14], in_=osq2[:].rearrange("p (g v) -> p g v", g=2), axis=AX.X, op=ALU.add),
                                 reads=[B_osq2], writes=[B_gsc])
                            yield
                            P.op('act', lambda e: e.activation(out=gsc[:, 14:16], in_=gsc[:, 12:14], func=AF.Ln, bias=eps_t[:], scale=1.0 / 256.0),
                                 reads=[B_gsc, B_const], writes=[B_gsc])
                            yield
                            P.op('act', lambda e: e.activation(out=gsc[:, 12:14], in_=gsc[:, 14:16], func=AF.Exp, scale=-0.5), reads=[B_gsc], writes=[B_gsc])
                            yield
                            P.op('dve', lambda e: e.tensor_tensor(out=y1[:].rearrange("p (g v) -> p g v", g=2),
                                                                  in0=y1[:].rearrange("p (g v) -> p g v", g=2),
                                                                  in1=gsc[:, 12:14].unsqueeze(2).broadcast_to([128, 2, 256]), op=ALU.mult),
                                 reads=[B_y1, B_gsc], writes=[B_y1])
                            yield
                            P.op('pool', lambda e: e.tensor_tensor(out=yb[:], in0=y1[:], in1=sng_bc[:], op=ALU.mult), reads=[B_y1, B_bc], writes=[B_yb])
                            yield
                            for q in range(4):
                                P.op('pe', lambda e, q=q: e.transpose(out=pbf[6][:, q * 128:(q + 1) * 128], in_=yb[:, q * 128:(q + 1) * 128],
                                                                      identity=ident_b[:]), reads=[B_yb, B_const], writes=[B6a])
                            yield
                            P.op('act', lambda e: e.copy(out=mixT[:, 4:8, :], in_=pbf[6][:, 0:512].rearrange("p (q t) -> p q t", q=4)),
                                 reads=[B6a], writes=[B_mixT])
                        g_ssd = gen_ssd()
                        pend = list(tails)
                        gens = [g_ssd, gen_gla(), g_ssd, gen_sgu()] + tails
                        tails = []
                        while gens:
                            for g_ in list(gens):
                                if g_ not in gens:
                                    continue
                                try:
                                    next(g_)
                                except StopIteration:
                                    while g_ in gens:
                                        gens.remove(g_)
                            for _rep in range(2):
                                if nxt is not None and not any(p_ in gens for p_ in pend):
                                    try:
                                        next(nxt)
                                    except StopIteration:
                                        nxt = None
                        for hf in range(2):
                            for fc in range(8):
                                P.op('pe', lambda e, hf=hf, fc=fc: e.matmul(psum[hf][:], lhsT=mixT[:, fc, :], rhs=w_out[:, fc, hf * 512:(hf + 1) * 512],
                                                                            start=(fc == 0), stop=(fc == 7)), reads=[B_mixT, B_wout], writes=[B_ps[hf]], cost=0.39)
                        for hf in range(2):
                            P.op('dve', lambda e, hf=hf: e.scalar_tensor_tensor(
                                out=xi[:, c, hf * 512:(hf + 1) * 512], in0=xi[:, c, hf * 512:(hf + 1) * 512], scalar=ALPHA,
                                in1=psum[hf][:], op0=ALU.mult, op1=ALU.add), reads=[Bxi, B_ps[hf]], writes=[Bxi])
                        tails = [gen_ln1(xi, Bxi, c, t, c == NCH - 1)]
                    if nxt is not None:
                        for _ in nxt:
                            pass
                for g_ in tails:
                    for _ in g_:
                        pass

        B_x = [Buf(f"x{i}") for i in range(ntile)]
        cur_d, cur_B = x_d, B_x
        first = True
        for pi, (kind, layer) in enumerate(phases):
            last = (pi == len(phases) - 1)
            if kind == 'B':
                dst_d, dst_B = (y_d, B_y) if last else (hB_d, B_hB)
                phase_b(layer, cur_d, cur_B, dst_d, dst_B, pre_ln=first)
            else:
                dst_d, dst_B = (y_d, B_y) if last else (hA_d, B_hA)
                phase_a(layer, cur_d, cur_B, dst_d, dst_B, pre_ln=first)
            P.barrier()
            cur_d, cur_B = dst_d, dst_B
            first = False
        P.wait_all('sp', B_y)
        print(f"[build] instructions={P.nins} waits={P.nwaits} sems={len(P.sems)}")
    return nc


_W_NAMES = ["ln_in_g", "ln_in_b", "w_in", "gla_w_gate", "gla_b_gate", "gla_norm_g", "sgu_norm_g", "sgu_norm_b",
            "sgu_w", "sgu_b", "ssd_conv_w", "ssd_conv_b", "ssd_dt_bias", "ssd_a_log", "ssd_d", "ssd_norm_g",
            "w_out", "ln1_g", "ln1_b", "ffn_w_up", "ffn_conv_w", "ffn_conv_b", "ffn_w_down", "ln2_g", "ln2_b"]


def run(inputs, ntok, seqlen, phases=None, ncores=NCORES, **kw):
    nc = build_program(ntok=ntok, seqlen=seqlen, phases=phases, **kw)
    x = np.ascontiguousarray(np.asarray(inputs["x"], dtype=np.float32)).reshape(-1, D)
    assert x.shape[0] == ntok * ncores
    wmap = {k: np.ascontiguousarray(np.asarray(inputs[k], dtype=np.float32)) for k in _W_NAMES}
    in_maps = []
    for c in range(ncores):
        m = dict(wmap)
        m["x"] = x[c * ntok:(c + 1) * ntok]
        in_maps.append(m)
    res = run_bass_kernel_spmd(nc, in_maps, core_ids=list(range(ncores)))
    return np.concatenate([r["y"] for r in res.results], axis=0)


def kernel(**inputs):
    x = inputs["x"]
    B, S, _ = x.shape
    y = run(inputs, ntok=(B * S) // NCORES, seqlen=S)
    return y.reshape(B, S, D).astype(np.float32)
```
